# Optimizing a Trainium2 kernel written in Bass

```python
import jax, jax.numpy as jnp
from jax import lax
import numpy as np

D_MODEL = 2048
BATCH = 1
SEQ = 8192
DEPTH = 4

GRID_W = 64
CTX_LEN = 256
N_FG = 4
FG_W = 256
F_WIDTH = N_FG * FG_W
N_HEADS = 16
N_KV_HEADS = 2
HEAD_DIM = 64
Q_GROUP = N_HEADS // N_KV_HEADS
ATT_WIDTH = N_HEADS * HEAD_DIM
KV_WIDTH = N_KV_HEADS * HEAD_DIM
WINDOW = 128
BLOCK = 128
ROPE_BASE = 10000.0
FA_IN = F_WIDTH + ATT_WIDTH + 2 * KV_WIDTH
FA_OUT = F_WIDTH + ATT_WIDTH
D_RNN = D_MODEL
N_RNN_BLOCKS = 8
RNN_BLOCK = D_RNN // N_RNN_BLOCKS
CONV_W = 4
CONV_LEFT = 2
LRU_C = 8.0
D_FF = 5632
FFN_CONV_W = 3
FFN_CONV_LEFT = 1
N_MOD = 6
EPS = 1e-6
NEG_INF = -1e30
N_FA = (DEPTH + 1) // 2
N_RG = DEPTH // 2

kernel_name = 'hybrid_fourier_swa_rglru_convffn_dit'


def rmsnorm(x, g):
    xf = x.astype(jnp.float32)
    r = lax.rsqrt(jnp.mean(xf * xf, axis=-1, keepdims=True) + EPS)
    return (xf * r).astype(x.dtype) * g


def modulate(h, shift, scale):
    return h * (1 + scale) + shift


def dwconv(x, w, b, left):
    k_w = w.shape[0]
    n = x.shape[1]
    xp = jnp.pad(x, ((0, 0), (left, k_w - 1 - left), (0, 0)))
    y = xp[:, 0:n] * w[0] + b
    for k in range(1, k_w):
        y = y + xp[:, k:k + n] * w[k]
    return y


def _rope_axis(t, pos):
    f = t.shape[-1] // 2
    inv = ROPE_BASE ** (-jnp.arange(f, dtype=jnp.float32) / f)
    ang = pos.astype(jnp.float32)[:, None] * inv[None, :]
    cos = jnp.cos(ang)[:, None, :].astype(t.dtype)
    sin = jnp.sin(ang)[:, None, :].astype(t.dtype)
    t1, t2 = t[..., :f], t[..., f:]
    return jnp.concatenate([t1 * cos - t2 * sin, t1 * sin + t2 * cos], axis=-1)


def rope_2d(t, row_ids, col_ids):
    half = t.shape[-1] // 2
    return jnp.concatenate([_rope_axis(t[..., :half], row_ids), _rope_axis(t[..., half:], col_ids)], axis=-1)


def fourier_mix(u):
    b, n, _ = u.shape
    ug = u.reshape(b, n, N_FG, FG_W).astype(jnp.float32)
    y = jnp.fft.fft2(ug, axes=(1, 3), norm='ortho').real
    return y.reshape(b, n, F_WIDTH).astype(u.dtype)


def _split_fa(u):
    b, n = u.shape[:2]
    f = u[..., :F_WIDTH]
    q = u[..., F_WIDTH:F_WIDTH + ATT_WIDTH].reshape(b, n, N_HEADS, HEAD_DIM)
    k = u[..., F_WIDTH + ATT_WIDTH:F_WIDTH + ATT_WIDTH + KV_WIDTH].reshape(b, n, N_KV_HEADS, HEAD_DIM)
    v = u[..., F_WIDTH + ATT_WIDTH + KV_WIDTH:].reshape(b, n, N_KV_HEADS, HEAD_DIM)
    return f, q, k, v


def _window_mask(nb):
    q_pos = jnp.arange(nb)[:, None, None] * BLOCK + jnp.arange(BLOCK)[None, :, None]
    k_pos = (jnp.arange(nb)[:, None, None] - 1) * BLOCK + jnp.arange(3 * BLOCK)[None, None, :]
    return (jnp.abs(k_pos - q_pos) <= WINDOW) & (k_pos >= 0) & (k_pos < nb * BLOCK)


def _latent_attention(q, k, v, kc, vc, sink):
    b, s = q.shape[:2]
    nb = s // BLOCK
    n_ctx = kc.shape[1]
    qb = (q * HEAD_DIM ** -0.5).reshape(b, nb, BLOCK, N_KV_HEADS, Q_GROUP, HEAD_DIM)

    def bands(t):
        tp = jnp.pad(t, ((0, 0), (BLOCK, BLOCK), (0, 0), (0, 0))).reshape(b, nb + 2, BLOCK, N_KV_HEADS, HEAD_DIM)
        return jnp.concatenate([tp[:, :-2], tp[:, 1:-1], tp[:, 2:]], axis=2)

    kw, vw = bands(k), bands(v)
    s_win = jnp.einsum('bnqkgd,bnskd->bnkgqs', qb, kw).astype(jnp.float32)
    s_win = jnp.where(_window_mask(nb)[None, :, None, None], s_win, NEG_INF)
    s_ctx = jnp.einsum('bnqkgd,bckd->bnkgqc', qb, kc).astype(jnp.float32)
    sink_col = jnp.broadcast_to(sink.astype(jnp.float32).reshape(1, 1, N_KV_HEADS, Q_GROUP, 1, 1), s_win.shape[:-1] + (1,))
    p = jax.nn.softmax(jnp.concatenate([s_win, s_ctx, sink_col], axis=-1), axis=-1)
    p_win = p[..., :3 * BLOCK].astype(v.dtype)
    p_ctx = p[..., 3 * BLOCK:3 * BLOCK + n_ctx].astype(v.dtype)
    o = jnp.einsum('bnkgqs,bnskd->bnqkgd', p_win, vw) + jnp.einsum('bnkgqc,bckd->bnqkgd', p_ctx, vc)
    return o.reshape(b, s, ATT_WIDTH)


def _context_attention(qc, kc, vc, sink):
    b, n_ctx = qc.shape[:2]
    qs = (qc * HEAD_DIM ** -0.5).reshape(b, n_ctx, N_KV_HEADS, Q_GROUP, HEAD_DIM)
    s = jnp.einsum('bqkgd,bckd->bkgqc', qs, kc).astype(jnp.float32)
    sink_col = jnp.broadcast_to(sink.astype(jnp.float32).reshape(1, N_KV_HEADS, Q_GROUP, 1, 1), s.shape[:-1] + (1,))
    p = jax.nn.softmax(jnp.concatenate([s, sink_col], axis=-1), axis=-1)[..., :n_ctx].astype(vc.dtype)
    o = jnp.einsum('bkgqc,bckd->bqkgd', p, vc)
    return o.reshape(b, n_ctx, ATT_WIDTH)


def fourier_attn_mixer(h_lat, h_ctx, w_in, w_out, sink, row_ids, col_ids, ctx_out):
    f, q, k, v = _split_fa(h_lat @ w_in)
    fc, qc, kc, vc = _split_fa(h_ctx @ w_in)
    q = rope_2d(q, row_ids, col_ids)
    k = rope_2d(k, row_ids, col_ids)
    y_lat = jnp.concatenate([fourier_mix(f), _latent_attention(q, k, v, kc, vc, sink)], axis=-1) @ w_out
    y_ctx = None
    if ctx_out:
        y_ctx = jnp.concatenate([fourier_mix(fc), _context_attention(qc, kc, vc, sink)], axis=-1) @ w_out
    return y_lat, y_ctx


def _rglru_gates(xs, w_a, b_a, w_i, b_i, lam):
    b, n = xs.shape[:2]
    xb = xs.reshape(b, n, N_RNN_BLOCKS, RNN_BLOCK)
    r = jax.nn.sigmoid(jnp.einsum('blhi,hij->blhj', xb, w_a.astype(jnp.float32)).reshape(b, n, D_RNN) + b_a.astype(jnp.float32))
    i = jax.nn.sigmoid(jnp.einsum('blhi,hij->blhj', xb, w_i.astype(jnp.float32)).reshape(b, n, D_RNN) + b_i.astype(jnp.float32))
    log_a = -LRU_C * r * jax.nn.softplus(-lam.astype(jnp.float32))
    a = jnp.exp(log_a)
    gx = jnp.sqrt(-jnp.expm1(2.0 * log_a)) * (i * xs)
    return a, gx


def _linear_scan(a, gx, h0, reverse):
    def combine(e1, e2):
        a1, b1 = e1
        a2, b2 = e2
        return a1 * a2, a2 * b1 + b2
    a_cum, b_cum = lax.associative_scan(combine, (a, gx), reverse=reverse, axis=1)
    return b_cum + a_cum * h0[:, None, :]


def rglru_mixer(h_lat, h_ctx, w_in, conv_w, conv_b, w_a, b_a, w_i, b_i, lam, w_out, ctx_out):
    gate, xs = jnp.split(h_lat @ w_in, 2, axis=-1)
    xs_c = h_ctx @ w_in[:, D_RNN:]
    xs = dwconv(xs, conv_w, conv_b, CONV_LEFT).astype(jnp.float32)
    xs_c = dwconv(xs_c, conv_w, conv_b, CONV_LEFT).astype(jnp.float32)
    b = xs_c.shape[0]
    h_dirs, hc_dirs = [], []
    for d, reverse in enumerate((False, True)):
        a, gx = _rglru_gates(xs, w_a[d], b_a[d], w_i[d], b_i[d], lam[d])
        ac, gxc = _rglru_gates(xs_c, w_a[d], b_a[d], w_i[d], b_i[d], lam[d])
        hc = _linear_scan(ac, gxc, jnp.zeros((b, D_RNN), jnp.float32), reverse)
        h0 = hc[:, 0] if reverse else hc[:, -1]
        h_dirs.append(_linear_scan(a, gx, h0, reverse))
        hc_dirs.append(hc)
    y = (h_dirs[0] + h_dirs[1]).astype(h_lat.dtype) * jax.nn.gelu(gate)
    y_lat = y @ w_out
    y_ctx = None
    if ctx_out:
        gate_c = h_ctx @ w_in[:, :D_RNN]
        y_ctx = ((hc_dirs[0] + hc_dirs[1]).astype(h_ctx.dtype) * jax.nn.gelu(gate_c)) @ w_out
    return y_lat, y_ctx


def conv_ffn(h, w_up, conv_w, conv_b, w_down):
    u = dwconv(h @ w_up, conv_w, conv_b, FFN_CONV_LEFT)
    g, v = jnp.split(u, 2, axis=-1)
    return (jax.nn.silu(g) * v) @ w_down


def setup_inputs(seed: int = 0) -> dict:
    key = jax.random.key(seed)
    ks = jax.random.split(key, 32)
    f32 = jnp.float32

    def nrm(k, shape, scale):
        return jax.random.normal(k, shape, f32) * scale

    D = D_MODEL
    u = jax.random.uniform(ks[18], (N_RG, 2, D_RNN), f32, 0.9, 0.999)
    s = u ** (1.0 / LRU_C)
    return {
        'x': nrm(ks[0], (BATCH, SEQ, D), 1.0),
        'c': nrm(ks[1], (BATCH, D), 1.0),
        'ctx': nrm(ks[2], (BATCH, CTX_LEN, D), 1.0),
        'c_ctx': nrm(ks[3], (D,), 1.0),
        'w_mod': nrm(ks[4], (DEPTH, D, N_MOD * D), 0.5 * D ** -0.5),
        'b_mod': nrm(ks[5], (DEPTH, N_MOD * D), 0.02),
        'g_mix': 1.0 + nrm(ks[6], (DEPTH, D), 0.02),
        'g_ffn': 1.0 + nrm(ks[7], (DEPTH, D), 0.02),
        'fa_w_in': nrm(ks[8], (N_FA, D, FA_IN), D ** -0.5),
        'fa_w_out': nrm(ks[9], (N_FA, FA_OUT, D), FA_OUT ** -0.5),
        'attn_sink': nrm(ks[10], (N_FA, N_HEADS), 0.5),
        'rg_w_in': nrm(ks[11], (N_RG, D, 2 * D_RNN), D ** -0.5),
        'rg_conv_w': nrm(ks[12], (N_RG, CONV_W, D_RNN), CONV_W ** -0.5),
        'rg_conv_b': nrm(ks[13], (N_RG, D_RNN), 0.02),
        'rg_w_a': nrm(ks[14], (N_RG, 2, N_RNN_BLOCKS, RNN_BLOCK, RNN_BLOCK), RNN_BLOCK ** -0.5),
        'rg_b_a': nrm(ks[15], (N_RG, 2, D_RNN), 0.02),
        'rg_w_i': nrm(ks[16], (N_RG, 2, N_RNN_BLOCKS, RNN_BLOCK, RNN_BLOCK), RNN_BLOCK ** -0.5),
        'rg_b_i': nrm(ks[17], (N_RG, 2, D_RNN), 0.02),
        'rg_lambda': jnp.log(s) - jnp.log1p(-s),
        'rg_w_out': nrm(ks[19], (N_RG, D_RNN, D), D_RNN ** -0.5),
        'ffn_w_up': nrm(ks[20], (DEPTH, D, 2 * D_FF), D ** -0.5),
        'ffn_conv_w': nrm(ks[21], (DEPTH, FFN_CONV_W, 2 * D_FF), FFN_CONV_W ** -0.5),
        'ffn_conv_b': nrm(ks[22], (DEPTH, 2 * D_FF), 0.02),
        'ffn_w_down': nrm(ks[23], (DEPTH, D_FF, D), D_FF ** -0.5),
        'g_final': 1.0 + nrm(ks[24], (D,), 0.02),
    }


def reference(x, c, ctx, c_ctx, w_mod, b_mod, g_mix, g_ffn, fa_w_in, fa_w_out, attn_sink,
              rg_w_in, rg_conv_w, rg_conv_b, rg_w_a, rg_b_a, rg_w_i, rg_b_i, rg_lambda, rg_w_out,
              ffn_w_up, ffn_conv_w, ffn_conv_b, ffn_w_down, g_final):
    n = x.shape[1]
    rows = n // GRID_W
    row_ids = jnp.repeat(jnp.arange(rows, dtype=jnp.int32), GRID_W)
    col_ids = jnp.tile(jnp.arange(GRID_W, dtype=jnp.int32), rows)
    x_lat, x_ctx = x, ctx
    s_lat = jax.nn.silu(c)
    s_ctx = jax.nn.silu(c_ctx)[None]
    for layer in range(DEPTH):
        ctx_out = layer < DEPTH - 1
        mod_lat = (s_lat @ w_mod[layer] + b_mod[layer])[:, None, :]
        mod_ctx = (s_ctx @ w_mod[layer] + b_mod[layer])[:, None, :]
        sh_m, sc_m, gt_m, sh_f, sc_f, gt_f = jnp.split(mod_lat, N_MOD, axis=-1)
        csh_m, csc_m, cgt_m, csh_f, csc_f, cgt_f = jnp.split(mod_ctx, N_MOD, axis=-1)
        h_lat = modulate(rmsnorm(x_lat, g_mix[layer]), sh_m, sc_m)
        h_ctx = modulate(rmsnorm(x_ctx, g_mix[layer]), csh_m, csc_m)
        i = layer // 2
        if layer % 2 == 0:
            y_lat, y_ctx = fourier_attn_mixer(h_lat, h_ctx, fa_w_in[i], fa_w_out[i], attn_sink[i],
                                              row_ids, col_ids, ctx_out)
        else:
            y_lat, y_ctx = rglru_mixer(h_lat, h_ctx, rg_w_in[i], rg_conv_w[i], rg_conv_b[i], rg_w_a[i], rg_b_a[i],
                                       rg_w_i[i], rg_b_i[i], rg_lambda[i], rg_w_out[i], ctx_out)
        x_lat = x_lat + gt_m * y_lat
        h_lat = modulate(rmsnorm(x_lat, g_ffn[layer]), sh_f, sc_f)
        x_lat = x_lat + gt_f * conv_ffn(h_lat, ffn_w_up[layer], ffn_conv_w[layer], ffn_conv_b[layer], ffn_w_down[layer])
        if ctx_out:
            x_ctx = x_ctx + cgt_m * y_ctx
            h_ctx = modulate(rmsnorm(x_ctx, g_ffn[layer]), csh_f, csc_f)
            x_ctx = x_ctx + cgt_f * conv_ffn(h_ctx, ffn_w_up[layer], ffn_conv_w[layer], ffn_conv_b[layer], ffn_w_down[layer])
    return rmsnorm(x_lat, g_final)
```

```python
import contextlib
import numpy as np
import ml_dtypes
import concourse.bass as bass
import concourse.mybir as mybir
from concourse.bass_utils import run_bass_kernel_spmd

F32 = mybir.dt.float32
BF16 = mybir.dt.bfloat16
AF = mybir.ActivationFunctionType
ALU = mybir.AluOpType
EPS = 1e-6
NCORES = 8


class Buf:
    __slots__ = ("w", "r", "pr", "name")

    def __init__(self, name=""):
        self.w = {}
        self.r = {}
        self.pr = {}
        self.name = name


def _merge(dst, src):
    for k, (s, v) in src.items():
        if k not in dst or dst[k][1] < v:
            dst[k] = (s, v)


class Tk:
    NDS = 12

    def __init__(self, nc, es):
        self.nc = nc
        self.es = es
        self.eng = {"pe": nc.tensor, "act": nc.scalar, "dve": nc.vector, "pool": nc.gpsimd, "sp": nc.sync}
        self.sem = {}
        self.cnt = {}
        for e in ("pe", "act", "dve", "pool"):
            self.sem[e] = es.enter_context(nc.semaphore("s_" + e))
            self.cnt[e] = 0
        self.seen = {e: {} for e in self.eng}
        self.dsem = {}
        self.dk = {}
        for q in ("sp", "pool", "act"):
            self.dsem[q] = [es.enter_context(nc.semaphore(f"d_{q}{i}")) for i in range(self.NDS)]
            self.dk[q] = 0
        self.nbuf = 0

    def buf(self, name=""):
        self.nbuf += 1
        return Buf(name or f"b{self.nbuf}")

    def sbuf(self, name, shape, dt):
        return self.es.enter_context(self.nc.sbuf_tensor(name, shape, dt))

    def psum(self, name, shape, dt=F32):
        return self.es.enter_context(self.nc.psum_tensor(name, shape, dt))

    def _wait(self, e, need):
        seen = self.seen[e]
        for k, (s, v) in need.items():
            if seen.get(k, 0) >= v:
                continue
            self.eng[e].wait_ge(s, v)
            seen[k] = v

    def _needs(self, e, reads, writes, nowaw):
        need = {}
        for b in reads:
            _merge(need, b.w)
        for b in writes:
            _merge(need, b.r)
            _merge(need, b.pr)
            if not nowaw:
                _merge(need, b.w)
        if e == "pe":
            need.pop("pe", None)
        return need

    def _mark(self, key, ev, reads, writes):
        for b in reads:
            if key not in b.r or b.r[key][1] < ev[1]:
                b.r[key] = ev
        for b in writes:
            if b.r:
                b.pr = dict(b.r)
                b.r = {}
                b.w = {}
            b.w[key] = ev

    def op(self, e, fn, reads=(), writes=(), nowaw=False):
        self._wait(e, self._needs(e, reads, writes, nowaw))
        ins = fn()
        self.cnt[e] += 1
        ins.then_inc(self.sem[e], 1)
        ev = (self.sem[e], self.cnt[e])
        self._mark(e, ev, reads, writes)
        return ins

    def dma(self, q, out, in_, reads=(), writes=(), nowaw=False):
        need = self._needs(q, reads, writes, nowaw)
        k = self.dk[q]
        slot = k % self.NDS
        gen = k // self.NDS
        s = self.dsem[q][slot]
        key = f"d_{q}{slot}"
        if gen > 0:
            _merge(need, {key: (s, 16 * gen)})
        self._wait(q, need)
        ins = self.eng[q].dma_start(out=out, in_=in_)
        ins.then_inc(s, 16)
        self.dk[q] = k + 1
        ev = (s, 16 * (gen + 1))
        self._mark(key, ev, reads, writes)
        return ins

    def finish(self, q, bufs):
        need = {}
        for b in bufs:
            _merge(need, b.w)
        self._wait(q, need)


def new_nc():
    return bass.Bass("TRN2", target_bir_lowering=False)


def dram_in(nc, name, shape, dt=F32):
    return nc.dram_tensor(name, list(shape), dt, kind="ExternalInput").ap()


def dram_out(nc, name, shape, dt=F32):
    return nc.dram_tensor(name, list(shape), dt, kind="ExternalOutput").ap()


def run(nc, in_maps, trace=False):
    res = run_bass_kernel_spmd(nc, in_maps, core_ids=list(range(len(in_maps))), trace=trace)
    return res


D = 2048
KC = 16


def col_banks(W):
    return [(s, min(s + 512, W)) for s in range(0, W, 512)]


def fm(v):
    return np.ascontiguousarray(np.asarray(v).reshape(-1, 128).T)


def to_fm(x):
    T, F = x.shape
    return np.ascontiguousarray(x.T.reshape(F // 128, 128, T).transpose(1, 0, 2))


def from_fm(y):
    return np.ascontiguousarray(y.transpose(2, 1, 0).reshape(y.shape[2], -1))


def tile_w(w, nsplit=None):
    K, N = w.shape
    t = w.reshape(K // 128, 128, N // 128, 128)
    return np.ascontiguousarray(t.transpose(2, 1, 0, 3)).reshape(N // 128, 128, (K // 128) * 128)


class Common:
    pass


def emit_consts(tk, nc):
    ones = tk.sbuf("ones", [128, 128], F32)
    b = tk.buf("ones")
    tk.op("dve", lambda: nc.vector.memset(ones[:], 1.0), writes=[b])
    return ones, b


def emit_norm_mod(tk, nc, X, bX, H, bH, segs, W, mods, gvec, bC, ones, bOnes, PT, bPT, tmp, btmp, RS, bRS,
                  AB, bAB, hmask=None):
    banks = col_banks(W)
    names = [s[0] for s in segs]
    for i, nm in enumerate(names):
        m = mods[nm]
        tk.op("dve", lambda i=i, m=m: nc.vector.tensor_scalar(
            out=AB[:, i, :], in0=m[:, 1, :], scalar1=1.0, scalar2=None, op0=ALU.add),
            reads=[bC], writes=[bAB], nowaw=True)
        tk.op("dve", lambda i=i: nc.vector.tensor_tensor(
            out=AB[:, i, :], in0=AB[:, i, :], in1=gvec[:], op=ALU.mult),
            reads=[bC, bAB], writes=[bAB])
    for c in range(KC):
        s = c % 2
        tk.op("act", lambda c=c, s=s: nc.scalar.activation(out=tmp[s][:, 0:W], in_=X[:, c, :], func=AF.Square),
              reads=[bX[c]], writes=[btmp[s]])
        for (b0, b1) in banks:
            tk.op("pe", lambda c=c, s=s, b0=b0, b1=b1: nc.tensor.matmul(
                PT[:, b0:b1], lhsT=ones[:], rhs=tmp[s][:, b0:b1], start=(c == 0), stop=(c == KC - 1)),
                reads=[btmp[s], bOnes], writes=[bPT], nowaw=(c > 0))
    tk.op("dve", lambda: nc.vector.tensor_scalar(out=RS[:, 0:W], in0=PT[:, 0:W], scalar1=1.0 / D, scalar2=EPS,
                                                  op0=ALU.mult, op1=ALU.add), reads=[bPT], writes=[bRS])
    tk.op("act", lambda: nc.scalar.activation(out=RS[:, 0:W], in_=RS[:, 0:W], func=AF.Sqrt), reads=[bRS], writes=[bRS])
    tk.op("dve", lambda: nc.vector.reciprocal(out=RS[:, 0:W], in_=RS[:, 0:W]), reads=[bRS], writes=[bRS])
    for c in range(KC):
        s = c % 2
        tk.op("dve", lambda c=c, s=s: nc.vector.tensor_tensor(out=tmp[s][:, 0:W], in0=X[:, c, :], in1=RS[:, 0:W],
                                                              op=ALU.mult),
              reads=[bX[c], bRS], writes=[btmp[s]])
        for i, (nm, s0, s1) in enumerate(segs):
            m = mods[nm]
            tk.op("act", lambda c=c, s=s, s0=s0, s1=s1, i=i, m=m: nc.scalar.activation(
                out=H[:, c, s0:s1], in_=tmp[s][:, s0:s1], func=AF.Identity,
                bias=m[:, 0, c:c + 1], scale=AB[:, i, c:c + 1]),
                reads=[btmp[s], bAB, bC], writes=[bH], nowaw=True)
    if hmask:
        edg, cols = hmask
        for k, col in enumerate(cols):
            tk.op("dve", lambda k=k, col=col: nc.vector.tensor_scalar(
                out=H[:, :, col:col + 1], in0=H[:, :, col:col + 1], scalar1=edg[:, k:k + 1], scalar2=None,
                op0=ALU.mult), reads=[bH, bC], writes=[bH])


class Linear:
    def __init__(self, tk, nc, name, nslots=3, kc=KC):
        self.tk, self.nc = tk, nc
        self.kc = kc
        self.WT = [tk.sbuf(f"{name}_w{i}", [128, kc * 128], BF16) for i in range(nslots)]
        self.bW = [tk.buf() for _ in range(nslots)]
        self.n = 0
        self.loaded = 0
        self.queue = []

    def plan(self, tiles):
        self.queue = list(tiles)
        for _ in range(len(self.WT) - 1):
            self._load()

    def _load(self):
        if self.loaded < len(self.queue):
            i = self.loaded
            s = i % len(self.WT)
            self.tk.dma("pool", self.WT[s][:], self.queue[i], writes=[self.bW[s]])
            self.loaded += 1

    def run(self, H, bH, W, PT, bPT, hreads=()):
        tk, nc = self.tk, self.nc
        bPTs = list(bPT) if isinstance(bPT, (list, tuple)) else [bPT]
        self._load()
        i = self.n
        s = i % len(self.WT)
        self.n += 1
        wt = self.WT[s]
        for c in range(self.kc):
            for (b0, b1) in col_banks(W):
                tk.op("pe", lambda c=c, b0=b0, b1=b1: nc.tensor.matmul(
                    PT[:, b0:b1], lhsT=wt[:, c * 128:(c + 1) * 128], rhs=H[:, c, b0:b1],
                    start=(c == 0), stop=(c == self.kc - 1)),
                    reads=[bH, self.bW[s]] + list(hreads), writes=bPTs, nowaw=(c > 0))


D = 2048
KC = 16
DFF = 5632
NP = 44
GP = 4
NG = NP // GP


def ffn_layout(NCTX):
    segs = [("lat", 0, 1026)]
    W = 1026
    if NCTX:
        segs.append(("ctx", 1026, 1026 + NCTX + 2))
        W += NCTX + 2
    return segs, W


def build_ffn(NCTX, final=False):
    segs, W = ffn_layout(NCTX)
    banks = [(s, min(s + 512, W)) for s in range(0, W, 512)]
    NOUT = 1024 + NCTX
    nc = new_nc()
    xT = dram_in(nc, "xT", [128, KC, W])
    edge = dram_in(nc, "edge", [128, 4])
    modl = dram_in(nc, "modl", [128, 3, KC])
    modc = dram_in(nc, "modc", [128, 3, KC])
    gfd = dram_in(nc, "gf", [128, KC])
    wup = dram_in(nc, "wup", [NP, 128, KC * 256])
    wdn = dram_in(nc, "wdn", [NG, 128, GP * D])
    cwd = dram_in(nc, "cw", [128, 2 * NP, 3])
    cbd = dram_in(nc, "cb", [128, 2 * NP])
    yT = dram_out(nc, "yT", [128, KC, NOUT])
    if final:
        gfind = dram_in(nc, "gfin", [128, KC])

    es = contextlib.ExitStack()
    with es:
        tk = Tk(nc, es)
        X = tk.sbuf("X", [128, KC, W], F32)
        H = tk.sbuf("H", [128, KC, W], BF16)
        Abuf = [tk.sbuf(f"A{i}", [128, GP, W], BF16) for i in range(2)]
        WU = [tk.sbuf(f"WU{i}", [128, KC * 256], BF16) for i in range(3)]
        WD = [tk.sbuf(f"WD{i}", [128, GP * D], BF16) for i in range(2)]
        UG = [tk.sbuf(f"UG{i}", [128, W], F32) for i in range(2)]
        UV = [tk.sbuf(f"UV{i}", [128, W], F32) for i in range(2)]
        TG = tk.sbuf("TG", [128, W], F32)
        TV = tk.sbuf("TV", [128, W], F32)
        SQ = [TG, TV]
        RS = UG[0]
        ones = tk.sbuf("ones", [128, 128], F32)
        edg = tk.sbuf("edg", [128, 4], F32)
        ml = tk.sbuf("ml", [128, 3, KC], F32)
        mc = tk.sbuf("mc", [128, 3, KC], F32)
        gf = tk.sbuf("gfs", [128, KC], F32)
        AB = tk.sbuf("AB", [128, 2, KC], F32)
        cw = tk.sbuf("cws", [128, 2 * NP, 3], F32)
        cb = tk.sbuf("cbs", [128, 2 * NP], F32)
        PG = tk.psum("PG", [128, 1536])
        PV = tk.psum("PV", [128, 1536])
        PD = [tk.psum(f"PD{i}", [128, 512]) for i in range(2)]

        bX = [tk.buf(f"X{c}") for c in range(KC)]
        bH = tk.buf("H")
        bA = [tk.buf() for _ in range(2)]
        bWU = [tk.buf() for _ in range(3)]
        bWD = [tk.buf() for _ in range(2)]
        bUG = [tk.buf() for _ in range(2)]
        bUV = [tk.buf() for _ in range(2)]
        bTG, bTV = tk.buf(), tk.buf()
        bRS = bUG[0]
        bSQ = [bTG, bTV]
        bPG, bPV = tk.buf(), tk.buf()
        bPD = [tk.buf() for _ in range(2)]
        bC = tk.buf("consts")

        for c in range(KC):
            tk.dma("sp", X[:, c, :], xT[:, c, :], writes=[bX[c]])
        tk.dma("sp", edg[:], edge[:, :], writes=[bC], nowaw=True)
        tk.dma("sp", ml[:], modl[:, :, :], writes=[bC], nowaw=True)
        tk.dma("sp", mc[:], modc[:, :, :], writes=[bC], nowaw=True)
        tk.dma("sp", gf[:], gfd[:, :], writes=[bC], nowaw=True)
        tk.dma("sp", cw[:], cwd[:, :, :], writes=[bC], nowaw=True)
        tk.dma("sp", cb[:], cbd[:, :], writes=[bC], nowaw=True)

        def load_wu(p):
            tk.dma("pool", WU[p % 3][:], wup[p, :, :], reads=(bX if p < 2 else []), writes=[bWU[p % 3]])

        def load_wd(g):
            tk.dma("pool", WD[g % 2][:], wdn[g, :, :], reads=(bX if g == 0 else []), writes=[bWD[g % 2]])

        load_wu(0)
        load_wu(1)
        load_wd(0)

        tk.op("dve", lambda: nc.vector.memset(ones[:], 1.0), writes=[bC], nowaw=True)
        for i in range(2):
            tk.op("dve", lambda i=i: nc.vector.memset(Abuf[i][:], 0.0), writes=[bA[i]])
        bAB = tk.buf()
        for i, m in enumerate((ml, mc)):
            tk.op("dve", lambda i=i, m=m: nc.vector.tensor_scalar(
                out=AB[:, i, :], in0=m[:, 1, :], scalar1=1.0, scalar2=None, op0=ALU.add),
                reads=[bC], writes=[bAB], nowaw=True)
            tk.op("dve", lambda i=i: nc.vector.tensor_tensor(
                out=AB[:, i, :], in0=AB[:, i, :], in1=gf[:], op=ALU.mult),
                reads=[bC, bAB], writes=[bAB])

        for c in range(KC):
            s = c % 2
            tk.op("act", lambda c=c, s=s: nc.scalar.activation(out=SQ[s][:], in_=X[:, c, :], func=AF.Square),
                  reads=[bX[c]], writes=[bSQ[s]])
            for (b0, b1) in banks:
                tk.op("pe", lambda c=c, s=s, b0=b0, b1=b1: nc.tensor.matmul(
                    PG[:, b0:b1], lhsT=ones[:], rhs=SQ[s][:, b0:b1], start=(c == 0), stop=(c == KC - 1)),
                    reads=[bSQ[s], bC], writes=[bPG], nowaw=(c > 0))
        tk.op("dve", lambda: nc.vector.tensor_scalar(out=RS[:], in0=PG[:, 0:W], scalar1=1.0 / D, scalar2=EPS,
                                                      op0=ALU.mult, op1=ALU.add), reads=[bPG], writes=[bRS])
        tk.op("act", lambda: nc.scalar.activation(out=RS[:], in_=RS[:], func=AF.Sqrt), reads=[bRS], writes=[bRS])
        tk.op("dve", lambda: nc.vector.reciprocal(out=RS[:], in_=RS[:]), reads=[bRS], writes=[bRS])
        for c in range(KC):
            s = c % 2
            tk.op("dve", lambda c=c, s=s: nc.vector.tensor_tensor(out=SQ[s][:], in0=X[:, c, :], in1=RS[:], op=ALU.mult),
                  reads=[bX[c], bRS], writes=[bSQ[s]])
            for (nm, s0, s1) in segs:
                i = 0 if nm == "lat" else 1
                m = ml if nm == "lat" else mc
                tk.op("act", lambda c=c, s=s, s0=s0, s1=s1, i=i, m=m: nc.scalar.activation(
                    out=H[:, c, s0:s1], in_=SQ[s][:, s0:s1], func=AF.Identity,
                    bias=m[:, 0, c:c + 1], scale=AB[:, i, c:c + 1]),
                    reads=[bSQ[s], bAB, bC], writes=[bH], nowaw=True)
        hcols = [0, 1025] + ([1026, W - 1] if NCTX else [])
        for k, col in enumerate(hcols):
            tk.op("dve", lambda k=k, col=col: nc.vector.tensor_scalar(
                out=H[:, :, col:col + 1], in0=H[:, :, col:col + 1], scalar1=edg[:, k:k + 1], scalar2=None,
                op0=ALU.mult), reads=[bH, bC], writes=[bH])

        def up_pair(p):
            g, j = divmod(p, GP)
            wu = WU[p % 3]
            s = p % 2
            for half, (PT, bPT, U, bU) in enumerate(((PG, bPG, UG[s], bUG[s]), (PV, bPV, UV[s], bUV[s]))):
                for c in range(KC):
                    for (b0, b1) in banks:
                        tk.op("pe", lambda c=c, b0=b0, b1=b1, PT=PT, half=half: nc.tensor.matmul(
                            PT[:, b0:b1], lhsT=wu[:, c * 256 + half * 128: c * 256 + half * 128 + 128],
                            rhs=H[:, c, b0:b1], start=(c == 0), stop=(c == KC - 1)),
                            reads=[bH, bWU[p % 3]], writes=[bPT], nowaw=(c > 0))
                tk.op("act", lambda PT=PT, U=U: nc.scalar.copy(out=U[:], in_=PT[:, 0:W]), reads=[bPT], writes=[bU])
            deferred = []
            for half, (U, bU, T, bT) in enumerate(((UG[s], bUG[s], TG, bTG), (UV[s], bUV[s], TV, bTV))):
                ch = p if half == 0 else NP + p
                deferred.append(lambda U=U, T=T, ch=ch, bU=bU, bT=bT: tk.op("dve", lambda: nc.vector.tensor_scalar(
                    out=T[:, 1:W - 1], in0=U[:, 1:W - 1], scalar1=cw[:, ch, 1:2], scalar2=cb[:, ch:ch + 1],
                    op0=ALU.mult, op1=ALU.add), reads=[bU, bC], writes=[bT]))
                deferred.append(lambda U=U, T=T, ch=ch, bU=bU, bT=bT: tk.op("dve", lambda: nc.vector.scalar_tensor_tensor(
                    out=T[:, 1:W - 1], in0=U[:, 0:W - 2], scalar=cw[:, ch, 0:1], in1=T[:, 1:W - 1],
                    op0=ALU.mult, op1=ALU.add), reads=[bU, bT, bC], writes=[bT]))
                deferred.append(lambda U=U, T=T, ch=ch, bU=bU, bT=bT: tk.op("dve", lambda: nc.vector.scalar_tensor_tensor(
                    out=T[:, 1:W - 1], in0=U[:, 2:W], scalar=cw[:, ch, 2:3], in1=T[:, 1:W - 1],
                    op0=ALU.mult, op1=ALU.add), reads=[bU, bT, bC], writes=[bT]))
            deferred.append(lambda: tk.op("act", lambda: nc.scalar.activation(
                out=TG[:, 1:W - 1], in_=TG[:, 1:W - 1], func=AF.Silu), reads=[bTG], writes=[bTG]))
            deferred.append(lambda g=g, j=j: tk.op("dve", lambda: nc.vector.tensor_tensor(
                out=Abuf[g % 2][:, j, 1:W - 1], in0=TG[:, 1:W - 1], in1=TV[:, 1:W - 1], op=ALU.mult),
                reads=[bTG, bTV], writes=[bA[g % 2]], nowaw=(j > 0)))
            return deferred

        dstate = {"n": 0}

        dbanks = [(1, 513), (513, 1025)]

        def down_part(g, part, deferred):
            items = [(bi, d) for bi in range(len(dbanks)) for d in range(KC)]
            per = (len(items) + GP - 1) // GP
            for (bi, d) in items[part * per:(part + 1) * per]:
                b0, b1 = dbanks[bi]
                k = dstate["n"] % 2
                dstate["n"] += 1
                for f in range(GP):
                    tk.op("pe", lambda f=f, d=d, b0=b0, b1=b1, k=k: nc.tensor.matmul(
                        PD[k][:, 0:b1 - b0], lhsT=WD[g % 2][:, f * D + d * 128: f * D + d * 128 + 128],
                        rhs=Abuf[g % 2][:, f, b0:b1], start=(f == 0), stop=(f == GP - 1)),
                        reads=[bA[g % 2], bWD[g % 2]], writes=[bPD[k]], nowaw=(f > 0))
                tk.op("dve", lambda d=d, b0=b0, b1=b1, k=k: nc.vector.scalar_tensor_tensor(
                    out=X[:, d, b0:b1], in0=PD[k][:, 0:b1 - b0], scalar=ml[:, 2, d:d + 1],
                    in1=X[:, d, b0:b1], op0=ALU.mult, op1=ALU.add),
                    reads=[bPD[k], bC, bX[d]], writes=[bX[d]])
                if deferred:
                    deferred.pop(0)()
            if NCTX and part == GP - 1:
                c0, c1 = 1027, 1027 + NCTX
                k = dstate["n"] % 2
                dstate["n"] += 1
                for d in range(KC):
                    for f in range(GP):
                        tk.op("pe", lambda f=f, d=d, k=k: nc.tensor.matmul(
                            PD[k][:, d * NCTX:(d + 1) * NCTX], lhsT=WD[g % 2][:, f * D + d * 128: f * D + d * 128 + 128],
                            rhs=Abuf[g % 2][:, f, c0:c1], start=(f == 0), stop=(f == GP - 1)),
                            reads=[bA[g % 2], bWD[g % 2]], writes=[bPD[k]], nowaw=not (d == 0 and f == 0))
                for d in range(KC):
                    tk.op("dve", lambda d=d, k=k: nc.vector.scalar_tensor_tensor(
                        out=X[:, d, c0:c1], in0=PD[k][:, d * NCTX:(d + 1) * NCTX], scalar=mc[:, 2, d:d + 1],
                        in1=X[:, d, c0:c1], op0=ALU.mult, op1=ALU.add),
                        reads=[bPD[k], bC, bX[d]], writes=[bX[d]])
                    if deferred:
                        deferred.pop(0)()

        prev_def = []
        for g in range(NG + 1):
            if 1 <= g < NG:
                load_wd(g)
            for j in range(GP):
                p = g * GP + j
                while prev_def:
                    prev_def.pop(0)()
                if g < NG:
                    if p + 2 < NP:
                        load_wu(p + 2)
                    prev_def = up_pair(p)
                if g >= 1:
                    down_part(g - 1, j, [])

        bOut = tk.buf()
        if final:
            gfin = tk.sbuf("gfins", [128, KC], F32)
            bGF = tk.buf()
            tk.dma("sp", gfin[:], gfind[:, :], writes=[bGF])
            for c in range(KC):
                s = c % 2
                tk.op("act", lambda c=c, s=s: nc.scalar.activation(out=SQ[s][:], in_=X[:, c, :], func=AF.Square),
                      reads=[bX[c]], writes=[bSQ[s]])
                for (b0, b1) in banks:
                    tk.op("pe", lambda c=c, s=s, b0=b0, b1=b1: nc.tensor.matmul(
                        PG[:, b0:b1], lhsT=ones[:], rhs=SQ[s][:, b0:b1], start=(c == 0), stop=(c == KC - 1)),
                        reads=[bSQ[s], bC], writes=[bPG], nowaw=(c > 0))
            tk.op("dve", lambda: nc.vector.tensor_scalar(out=RS[:], in0=PG[:, 0:W], scalar1=1.0 / D, scalar2=EPS,
                                                          op0=ALU.mult, op1=ALU.add), reads=[bPG], writes=[bRS])
            tk.op("act", lambda: nc.scalar.activation(out=RS[:], in_=RS[:], func=AF.Sqrt), reads=[bRS], writes=[bRS])
            tk.op("dve", lambda: nc.vector.reciprocal(out=RS[:], in_=RS[:]), reads=[bRS], writes=[bRS])
            for c in range(KC):
                s = c % 2
                tk.op("dve", lambda c=c, s=s: nc.vector.tensor_tensor(out=SQ[s][:], in0=X[:, c, :], in1=RS[:], op=ALU.mult),
                      reads=[bX[c], bRS], writes=[bSQ[s]])
                tk.op("act", lambda c=c, s=s: nc.scalar.activation(out=SQ[s][:], in_=SQ[s][:], func=AF.Identity,
                                                                   scale=gfin[:, c:c + 1]),
                      reads=[bSQ[s], bGF], writes=[bSQ[s]])
                tk.dma("sp", yT[:, c, 0:1024], SQ[s][:, 1:1025], reads=[bSQ[s]], writes=[bOut], nowaw=True)
        for c in range(KC if not final else 0):
            tk.dma("sp", yT[:, c, 0:1024], X[:, c, 1:1025], reads=[bX[c]], writes=[bOut], nowaw=True)
            if NCTX:
                tk.dma("sp", yT[:, c, 1024:1024 + NCTX], X[:, c, 1027:1027 + NCTX], reads=[bX[c]], writes=[bOut],
                       nowaw=True)
        tk.finish("sp", [bOut])
    return nc


def fm(v):
    return np.ascontiguousarray(v.reshape(KC, 128).T)


def ffn_weights(w_up, conv_w, conv_b, w_down):
    wu = w_up.reshape(KC, 128, 2, NP, 128)
    wu = np.ascontiguousarray(wu.transpose(3, 1, 0, 2, 4)).reshape(NP, 128, KC * 256)
    wd = w_down.reshape(NG, GP, 128, D)
    wd = np.ascontiguousarray(wd.transpose(0, 2, 1, 3)).reshape(NG, 128, GP * D)
    cw = np.ascontiguousarray(conv_w.reshape(3, 2 * NP, 128).transpose(2, 1, 0))
    cb = np.ascontiguousarray(conv_b.reshape(2 * NP, 128).T)
    return dict(wup=wu, wdn=wd, cw=cw, cb=cb)


def ffn_inputs(x_lat, x_ctx, NCTX):
    segs, W = ffn_layout(NCTX)
    outs = []
    for i in range(NCORES):
        cols = np.zeros((W, D), np.float32)
        lo, hi = i * 1024, (i + 1) * 1024
        cols[1:1025] = x_lat[lo:hi]
        e = np.zeros((128, 4), np.float32)
        if i > 0:
            cols[0] = x_lat[lo - 1]; e[:, 0] = 1
        if i < NCORES - 1:
            cols[1025] = x_lat[hi]; e[:, 1] = 1
        if NCTX:
            clo, chi = i * NCTX, (i + 1) * NCTX
            cols[1027:1027 + NCTX] = x_ctx[clo:chi]
            if i > 0:
                cols[1026] = x_ctx[clo - 1]; e[:, 2] = 1
            if i < NCORES - 1:
                cols[W - 1] = x_ctx[chi]; e[:, 3] = 1
        xT = np.ascontiguousarray(cols.T.reshape(KC, 128, W).transpose(1, 0, 2))
        outs.append(dict(xT=xT, edge=e))
    return outs


def ffn_outputs(results, NCTX):
    lat, ctx = [], []
    for r in results:
        y = r["yT"]
        t = y.transpose(2, 1, 0).reshape(y.shape[2], D)
        lat.append(t[:1024])
        if NCTX:
            ctx.append(t[1024:])
    return np.concatenate(lat, 0), (np.concatenate(ctx, 0) if NCTX else None)


W1 = 1056
NLAT = 1024
NCTX_FA = 32
SEGS1 = [("lat", 0, 1024), ("ctx", 1024, 1056)]
NOC = 18


def build_fa1(stage=9):
    W = W1
    nc = new_nc()
    xT = dram_in(nc, "xT", [128, KC, W])
    modl = dram_in(nc, "modl", [128, 3, KC])
    modc = dram_in(nc, "modc", [128, 3, KC])
    gd = dram_in(nc, "g", [128, KC])
    win = dram_in(nc, "win", [NOC, 128, KC * 128])
    cosd = dram_in(nc, "cosT", [128, NLAT])
    sind = dram_in(nc, "sinT", [128, NLAT])
    rmd = dram_in(nc, "Rm", [128, 128])
    csd = dram_in(nc, "CS", [128, 2 * 512])
    Zo = dram_out(nc, "Z", [W, 2048], BF16)
    qo = dram_out(nc, "qT", [128, 8, W], BF16)
    ko = dram_out(nc, "kT", [128, W], BF16)
    vo = dram_out(nc, "vT", [128, W], BF16)

    es = contextlib.ExitStack()
    with es:
        tk = Tk(nc, es)
        X = tk.sbuf("X", [128, KC, W], F32)
        H = tk.sbuf("H", [128, KC, W], BF16)
        FT = tk.sbuf("FT", [128, 8, W], BF16)
        tmp = [tk.sbuf(f"tmp{i}", [128, W], F32) for i in range(2)]
        RS = tk.sbuf("RS", [128, W], F32)
        ml = tk.sbuf("ml", [128, 3, KC], F32)
        mc = tk.sbuf("mc", [128, 3, KC], F32)
        g = tk.sbuf("gs", [128, KC], F32)
        AB = tk.sbuf("AB", [128, 2, KC], F32)
        COS = tk.sbuf("COS", [128, NLAT], F32)
        SIN = tk.sbuf("SIN", [128, NLAT], F32)
        RM = tk.sbuf("RM", [128, 128], BF16)
        CS = tk.sbuf("CSs", [128, 2 * 512], BF16)
        QR = [tk.sbuf(f"QR{i}", [128, NLAT], BF16) for i in range(2)]
        QO = [tk.sbuf(f"QO{i}", [128, W], BF16) for i in range(2)]
        ZT = [tk.sbuf(f"ZT{i}", [128, 2048], BF16) for i in range(2)]
        PA = tk.psum("PA", [128, 1536])
        PB = tk.psum("PB", [128, 1536])
        PR = tk.psum("PR", [128, 1024])

        bX = [tk.buf() for _ in range(KC)]
        bH, bFT, bRS, bAB, bC = tk.buf(), tk.buf(), tk.buf(), tk.buf(), tk.buf()
        btmp = [tk.buf() for _ in range(2)]
        bQR = [tk.buf() for _ in range(2)]
        bQO = [tk.buf() for _ in range(2)]
        bZT = [tk.buf() for _ in range(2)]
        bPA, bPB = tk.buf(), tk.buf()
        bPR = [tk.buf(), tk.buf()]

        for c in range(KC):
            tk.dma("sp", X[:, c, :], xT[:, c, :], writes=[bX[c]])
        for (dst, src) in ((ml[:], modl[:, :, :]), (mc[:], modc[:, :, :]), (g[:], gd[:, :]),
                           (COS[:], cosd[:, :]), (SIN[:], sind[:, :])):
            tk.dma("sp", dst, src, writes=[bC], nowaw=True)
        tk.dma("pool", RM[:], rmd[:, :], writes=[bC], nowaw=True)
        tk.dma("pool", CS[:], csd[:, :], writes=[bC], nowaw=True)
        lin = Linear(tk, nc, "win")
        lin.plan([win[i, :, :] for i in range(NOC)])
        ones, bOnes = emit_consts(tk, nc)

        emit_norm_mod(tk, nc, X, bX, H, bH, SEGS1, W, {"lat": ml, "ctx": mc}, g, bC, ones, bOnes, PA, bPA,
                      tmp, btmp, RS, bRS, AB, bAB)

        outbuf = tk.buf()
        PTs = [(PA, bPA), (PB, bPB)]
        for oc in range(NOC):
            if stage <= 2 and oc >= 8:
                break
            if stage in (3, 31, 32, 33) and oc >= 9:
                break
            if stage == 4 and oc >= 17:
                break
            PT, bPT = PTs[oc % 2]
            lin.run(H, bH, W, PT, bPT)
            if oc < 8:
                tk.op("act", lambda oc=oc, PT=PT: nc.scalar.copy(out=FT[:, oc, :], in_=PT[:, 0:W]),
                      reads=[bPT], writes=[bFT], nowaw=True)
                if oc == 7 and stage >= 2:
                    tiles = [(t * 128, 128, 0) for t in range(8)] + [(W - 128, 128, 128 - NCTX_FA)]
                    for ti, (t0, nt, r0) in enumerate(tiles):
                        zt, bzt = ZT[ti % 2], bZT[ti % 2]
                        for gi in range(4):
                            hb = gi % 2
                            for cc in range(2):
                                tk.op("pe", lambda gi=gi, cc=cc, t0=t0, nt=nt, hb=hb: nc.tensor.matmul(
                                    PR[0:nt, hb * 512:(hb + 1) * 512], lhsT=FT[:, 2 * gi + cc, t0:t0 + nt],
                                    rhs=CS[:, cc * 512:(cc + 1) * 512], start=(cc == 0), stop=(cc == 1)),
                                    reads=[bFT, bC], writes=[bPR[hb]], nowaw=(cc > 0))
                            tk.op("act", lambda gi=gi, nt=nt, hb=hb, zt=zt: nc.scalar.copy(
                                out=zt[0:nt, gi * 512:(gi + 1) * 512], in_=PR[0:nt, hb * 512:(hb + 1) * 512]),
                                reads=[bPR[hb]], writes=[bzt], nowaw=(gi > 0))
                        tk.dma("sp", Zo[t0 + r0:t0 + nt, :], zt[r0:nt, :], reads=[bzt], writes=[outbuf], nowaw=True)
            elif oc < 17:
                s = oc % 2
                qr, bqr, qo_, bqo = QR[s], bQR[s], QO[s], bQO[s]
                tk.op("act", lambda qr=qr, PT=PT: nc.scalar.copy(out=qr[:], in_=PT[:, 0:NLAT]),
                      reads=[bPT], writes=[bqr])
                for hb in range(2):
                    tk.op("pe", lambda hb=hb, qr=qr: nc.tensor.matmul(
                        PR[:, hb * 512:(hb + 1) * 512], lhsT=RM[:], rhs=qr[:, hb * 512:(hb + 1) * 512],
                        start=True, stop=True), reads=[bqr, bC], writes=[bPR[hb]])
                if stage == 31:
                    continue
                tk.op("act", lambda PT=PT: nc.scalar.copy(out=tmp[0][:, 0:NLAT], in_=PT[:, 0:NLAT]),
                      reads=[bPT], writes=[btmp[0]])
                tk.op("act", lambda: nc.scalar.copy(out=tmp[1][:, 0:NLAT], in_=PR[:, 0:NLAT]),
                      reads=[bPR[0], bPR[1]], writes=[btmp[1]])
                tk.op("dve", lambda: nc.vector.tensor_tensor(
                    out=tmp[0][:, 0:NLAT], in0=tmp[0][:, 0:NLAT], in1=COS[:], op=ALU.mult),
                    reads=[btmp[0], bC], writes=[btmp[0]])
                tk.op("dve", lambda: nc.vector.tensor_tensor(
                    out=tmp[1][:, 0:NLAT], in0=tmp[1][:, 0:NLAT], in1=SIN[:], op=ALU.mult),
                    reads=[btmp[1], bC], writes=[btmp[1]])
                tk.op("dve", lambda qo_=qo_: nc.vector.tensor_tensor(
                    out=qo_[:, 0:NLAT], in0=tmp[0][:, 0:NLAT], in1=tmp[1][:, 0:NLAT], op=ALU.add),
                    reads=[btmp[0], btmp[1]], writes=[bqo])
                if stage == 32:
                    continue
                tk.op("act", lambda qo_=qo_, PT=PT: nc.scalar.copy(out=qo_[:, NLAT:W], in_=PT[:, NLAT:W]),
                      reads=[bPT], writes=[bqo], nowaw=True)
                if stage == 33:
                    continue
                dst = qo[:, oc - 8, :] if oc < 16 else ko[:, :]
                tk.dma("sp", dst, qo_[:], reads=[bqo], writes=[outbuf], nowaw=True)
            else:
                s = oc % 2
                qo_, bqo = QO[s], bQO[s]
                tk.op("act", lambda qo_=qo_, PT=PT: nc.scalar.copy(out=qo_[:], in_=PT[:, 0:W]),
                      reads=[bPT], writes=[bqo])
                tk.dma("sp", vo[:, :], qo_[:], reads=[bqo], writes=[outbuf], nowaw=True)
        tk.finish("sp", [outbuf])
    return nc


def rope_tables(core):
    t = np.arange(core * 1024, (core + 1) * 1024)
    row, col = t // 64, t % 64
    p = np.arange(128)
    dd = p % 64
    half = dd // 32
    within = dd % 32
    j = within % 16
    part = within // 16
    inv = (10000.0 ** (-np.arange(16, dtype=np.float32) / 16)).astype(np.float32)
    pos = np.where(half[:, None] == 0, row[None, :], col[None, :]).astype(np.float32)
    ang = pos * inv[j][:, None]
    return np.cos(ang).astype(np.float32), np.sin(ang).astype(np.float32)


def rope_rm():
    Rm = np.zeros((128, 128), np.float32)
    for pp in range(128):
        within = (pp % 64) % 32
        if within < 16:
            Rm[pp + 16, pp] = -1.0
        else:
            Rm[pp - 16, pp] = 1.0
    return Rm


def chan_dft_table():
    c = np.arange(256)
    ang = 2 * np.pi * np.outer(c, c) / 256.0
    C = (np.cos(ang) / 16.0).astype(np.float32)
    S = (np.sin(ang) / 16.0).astype(np.float32)
    CS = np.zeros((128, 2, 512), np.float32)
    for cc in range(2):
        CS[:, cc, 0:256] = C[cc * 128:(cc + 1) * 128]
        CS[:, cc, 256:512] = S[cc * 128:(cc + 1) * 128]
    return CS.reshape(128, 1024)


def fa1_xin(x_lat, x_ctx, core):
    cols = np.concatenate([x_lat[core * 1024:(core + 1) * 1024], x_ctx[core * NCTX_FA:(core + 1) * NCTX_FA]], 0)
    return to_fm(cols)


W_FA2 = 1056
NLAT = 1024
NCTX_FA = 32
NSEQ = 8192
NKB = 10
NVT = 12
KW = NKB * 128
SEGS_FA2 = [("lat", 0, 1024), ("ctx", 1024, 1056)]


def build_fa2():
    nc = new_nc()
    Zl = dram_in(nc, "Zl", [NSEQ, 2048], BF16)
    Zc = dram_in(nc, "Zc", [256, 2048], BF16)
    Tl = dram_in(nc, "Tl", [64, 128, 2, 1024], BF16)
    Tc = dram_in(nc, "Tc", [2, 128, 2, NCTX_FA], BF16)
    qd = dram_in(nc, "qT", [128, 8, W_FA2], BF16)
    kd = dram_in(nc, "Kd", [2, 128, KW + 256], BF16)
    vd = dram_in(nc, "Vw", [128, NVT, 128], BF16)
    md = dram_in(nc, "mask", [128, NKB, 384], BF16)
    sd = dram_in(nc, "sink", [128, 16])
    wout = dram_in(nc, "wout", [16, 128, KC * 128])
    xT = dram_in(nc, "xT", [128, KC, W_FA2])
    modl = dram_in(nc, "modl", [128, 3, KC])
    modc = dram_in(nc, "modc", [128, 3, KC])
    yT = dram_out(nc, "yT", [128, KC, W_FA2])

    es = contextlib.ExitStack()
    with es:
        tk = Tk(nc, es)
        CAT = tk.sbuf("CAT", [128, KC, W_FA2], BF16)
        Q = tk.sbuf("Q", [128, 8, W_FA2], BF16)
        ZB = [tk.sbuf(f"ZB{i}", [128, 2048], BF16) for i in range(4)]
        TB = [tk.sbuf(f"TB{i}", [128, 2, 512], BF16) for i in range(4)]
        KD = tk.sbuf("KD", [128, 2, KW + 256], BF16)
        VW = tk.sbuf("VW", [128, NVT, 128], BF16)
        VA = tk.sbuf("VA", [128, 2, 2, NVT, 128], BF16)
        MK = tk.sbuf("MK", [128, NKB, 384], BF16)
        SK = tk.sbuf("SK", [128, 16], F32)
        EB = [tk.sbuf(f"EB{i}", [128, 512], BF16) for i in range(4)]
        RD = tk.sbuf("RD", [128, W_FA2], F32)
        NUM = tk.sbuf("NUM", [128, W_FA2], F32)
        XB = [tk.sbuf(f"XB{i}", [128, W_FA2], F32) for i in range(3)]
        ml = tk.sbuf("ml", [128, 3, KC], F32)
        mc = tk.sbuf("mc", [128, 3, KC], F32)
        ALL = tk.psum("ALL", [128, 4096])

        def bank(b, n=1):
            return ALL[:, b * 512:(b + n) * 512]

        bBank = [tk.buf(f"bank{i}") for i in range(8)]
        bCAT, bQ, bKD, bVW, bVA, bMK, bSK, bC = (tk.buf() for _ in range(8))
        bZB = [tk.buf() for _ in range(4)]
        bTB = [tk.buf() for _ in range(4)]
        bEB = [tk.buf() for _ in range(4)]
        bRD, bNUM = tk.buf(), tk.buf()
        bXB = [tk.buf() for _ in range(3)]

        tk.dma("sp", Q[:], qd[:, :, :], writes=[bQ])
        for kv in range(2):
            tk.dma("sp", KD[:, kv, :], kd[kv, :, :], writes=[bKD], nowaw=True)
        tk.dma("sp", VW[:], vd[:, :, :], writes=[bVW])
        tk.dma("sp", MK[:], md[:, :, :], writes=[bMK])
        tk.dma("sp", SK[:], sd[:, :], writes=[bSK])
        tk.dma("sp", ml[:], modl[:, :, :], writes=[bC], nowaw=True)
        tk.dma("sp", mc[:], modc[:, :, :], writes=[bC], nowaw=True)
        lin = Linear(tk, nc, "wout")
        lin.plan([wout[i, :, :] for i in range(16)])

        for hb in range(2):
            for n in range(64):
                s = n % 4
                tk.dma("sp", ZB[s][:], Zl[n * 128:(n + 1) * 128, :], writes=[bZB[s]])
                tk.dma("sp", TB[s][:], Tl[n, :, :, hb * 512:(hb + 1) * 512], writes=[bTB[s]])
                for fc in range(8):
                    g, ch = divmod(fc, 2)
                    for pq in range(2):
                        tk.op("pe", lambda fc=fc, g=g, ch=ch, pq=pq, s=s, n=n: nc.tensor.matmul(
                            bank(fc), lhsT=ZB[s][:, g * 512 + pq * 256 + ch * 128: g * 512 + pq * 256 + ch * 128 + 128],
                            rhs=TB[s][:, pq, :], start=(n == 0 and pq == 0), stop=(n == 63 and pq == 1)),
                            reads=[bZB[s], bTB[s]], writes=[bBank[fc]], nowaw=not (n == 0 and pq == 0))
            for fc in range(8):
                tk.op("act", lambda fc=fc, hb=hb: nc.scalar.copy(out=CAT[:, fc, hb * 512:(hb + 1) * 512], in_=bank(fc)),
                      reads=[bBank[fc]], writes=[bCAT], nowaw=True)
        for n in range(2):
            s = n % 4
            tk.dma("sp", ZB[s][:], Zc[n * 128:(n + 1) * 128, :], writes=[bZB[s]])
            tk.dma("sp", TB[s][:, :, 0:NCTX_FA], Tc[n, :, :, :], writes=[bTB[s]])
            for fc in range(8):
                g, ch = divmod(fc, 2)
                for pq in range(2):
                    tk.op("pe", lambda fc=fc, g=g, ch=ch, pq=pq, s=s, n=n: nc.tensor.matmul(
                        ALL[:, fc * 512: fc * 512 + NCTX_FA],
                        lhsT=ZB[s][:, g * 512 + pq * 256 + ch * 128: g * 512 + pq * 256 + ch * 128 + 128],
                        rhs=TB[s][:, pq, 0:NCTX_FA], start=(n == 0 and pq == 0), stop=(n == 1 and pq == 1)),
                        reads=[bZB[s], bTB[s]], writes=[bBank[fc]], nowaw=not (n == 0 and pq == 0))
        for fc in range(8):
            tk.op("act", lambda fc=fc: nc.scalar.copy(out=CAT[:, fc, NLAT:W_FA2], in_=ALL[:, fc * 512: fc * 512 + NCTX_FA]),
                  reads=[bBank[fc]], writes=[bCAT], nowaw=True)

        tk.op("act", lambda: nc.scalar.activation(out=SK[:], in_=SK[:], func=AF.Exp), reads=[bSK], writes=[bSK])
        tk.op("dve", lambda: nc.vector.memset(VA[:], 1.0), writes=[bVA])
        for kv in range(2):
            tk.op("dve", lambda kv=kv: nc.vector.tensor_copy(out=VA[:, 0, kv, :, 0:64], in_=VW[:, :, kv * 64:(kv + 1) * 64]),
                  reads=[bVW], writes=[bVA])
            tk.op("dve", lambda kv=kv: nc.vector.tensor_copy(out=VA[:, 1, kv, :, 64:128], in_=VW[:, :, kv * 64:(kv + 1) * 64]),
                  reads=[bVW], writes=[bVA])

        ectr = {"n": 0}
        allwork = []
        cb_banks = col_banks(W_FA2)
        for h in range(16):
            ch, var = divmod(h, 2)
            base = var * 64
            dbase = 64 - base
            kv = h // 8
            pob = 2 + 3 * (h % 2)
            bPO = bBank[pob:pob + 3]
            work = []
            for cbk in range(2):
                for (c0, c1) in cb_banks:
                    work.append((NKB + cbk, KW + cbk * 128, c0, c1, None))
            for j in range(NKB):
                q0, q1 = max(0, j - 2) * 128, min(8, j + 1) * 128
                pieces = [(q0, q1)] if (q0 // 512 == (q1 - 1) // 512) else [(q0, 512), (512, q1)]
                for (c0, c1) in pieces:
                    work.append((j, j * 128, c0, c1, (j, c0 - (j - 2) * 128, c1 - (j - 2) * 128)))
            last = {}
            first = {}
            for wi, wk in enumerate(work):
                b = wk[2] // 512
                last[b] = wi
                first.setdefault(b, wi)
            hctx = dict(h=h, ch=ch, var=var, base=base, dbase=dbase, kv=kv, pob=pob, bPO=bPO, first=first, last=last,
                        nwork=len(work))
            for wi, wk in enumerate(work):
                allwork.append((hctx, wi, wk))

        def emit_front(hc, wi, wk):
            (vt, kc0, c0, c1, msk) = wk
            base, kv, ch = hc["base"], hc["kv"], hc["ch"]
            sb = ectr["n"] % 2
            e = ectr["n"] % 4
            ectr["n"] += 1
            n = c1 - c0
            tk.op("pe", lambda: nc.tensor.matmul(
                ALL[:, sb * 512: sb * 512 + n], lhsT=KD[base:base + 64, kv, kc0:kc0 + 128],
                rhs=Q[base:base + 64, ch, c0:c1], start=True, stop=True),
                reads=[bKD, bQ], writes=[bBank[sb]])
            tk.op("act", lambda: nc.scalar.activation(
                out=EB[e][:, 0:n], in_=ALL[:, sb * 512: sb * 512 + n], func=AF.Exp, scale=0.125),
                reads=[bBank[sb]], writes=[bEB[e]])
            if msk is not None:
                mj, m0, m1 = msk
                tk.op("dve", lambda: nc.vector.tensor_tensor(
                    out=EB[e][:, 0:n], in0=EB[e][:, 0:n], in1=MK[:, mj, m0:m1], op=ALU.mult),
                    reads=[bEB[e], bMK], writes=[bEB[e]])
            return e

        def emit_back(hc, wi, wk, e):
            (vt, kc0, c0, c1, msk) = wk
            base, dbase, kv, ch, var, pob, bPO, h = (hc[k] for k in ("base", "dbase", "kv", "ch", "var", "pob", "bPO", "h"))
            first, last = hc["first"], hc["last"]
            n = c1 - c0
            b = c0 // 512
            tk.op("pe", lambda: nc.tensor.matmul(
                ALL[:, pob * 512 + c0: pob * 512 + c1], lhsT=VA[:, var, kv, vt, :], rhs=EB[e][:, 0:n],
                start=(first[b] == wi), stop=(last[b] == wi)),
                reads=[bVA, bEB[e]], writes=[bPO[b]], nowaw=(first[b] != wi))
            if wi != hc["nwork"] - 1:
                return
            PO = ALL[:, pob * 512: pob * 512 + W_FA2]
            tk.op("dve", lambda: nc.vector.tensor_scalar(
                out=RD[base:base + 64, :], in0=PO[dbase:dbase + 64, :], scalar1=SK[dbase:dbase + 64, h:h + 1],
                scalar2=None, op0=ALU.add), reads=bPO + [bSK], writes=[bRD])
            tk.op("dve", lambda: nc.vector.reciprocal(out=RD[base:base + 64, :], in_=RD[base:base + 64, :]),
                  reads=[bRD], writes=[bRD])
            tk.op("act", lambda: nc.scalar.copy(out=NUM[base:base + 64, :], in_=PO[base:base + 64, :]),
                  reads=bPO, writes=[bNUM])
            tk.op("dve", lambda: nc.vector.tensor_tensor(
                out=CAT[base:base + 64, 8 + ch, :], in0=NUM[base:base + 64, :], in1=RD[base:base + 64, :],
                op=ALU.mult), reads=[bNUM, bRD], writes=[bCAT], nowaw=True)

        DEPTH = 2
        pend = []
        for (hc, wi, wk) in allwork:
            e = emit_front(hc, wi, wk)
            pend.append((hc, wi, wk, e))
            if len(pend) > DEPTH:
                emit_back(*pend.pop(0))
        while pend:
            emit_back(*pend.pop(0))

        bOut = tk.buf()
        for oc in range(16):
            pb = 3 * (oc % 2)
            PT = ALL[:, pb * 512: pb * 512 + 1536]
            xs = oc % 3
            tk.dma("sp", XB[xs][:], xT[:, oc, :], writes=[bXB[xs]])
            lin.run(CAT, bCAT, W_FA2, PT, bBank[pb:pb + 3])
            for (nm, s0, s1) in SEGS_FA2:
                m = ml if nm == "lat" else mc
                tk.op("dve", lambda oc=oc, s0=s0, s1=s1, m=m, PT=PT, xs=xs: nc.vector.scalar_tensor_tensor(
                    out=XB[xs][:, s0:s1], in0=PT[:, s0:s1], scalar=m[:, 2, oc:oc + 1], in1=XB[xs][:, s0:s1],
                    op0=ALU.mult, op1=ALU.add), reads=bBank[pb:pb + 3] + [bXB[xs], bC], writes=[bXB[xs]])
            tk.dma("sp", yT[:, oc, :], XB[xs][:], reads=[bXB[xs]], writes=[bOut], nowaw=True)
        tk.finish("sp", [bOut])
    return nc


def seq_tables(core):
    n = np.arange(NSEQ, dtype=np.int64)
    k = np.arange(core * 1024, (core + 1) * 1024, dtype=np.int64)
    ph = (np.outer(n, k) % NSEQ).astype(np.float64) * (2 * np.pi / NSEQ)
    sc = 1.0 / np.sqrt(NSEQ)
    T = np.stack([np.cos(ph) * sc, -np.sin(ph) * sc], 1)
    Tl = T.reshape(64, 128, 2, 1024).astype(ml_dtypes.bfloat16)
    n = np.arange(256, dtype=np.int64)
    k = np.arange(core * NCTX_FA, (core + 1) * NCTX_FA, dtype=np.int64)
    ph = (np.outer(n, k) % 256).astype(np.float64) * (2 * np.pi / 256)
    T = np.stack([np.cos(ph) / 16.0, -np.sin(ph) / 16.0], 1)
    Tc = T.reshape(2, 128, 2, NCTX_FA).astype(ml_dtypes.bfloat16)
    return Tl, Tc


def attn_mask(core):
    lo = core * 1024
    kk = np.arange(128)
    m = np.zeros((128, NKB, 384), np.float32)
    for j in range(NKB):
        kpos = lo - 128 + 128 * j + kk
        qpos = lo + (j - 2) * 128 + np.arange(384)
        ok = (np.abs(kpos[:, None] - qpos[None, :]) <= 128) & (kpos[:, None] >= 0) & (kpos[:, None] < NSEQ)
        m[:, j, :] = ok
    return m.astype(ml_dtypes.bfloat16)


def fa2_kv(kT_all, kcT, vT_all, vcT, core):
    lo, hi = core * 1024, (core + 1) * 1024
    kw = np.zeros((128, KW), kT_all.dtype)
    vw = np.zeros((128, KW), vT_all.dtype)
    a, b = max(lo - 128, 0), min(hi + 128, NSEQ)
    kw[:, a - (lo - 128): b - (lo - 128)] = kT_all[:, a:b]
    vw[:, a - (lo - 128): b - (lo - 128)] = vT_all[:, a:b]
    kfull = np.concatenate([kw, kcT], 1)
    Kd = np.stack([np.concatenate([kfull[kv * 64:(kv + 1) * 64]] * 2, 0) for kv in range(2)], 0)
    vfull = np.concatenate([vw, vcT], 1).T
    Vw = np.ascontiguousarray(vfull.reshape(NVT, 128, 128).transpose(1, 0, 2))
    return np.ascontiguousarray(Kd), Vw


W_RG = 1062
NCTX_RG = 32
SEGS_RG = [("lat", 0, 1027), ("ctx", 1027, 1062)]
VALID = [(2, 1026), (1029, 1061)]
HALO = [0, 1, 1026, 1027, 1028, 1061]
NOUT_RG = 1024 + NCTX_RG
LRU_C = 8.0
NT = 11


def rev(t2d):
    return bass.AP(tensor=t2d.tensor, offset=t2d.offset + (t2d.ap[-1][1] - 1) * t2d.ap[-1][0],
                   ap=[list(t2d.ap[0]), [-t2d.ap[-1][0], t2d.ap[-1][1]]])


def build_rg(phase):
    nc = new_nc()
    xT = dram_in(nc, "xT", [128, KC, W_RG])
    edge = dram_in(nc, "edge", [128, 6])
    modl = dram_in(nc, "modl", [128, 3, KC])
    modc = dram_in(nc, "modc", [128, 3, KC])
    gd = dram_in(nc, "g", [128, KC])
    wall = dram_in(nc, "wall", [32 if phase == 1 else 48, 128, KC * 128])
    cwd = dram_in(nc, "cw", [128, KC, 4])
    cbd = dram_in(nc, "cb", [128, KC])
    wgd = dram_in(nc, "wg", [128, 64 * 256])
    gbd = dram_in(nc, "gb", [128, 64])
    lamd = dram_in(nc, "lam", [128, 32])
    if phase == 2:
        cad = dram_in(nc, "CA", [128, 64, 16])
        cbd2 = dram_in(nc, "CB", [128, 64, 16])
        yT = dram_out(nc, "yT", [128, KC, NOUT_RG])
    else:
        So = dram_out(nc, "S", [128, 2, 2, 2, KC])
        Y0o = dram_out(nc, "Y0", [128, KC, NOUT_RG], BF16)
        PFo = dram_out(nc, "PF", [128, KC, NOUT_RG], BF16)
        PRo = dram_out(nc, "PR", [128, KC, NOUT_RG], BF16)

    es = contextlib.ExitStack()
    with es:
        tk = Tk(nc, es)
        H = tk.sbuf("H", [128, KC, W_RG], BF16)
        XCB = tk.sbuf("XCB", [128, KC, W_RG], BF16)
        if phase == 2:
            Y = tk.sbuf("Y", [128, KC, W_RG], BF16)
        else:
            OB = [tk.sbuf(f"OB{i}", [128, 3, W_RG], BF16) for i in range(2)]
            bOB = [tk.buf() for _ in range(2)]
            PC = [tk.sbuf(f"PC{i}", [128, W_RG], F32) for i in range(2)]
            bPC = [tk.buf() for _ in range(2)]
        WG = tk.sbuf("WG", [128, 64 * 256], BF16)
        T = [tk.sbuf(f"T{i}", [128, W_RG], F32) for i in range(NT)]
        XT = [tk.sbuf(f"XT{i}", [128, W_RG], F32) for i in range(4)]
        XB = [tk.sbuf(f"XB{i}", [128, W_RG], F32) for i in range(2)]
        edg = tk.sbuf("edg", [128, 6], F32)
        ml = tk.sbuf("ml", [128, 3, KC], F32)
        mc = tk.sbuf("mc", [128, 3, KC], F32)
        g = tk.sbuf("gs", [128, KC], F32)
        AB = tk.sbuf("AB", [128, 2, KC], F32)
        cw = tk.sbuf("cws", [128, KC, 4], F32)
        cb = tk.sbuf("cbs", [128, KC], F32)
        nb = tk.sbuf("nb", [128, 64], F32)
        c8 = tk.sbuf("c8", [128, 32], F32)
        c16 = tk.sbuf("c16", [128, 32], F32)
        S = tk.sbuf("Ssum", [128, 2, 2, 2, KC], F32)
        H0 = tk.sbuf("H0", [128, 64], F32)
        if phase == 2:
            CA = tk.sbuf("CAs", [128, 64, 16], F32)
            CB = tk.sbuf("CBs", [128, 64, 16], F32)
            CT = tk.sbuf("CT", [128, 16], F32)
        PA = tk.psum("PA", [128, 1536])
        PB = tk.psum("PB", [128, 1536])

        bH, bXCB, bY, bWG, bC, bAB, bS, bH0 = (tk.buf() for _ in range(8))
        bT = [tk.buf() for _ in range(NT)]
        bXT = [tk.buf() for _ in range(4)]
        bXB = [tk.buf() for _ in range(2)]
        bPA, bPB = tk.buf(), tk.buf()
        PS = [(PA, bPA), (PB, bPB)]
        pctr = {"n": 0}

        def next_ps():
            p = PS[pctr["n"] % 2]
            pctr["n"] += 1
            return p

        for (dst, src) in ((edg[:], edge[:, :]), (ml[:], modl[:, :, :]), (mc[:], modc[:, :, :]), (g[:], gd[:, :]),
                           (cw[:], cwd[:, :, :]), (cb[:], cbd[:, :]), (nb[:], gbd[:, :]), (c8[:], lamd[:, :])):
            tk.dma("sp", dst, src, writes=[bC], nowaw=True)
        if phase == 2:
            tk.dma("sp", CA[:], cad[:, :, :], writes=[bC], nowaw=True)
            tk.dma("sp", CB[:], cbd2[:, :, :], writes=[bC], nowaw=True)
        tk.dma("pool", WG[:], wgd[:, :], writes=[bWG])
        lin = Linear(tk, nc, "wall", nslots=2)
        order = [0, 1]
        for c in range(KC):
            if c + 2 < KC:
                order.append(c + 2)
            order.append(16 + c)
        if phase == 2:
            order += list(range(32, 48))
        lin.plan([wall[i, :, :] for i in order])
        ones, bOnes = emit_consts(tk, nc)
        bXCBc_init = []
        tk.op("dve", lambda: nc.vector.memset(XCB[:], 0.0), writes=[bXCB])
        tk.op("dve", lambda: nc.vector.memset(T[9][:], 0.0), writes=[bT[9]])
        ZERO = T[9]
        tk.op("act", lambda: nc.scalar.activation(out=c8[:], in_=c8[:], func=AF.Exp, scale=-1.0), reads=[bC], writes=[bC])
        tk.op("act", lambda: nc.scalar.activation(out=c8[:], in_=c8[:], func=AF.Ln, bias=1.0), reads=[bC], writes=[bC])
        tk.op("dve", lambda: nc.vector.tensor_scalar(out=c16[:], in0=c8[:], scalar1=-2.0 * LRU_C, scalar2=None, op0=ALU.mult),
              reads=[bC], writes=[bC])
        tk.op("dve", lambda: nc.vector.tensor_scalar(out=c8[:], in0=c8[:], scalar1=-LRU_C, scalar2=None, op0=ALU.mult),
              reads=[bC], writes=[bC])

        if phase == 2:
            for idx in range(64):
                tk.op("dve", lambda idx=idx: nc.vector.tensor_tensor_scan(
                    out=CT[:], data0=CA[:, idx, :], data1=CB[:, idx, :], initial=0.0, op0=ALU.mult, op1=ALU.add),
                    reads=[bC], writes=[bH0])
                tk.op("dve", lambda idx=idx: nc.vector.tensor_copy(out=H0[:, idx:idx + 1], in_=CT[:, 15:16]),
                      reads=[bH0], writes=[bH0])

        class XS_:
            n = 0

        def xload(c):
            s = XS_.n % 2
            XS_.n += 1
            tk.dma("sp", XB[s][:], xT[:, c, :], writes=[bXB[s]])
            return XB[s], bXB[s]

        class XView:
            cur = {}

        banks = col_banks(W_RG)
        mods = {"lat": ml, "ctx": mc}
        for i, (nm, _, _) in enumerate(SEGS_RG):
            m = mods[nm]
            tk.op("dve", lambda i=i, m=m: nc.vector.tensor_scalar(
                out=AB[:, i, :], in0=m[:, 1, :], scalar1=1.0, scalar2=None, op0=ALU.add),
                reads=[bC], writes=[bAB], nowaw=True)
            tk.op("dve", lambda i=i: nc.vector.tensor_tensor(
                out=AB[:, i, :], in0=AB[:, i, :], in1=g[:], op=ALU.mult), reads=[bC, bAB], writes=[bAB])
        for c in range(KC):
            xb, bxb = xload(c)
            s = c % 2
            tk.op("act", lambda xb=xb, s=s: nc.scalar.activation(out=T[s][:], in_=xb[:], func=AF.Square),
                  reads=[bxb], writes=[bT[s]])
            for (b0, b1) in banks:
                tk.op("pe", lambda c=c, s=s, b0=b0, b1=b1: nc.tensor.matmul(
                    PA[:, b0:b1], lhsT=ones[:], rhs=T[s][:, b0:b1], start=(c == 0), stop=(c == KC - 1)),
                    reads=[bT[s], bOnes], writes=[bPA], nowaw=(c > 0))
        RS, bRS = T[2], bT[2]
        tk.op("dve", lambda: nc.vector.tensor_scalar(out=RS[:], in0=PA[:, 0:W_RG], scalar1=1.0 / D, scalar2=EPS,
                                                      op0=ALU.mult, op1=ALU.add), reads=[bPA], writes=[bRS])
        tk.op("act", lambda: nc.scalar.activation(out=RS[:], in_=RS[:], func=AF.Sqrt), reads=[bRS], writes=[bRS])
        tk.op("dve", lambda: nc.vector.reciprocal(out=RS[:], in_=RS[:]), reads=[bRS], writes=[bRS])
        for c in range(KC):
            xb, bxb = xload(c)
            s = c % 2
            tk.op("dve", lambda xb=xb, s=s: nc.vector.tensor_tensor(out=T[s][:], in0=xb[:], in1=RS[:], op=ALU.mult),
                  reads=[bxb, bRS], writes=[bT[s]])
            for i, (nm, s0, s1) in enumerate(SEGS_RG):
                m = mods[nm]
                tk.op("act", lambda c=c, s=s, s0=s0, s1=s1, i=i, m=m: nc.scalar.activation(
                    out=H[:, c, s0:s1], in_=T[s][:, s0:s1], func=AF.Identity,
                    bias=m[:, 0, c:c + 1], scale=AB[:, i, c:c + 1]),
                    reads=[bT[s], bAB, bC], writes=[bH], nowaw=True)
        for k, col in enumerate(HALO):
            tk.op("dve", lambda k=k, col=col: nc.vector.tensor_scalar(
                out=H[:, :, col:col + 1], in0=H[:, :, col:col + 1], scalar1=edg[:, k:k + 1], scalar2=None,
                op0=ALU.mult), reads=[bH, bC], writes=[bH])

        def xs_chunk(c):
            PT, bPT = next_ps()
            lin.run(H, bH, W_RG, PT, bPT)
            s = c % 2
            xs, bxs = XT[s], bXT[s]
            xc, bxc = XT[2 + s], bXT[2 + s]
            tk.op("act", lambda PT=PT, xs=xs: nc.scalar.copy(out=xs[:], in_=PT[:, 0:W_RG]), reads=[bPT], writes=[bxs])
            tk.op("dve", lambda c=c, xs=xs, xc=xc: nc.vector.tensor_scalar(
                out=xc[:, 2:W_RG - 1], in0=xs[:, 0:W_RG - 3], scalar1=cw[:, c, 0:1], scalar2=cb[:, c:c + 1],
                op0=ALU.mult, op1=ALU.add), reads=[bxs, bC], writes=[bxc])
            for k in range(1, 4):
                last = (k == 3)
                dst = XCB[:, c, 2:W_RG - 1] if last else xc[:, 2:W_RG - 1]
                tk.op("dve", lambda c=c, xs=xs, xc=xc, k=k, dst=dst: nc.vector.scalar_tensor_tensor(
                    out=dst, in0=xs[:, k:W_RG - 3 + k], scalar=cw[:, c, k:k + 1], in1=xc[:, 2:W_RG - 1],
                    op0=ALU.mult, op1=ALU.add), reads=[bxs, bxc, bC],
                    writes=([bXCBc[c]] if last else [bxc]), nowaw=last)

        bXCBc = [tk.buf() for _ in range(KC)]
        xs_chunk(0)
        xs_chunk(1)

        bOutA = tk.buf()
        if phase == 1:
            for d in range(2):
                tk.op("dve", lambda d=d: nc.vector.memset(PC[d][:], 0.0), writes=[bPC[d]])
        def sigmoid_from_psum(PT, bPT, dst, bdst, bias_ap):
            tk.op("act", lambda: nc.scalar.activation(out=dst[:], in_=PT[:, 0:W_RG], func=AF.Sigmoid, bias=bias_ap),
                  reads=[bPT, bC], writes=[bdst])

        for c in range(KC):
            hb = c // 2
            if c + 2 < KC:
                xs_chunk(c + 2)
            HS = []
            for d in range(2):
                tr, btr = T[3 * d + 0], bT[3 * d + 0]
                ti, bti = T[3 * d + 1], bT[3 * d + 1]
                ta, bta = T[3 * d + 2], bT[3 * d + 2]
                hs, bhs = T[6 + d], bT[6 + d]
                for typ, (dst, bdst) in enumerate(((tr, btr), (ti, bti))):
                    PT, bPT = next_ps()
                    widx = (d * 2 + typ) * 16 + c
                    for ic in range(2):
                        for (b0, b1) in banks:
                            tk.op("pe", lambda ic=ic, b0=b0, b1=b1, PT=PT, widx=widx: nc.tensor.matmul(
                                PT[:, b0:b1], lhsT=WG[:, widx * 256 + ic * 128: widx * 256 + ic * 128 + 128],
                                rhs=XCB[:, hb * 2 + ic, b0:b1], start=(ic == 0), stop=(ic == 1)),
                                reads=[bXCB, bXCBc[hb * 2 + ic], bWG], writes=[bPT], nowaw=(ic > 0))
                    sigmoid_from_psum(PT, bPT, dst, bdst, nb[:, widx:widx + 1])
                tk.op("act", lambda d=d, tr=tr, ta=ta: nc.scalar.activation(
                    out=ta[:], in_=tr[:], func=AF.Exp, scale=c8[:, d * 16 + c: d * 16 + c + 1]),
                    reads=[btr, bC], writes=[bta])
                tk.op("act", lambda d=d, tr=tr: nc.scalar.activation(
                    out=tr[:], in_=tr[:], func=AF.Exp, scale=c16[:, d * 16 + c: d * 16 + c + 1]),
                    reads=[btr, bC], writes=[btr])
                tk.op("act", lambda tr=tr: nc.scalar.activation(out=tr[:], in_=tr[:], func=AF.Ln, scale=-1.0, bias=1.0),
                      reads=[btr], writes=[btr])
                tk.op("act", lambda tr=tr: nc.scalar.activation(out=tr[:], in_=tr[:], func=AF.Exp, scale=0.5),
                      reads=[btr], writes=[btr])
                tk.op("dve", lambda ti=ti: nc.vector.tensor_tensor(out=ti[:], in0=ti[:], in1=XCB[:, c, :], op=ALU.mult),
                      reads=[bti, bXCB, bXCBc[c]], writes=[bti])
                tk.op("dve", lambda ti=ti, tr=tr: nc.vector.tensor_tensor(out=ti[:], in0=ti[:], in1=tr[:], op=ALU.mult),
                      reads=[bti, btr], writes=[bti])
                for si, (v0, v1) in enumerate(VALID):
                    idx = (d * 2 + si) * 16 + c
                    init = 0.0 if phase == 1 else H0[:, idx:idx + 1]
                    o_, a_, g_ = hs[:, v0:v1], ta[:, v0:v1], ti[:, v0:v1]
                    if d == 1:
                        o_, a_, g_ = rev(o_), rev(a_), rev(g_)
                    tk.op("dve", lambda o_=o_, a_=a_, g_=g_, init=init: nc.vector.tensor_tensor_scan(
                        out=o_, data0=a_, data1=g_, initial=init, op0=ALU.mult, op1=ALU.add),
                        reads=[bta, bti, bH0], writes=[bhs], nowaw=(si > 0))
                    if phase == 1:
                        endc = (v1 - 1) if d == 0 else v0
                        tk.op("dve", lambda hs=hs, d=d, si=si, endc=endc: nc.vector.tensor_copy(
                            out=S[:, d, si, 1, c:c + 1], in_=hs[:, endc:endc + 1]), reads=[bhs], writes=[bS], nowaw=True)
                        z_ = ZERO[:, v0:v1]
                        p_ = PC[d][:, v0:v1]
                        if d == 1:
                            z_, p_ = rev(z_), rev(p_)
                        tk.op("dve", lambda p_=p_, a_=a_, z_=z_: nc.vector.tensor_tensor_scan(
                            out=p_, data0=a_, data1=z_, initial=1.0, op0=ALU.mult, op1=ALU.add),
                            reads=[bta, bT[9]], writes=[bPC[d]], nowaw=(si > 0))
                        tk.op("dve", lambda d=d, si=si, endc=endc: nc.vector.tensor_copy(
                            out=S[:, d, si, 0, c:c + 1], in_=PC[d][:, endc:endc + 1]), reads=[bPC[d]], writes=[bS], nowaw=True)
                HS.append((hs, bhs))
            if True:
                PT, bPT = next_ps()
                lin.run(H, bH, W_RG, PT, bPT)
                gt_, bgt = T[8], bT[8]
                t1, bt1 = T[10], bT[10]
                tk.op("act", lambda PT=PT: nc.scalar.copy(out=gt_[:], in_=PT[:, 0:W_RG]), reads=[bPT], writes=[bgt])
                tk.op("dve", lambda: nc.vector.tensor_tensor(out=t1[:], in0=gt_[:], in1=gt_[:], op=ALU.mult),
                      reads=[bgt], writes=[bt1])
                tk.op("dve", lambda: nc.vector.tensor_scalar(out=t1[:], in0=t1[:], scalar1=0.044715, scalar2=1.0,
                                                              op0=ALU.mult, op1=ALU.add), reads=[bt1], writes=[bt1])
                tk.op("dve", lambda: nc.vector.tensor_tensor(out=t1[:], in0=t1[:], in1=gt_[:], op=ALU.mult),
                      reads=[bt1, bgt], writes=[bt1])
                tk.op("act", lambda: nc.scalar.activation(out=t1[:], in_=t1[:], func=AF.Sigmoid, scale=1.5957691216),
                      reads=[bt1], writes=[bt1])
                tk.op("dve", lambda: nc.vector.tensor_tensor(out=gt_[:], in0=gt_[:], in1=t1[:], op=ALU.mult),
                      reads=[bt1, bgt], writes=[bgt])
                (h0_, bh0_), (h1_, bh1_) = HS
                tk.op("dve", lambda: nc.vector.tensor_tensor(out=h0_[:], in0=h0_[:], in1=h1_[:], op=ALU.add),
                      reads=[bh0_, bh1_], writes=[bh0_])
                if phase == 2:
                    tk.op("dve", lambda c=c: nc.vector.tensor_tensor(out=Y[:, c, :], in0=h0_[:], in1=gt_[:], op=ALU.mult),
                          reads=[bh0_, bgt], writes=[bY], nowaw=True)
                else:
                    ob, bob = OB[c % 2], bOB[c % 2]
                    tk.op("pool", lambda ob=ob: nc.gpsimd.tensor_tensor(out=ob[:, 0, :], in0=h0_[:], in1=gt_[:], op=ALU.mult),
                          reads=[bh0_, bgt], writes=[bob])
                    for d in range(2):
                        tk.op("pool", lambda ob=ob, d=d: nc.gpsimd.tensor_tensor(
                            out=ob[:, 1 + d, :], in0=PC[d][:], in1=gt_[:], op=ALU.mult),
                            reads=[bPC[d], bgt], writes=[bob], nowaw=True)
                    for k, dst in enumerate((Y0o, PFo, PRo)):
                        tk.dma("sp", dst[:, c, 0:1024], ob[:, k, 2:1026], reads=[bob], writes=[bOutA], nowaw=True)
                        tk.dma("sp", dst[:, c, 1024:NOUT_RG], ob[:, k, 1029:1061], reads=[bob], writes=[bOutA], nowaw=True)

        bOut = tk.buf()
        if phase == 1:
            tk.dma("sp", So[:, :, :, :, :], S[:], reads=[bS], writes=[bOut])
            tk.finish("sp", [bOutA])
        else:
            for oc in range(KC):
                PT, bPT = next_ps()
                xb, bxb = xload(oc)
                lin.run(Y, bY, W_RG, PT, bPT)
                for (nm, s0, s1) in SEGS_RG:
                    m = ml if nm == "lat" else mc
                    tk.op("dve", lambda oc=oc, s0=s0, s1=s1, m=m, PT=PT, xb=xb: nc.vector.scalar_tensor_tensor(
                        out=xb[:, s0:s1], in0=PT[:, s0:s1], scalar=m[:, 2, oc:oc + 1], in1=xb[:, s0:s1],
                        op0=ALU.mult, op1=ALU.add), reads=[bPT, bxb, bC], writes=[bxb])
                tk.dma("sp", yT[:, oc, 0:1024], xb[:, 2:1026], reads=[bxb], writes=[bOut], nowaw=True)
                tk.dma("sp", yT[:, oc, 1024:NOUT_RG], xb[:, 1029:1061], reads=[bxb], writes=[bOut], nowaw=True)
        tk.finish("sp", [bOut])
    return nc


def rg_xin(x_lat, x_ctx, core):
    cols = np.zeros((W_RG, D), np.float32)
    e = np.zeros((128, 6), np.float32)
    lo, hi = core * 1024, (core + 1) * 1024
    cols[2:1026] = x_lat[lo:hi]
    if core > 0:
        cols[0:2] = x_lat[lo - 2:lo]; e[:, 0:2] = 1
    if core < NCORES - 1:
        cols[1026] = x_lat[hi]; e[:, 2] = 1
    clo, chi = core * NCTX_RG, (core + 1) * NCTX_RG
    cols[1029:1061] = x_ctx[clo:chi]
    if core > 0:
        cols[1027:1029] = x_ctx[clo - 2:clo]; e[:, 3:5] = 1
    if core < NCORES - 1:
        cols[1061] = x_ctx[chi]; e[:, 5] = 1
    return to_fm(cols), e


def rg_weights(w_in, conv_w, conv_b, w_a, b_a, w_i, b_i, lam, w_out):
    wall = np.concatenate([tile_w(w_in[:, D:]), tile_w(w_in[:, :D]), tile_w(w_out)], 0)
    cw = np.ascontiguousarray(conv_w.reshape(4, KC, 128).transpose(2, 1, 0))
    cb = fm(conv_b)
    wg = np.zeros((128, 64, 256), np.float32)
    gb = np.zeros((128, 64), np.float32)
    for d in range(2):
        for typ, (wm, bm) in enumerate(((w_a, b_a), (w_i, b_i))):
            for c in range(KC):
                hb, jc = divmod(c, 2)
                idx = (d * 2 + typ) * 16 + c
                blk = wm[d, hb][:, jc * 128:(jc + 1) * 128]
                wg[:, idx, :] = blk.reshape(2, 128, 128).transpose(1, 0, 2).reshape(128, 256)
                gb[:, idx] = bm[d, c * 128:(c + 1) * 128]
    lamf = np.concatenate([fm(lam[0]), fm(lam[1])], 1)
    return dict(wall=wall, cw=cw, cb=cb, wg=wg.reshape(128, 64 * 256), gb=gb, lam=lamf)


def rg_carries(S_all):
    out = []
    for i in range(NCORES):
        CA = np.ones((128, 2, 2, KC, 16), np.float32)
        CB = np.zeros((128, 2, 2, KC, 16), np.float32)
        chains = {
            (0, 1): [(1, j) for j in range(i)],
            (0, 0): [(1, j) for j in range(NCORES)] + [(0, j) for j in range(i)],
            (1, 1): [(1, j) for j in range(NCORES - 1, i, -1)],
            (1, 0): [(1, j) for j in range(NCORES - 1, -1, -1)] + [(0, j) for j in range(NCORES - 1, i, -1)],
        }
        for (d, seg), ch in chains.items():
            pad = 16 - len(ch)
            for n, (sseg, j) in enumerate(ch):
                CA[:, d, seg, :, pad + n] = S_all[j][:, d, sseg, 0, :]
                CB[:, d, seg, :, pad + n] = S_all[j][:, d, sseg, 1, :]
        out.append((CA.reshape(128, 64, 16), CB.reshape(128, 64, 16)))
    return out


W_B = 1056
SEGS_B = [("lat", 0, 1024), ("ctx", 1024, 1056)]


def build_rgb():
    nc = new_nc()
    Wb = W_B
    y0d = dram_in(nc, "Y0", [128, KC, Wb], BF16)
    pfd = dram_in(nc, "PF", [128, KC, Wb], BF16)
    prd = dram_in(nc, "PR", [128, KC, Wb], BF16)
    cad = dram_in(nc, "CA", [128, 64, 16])
    cbd2 = dram_in(nc, "CB", [128, 64, 16])
    wout = dram_in(nc, "wout", [16, 128, KC * 128])
    xT = dram_in(nc, "xT", [128, KC, Wb])
    modl = dram_in(nc, "modl", [128, 3, KC])
    modc = dram_in(nc, "modc", [128, 3, KC])
    yT = dram_out(nc, "yT", [128, KC, Wb])
    es = contextlib.ExitStack()
    with es:
        tk = Tk(nc, es)
        Y0 = tk.sbuf("Y0s", [128, KC, Wb], BF16)
        PF = tk.sbuf("PFs", [128, KC, Wb], BF16)
        PR = tk.sbuf("PRs", [128, KC, Wb], BF16)
        Y = tk.sbuf("Y", [128, KC, Wb], BF16)
        TT = [tk.sbuf(f"TT{i}", [128, Wb], F32) for i in range(2)]
        XB = [tk.sbuf(f"XB{i}", [128, Wb], F32) for i in range(3)]
        CA = tk.sbuf("CAs", [128, 64, 16], F32)
        CB = tk.sbuf("CBs", [128, 64, 16], F32)
        CT = tk.sbuf("CT", [128, 16], F32)
        H0 = tk.sbuf("H0", [128, 64], F32)
        ml = tk.sbuf("ml", [128, 3, KC], F32)
        mc = tk.sbuf("mc", [128, 3, KC], F32)
        PA = tk.psum("PA", [128, 1536])
        PB = tk.psum("PB", [128, 1536])
        bIn = [tk.buf() for _ in range(KC)]
        bC, bH0, bY = tk.buf(), tk.buf(), tk.buf()
        bTT = [tk.buf() for _ in range(2)]
        bXB = [tk.buf() for _ in range(3)]
        bPA, bPB = tk.buf(), tk.buf()
        tk.dma("sp", CA[:], cad[:, :, :], writes=[bC], nowaw=True)
        tk.dma("sp", CB[:], cbd2[:, :, :], writes=[bC], nowaw=True)
        tk.dma("sp", ml[:], modl[:, :, :], writes=[bC], nowaw=True)
        tk.dma("sp", mc[:], modc[:, :, :], writes=[bC], nowaw=True)
        for c in range(KC):
            tk.dma("sp", Y0[:, c, :], y0d[:, c, :], writes=[bIn[c]], nowaw=True)
            tk.dma("sp", PF[:, c, :], pfd[:, c, :], writes=[bIn[c]], nowaw=True)
            tk.dma("sp", PR[:, c, :], prd[:, c, :], writes=[bIn[c]], nowaw=True)
        lin = Linear(tk, nc, "wout")
        lin.plan([wout[i, :, :] for i in range(16)])
        for idx in range(64):
            tk.op("dve", lambda idx=idx: nc.vector.tensor_tensor_scan(
                out=CT[:], data0=CA[:, idx, :], data1=CB[:, idx, :], initial=0.0, op0=ALU.mult, op1=ALU.add),
                reads=[bC], writes=[bH0])
            tk.op("dve", lambda idx=idx: nc.vector.tensor_copy(out=H0[:, idx:idx + 1], in_=CT[:, 15:16]),
                  reads=[bH0], writes=[bH0])
        for c in range(KC):
            t, bt = TT[c % 2], bTT[c % 2]
            for si, (nm, s0, s1) in enumerate(SEGS_B):
                i_f = (0 * 2 + si) * 16 + c
                i_r = (1 * 2 + si) * 16 + c
                tk.op("dve", lambda c=c, s0=s0, s1=s1, i_f=i_f, t=t: nc.vector.scalar_tensor_tensor(
                    out=t[:, s0:s1], in0=PF[:, c, s0:s1], scalar=H0[:, i_f:i_f + 1], in1=Y0[:, c, s0:s1],
                    op0=ALU.mult, op1=ALU.add), reads=[bIn[c], bH0], writes=[bt], nowaw=(si > 0))
                tk.op("dve", lambda c=c, s0=s0, s1=s1, i_r=i_r, t=t: nc.vector.scalar_tensor_tensor(
                    out=Y[:, c, s0:s1], in0=PR[:, c, s0:s1], scalar=H0[:, i_r:i_r + 1], in1=t[:, s0:s1],
                    op0=ALU.mult, op1=ALU.add), reads=[bIn[c], bH0, bt], writes=[bY], nowaw=True)
        bOut = tk.buf()
        PS = [(PA, bPA), (PB, bPB)]
        for oc in range(KC):
            PT, bPT = PS[oc % 2]
            xs = oc % 3
            tk.dma("sp", XB[xs][:], xT[:, oc, :], writes=[bXB[xs]])
            lin.run(Y, bY, Wb, PT, bPT)
            for (nm, s0, s1) in SEGS_B:
                m = ml if nm == "lat" else mc
                tk.op("dve", lambda oc=oc, s0=s0, s1=s1, m=m, PT=PT, xs=xs: nc.vector.scalar_tensor_tensor(
                    out=XB[xs][:, s0:s1], in0=PT[:, s0:s1], scalar=m[:, 2, oc:oc + 1], in1=XB[xs][:, s0:s1],
                    op0=ALU.mult, op1=ALU.add), reads=[bPT, bXB[xs], bC], writes=[bXB[xs]])
            tk.dma("sp", yT[:, oc, :], XB[xs][:], reads=[bXB[xs]], writes=[bOut], nowaw=True)
        tk.finish("sp", [bOut])
    return nc


NMT = 48


def build_mod():
    nc = new_nc()
    sT = dram_in(nc, "sT", [128, KC, 2])
    wm = dram_in(nc, "wm", [NMT, 128, KC * 128])
    bmd = dram_in(nc, "bm", [128, NMT])
    mo = dram_out(nc, "mo", [128, NMT, 2])
    es = contextlib.ExitStack()
    with es:
        tk = Tk(nc, es)
        S = tk.sbuf("Ssilu", [128, KC, 2], F32)
        BM = tk.sbuf("BM", [128, NMT], F32)
        OUT = tk.sbuf("OUT", [128, NMT, 2], F32)
        WT = [tk.sbuf(f"WT{i}", [128, KC * 128], F32) for i in range(4)]
        ALL = tk.psum("ALL", [128, 4096])
        bS, bBM, bOUT = tk.buf(), tk.buf(), tk.buf()
        bWT = [tk.buf() for _ in range(4)]
        bB = [tk.buf() for _ in range(8)]
        tk.dma("sp", S[:], sT[:, :, :], writes=[bS])
        tk.dma("sp", BM[:], bmd[:, :], writes=[bBM])
        tk.op("act", lambda: nc.scalar.activation(out=S[:], in_=S[:], func=AF.Silu), reads=[bS], writes=[bS])
        for t in range(NMT):
            s = t % 4
            tk.dma("sp", WT[s][:], wm[t, :, :], writes=[bWT[s]])
            b = t % 8
            for c in range(KC):
                tk.op("pe", lambda c=c, s=s, b=b: nc.tensor.matmul(
                    ALL[:, b * 512: b * 512 + 2], lhsT=WT[s][:, c * 128:(c + 1) * 128], rhs=S[:, c, :],
                    start=(c == 0), stop=(c == KC - 1)), reads=[bWT[s], bS], writes=[bB[b]], nowaw=(c > 0))
            tk.op("act", lambda t=t, b=b: nc.scalar.activation(
                out=OUT[:, t, :], in_=ALL[:, b * 512: b * 512 + 2], func=AF.Identity, bias=BM[:, t:t + 1]),
                reads=[bB[b], bBM], writes=[bOUT], nowaw=True)
        bo = tk.buf()
        tk.dma("sp", mo[:, :, :], OUT[:], reads=[bOUT], writes=[bo])
        tk.finish("sp", [bo])
    return nc


def mod_inputs(c, c_ctx, w_mod, b_mod):
    sT = np.ascontiguousarray(np.stack([fm(c.reshape(-1)), fm(c_ctx.reshape(-1))], 2))
    ins = []
    for core in range(NCORES):
        tiles, bias = [], []
        for l in range(4):
            cols = slice(core * 1536, (core + 1) * 1536)
            tiles.append(tile_w(w_mod[l][:, cols]))
            bias.append(b_mod[l][cols].reshape(12, 128).T)
        ins.append(dict(sT=sT, wm=np.concatenate(tiles, 0), bm=np.ascontiguousarray(np.concatenate(bias, 1))))
    return ins


def mod_outputs(results):
    full = np.zeros((4, 2, 12288), np.float32)
    for core, r in enumerate(results):
        mo = r["mo"]
        for l in range(4):
            blk = mo[:, l * 12:(l + 1) * 12, :]
            for s in range(2):
                full[l, s, core * 1536:(core + 1) * 1536] = blk[:, :, s].T.reshape(-1)
    return full.reshape(4, 2, 6, 2048)


def _modtile(m3):
    return np.ascontiguousarray(np.stack([fm(m3[0]), fm(m3[1]), fm(m3[2])], 1))


_PROGS = {}


def _prog(key, fn):
    if key not in _PROGS:
        _PROGS[key] = fn()
    return _PROGS[key]


def kernel(x, c, ctx, c_ctx, w_mod, b_mod, g_mix, g_ffn, fa_w_in, fa_w_out, attn_sink,
           rg_w_in, rg_conv_w, rg_conv_b, rg_w_a, rg_b_a, rg_w_i, rg_b_i, rg_lambda, rg_w_out,
           ffn_w_up, ffn_conv_w, ffn_conv_b, ffn_w_down, g_final):
    f32 = lambda a: np.asarray(a, dtype=np.float32)
    x_lat = f32(x)[0]
    x_ctx = f32(ctx)[0]
    r = run(_prog("mod", build_mod), mod_inputs(f32(c), f32(c_ctx), f32(w_mod), f32(b_mod))).results
    mod = mod_outputs(r)
    seqtab = {}
    out = None
    for layer in range(4):
        i = layer // 2
        mlm, mcm = _modtile(mod[layer, 0, 0:3]), _modtile(mod[layer, 1, 0:3])
        mlf, mcf = _modtile(mod[layer, 0, 3:6]), _modtile(mod[layer, 1, 3:6])
        gm = fm(f32(g_mix)[layer])
        if layer % 2 == 0:
            w_in, w_out = f32(fa_w_in)[i], f32(fa_w_out)[i]
            common1 = dict(modl=mlm, modc=mcm, g=gm, win=tile_w(w_in), Rm=rope_rm(), CS=chan_dft_table())
            in1 = []
            for k in range(NCORES):
                c_, s_ = rope_tables(k)
                in1.append(dict(common1, xT=fa1_xin(x_lat, x_ctx, k), cosT=c_, sinT=s_))
            r1 = run(_prog("fa1", build_fa1), in1).results
            Zl = np.concatenate([q["Z"][:1024] for q in r1], 0)
            Zc = np.concatenate([q["Z"][1024:] for q in r1], 0)
            kT_all = np.concatenate([q["kT"][:, :1024] for q in r1], 1)
            kcT = np.concatenate([q["kT"][:, 1024:] for q in r1], 1)
            vT_all = np.concatenate([q["vT"][:, :1024] for q in r1], 1)
            vcT = np.concatenate([q["vT"][:, 1024:] for q in r1], 1)
            wo_t = tile_w(w_out)
            sk = np.ascontiguousarray(np.tile(f32(attn_sink)[i][None, :], (128, 1)))
            in2 = []
            for k in range(NCORES):
                if k not in seqtab:
                    seqtab[k] = (seq_tables(k), attn_mask(k))
                (Tl, Tc), msk = seqtab[k]
                Kd, Vw = fa2_kv(kT_all, kcT, vT_all, vcT, k)
                in2.append(dict(Zl=Zl, Zc=Zc, Tl=Tl, Tc=Tc, qT=r1[k]["qT"], Kd=Kd, Vw=Vw, mask=msk, sink=sk,
                                wout=wo_t, xT=fa1_xin(x_lat, x_ctx, k), modl=mlm, modc=mcm))
            r2 = run(_prog("fa2", build_fa2), in2).results
            ys = [from_fm(q["yT"]) for q in r2]
            x_lat = np.concatenate([y[:1024] for y in ys], 0)
            x_ctx = np.concatenate([y[1024:] for y in ys], 0)
        else:
            Wt = rg_weights(f32(rg_w_in)[i], f32(rg_conv_w)[i], f32(rg_conv_b)[i], f32(rg_w_a)[i], f32(rg_b_a)[i],
                            f32(rg_w_i)[i], f32(rg_b_i)[i], f32(rg_lambda)[i], f32(rg_w_out)[i])
            common = dict(modl=mlm, modc=mcm, g=gm, **Wt)
            xin = [rg_xin(x_lat, x_ctx, k) for k in range(NCORES)]
            wA = np.ascontiguousarray(Wt["wall"][:32])
            wB = np.ascontiguousarray(Wt["wall"][32:48])
            in1 = [dict(common, wall=wA, xT=xin[k][0], edge=xin[k][1]) for k in range(NCORES)]
            r1 = run(_prog("rgA", lambda: build_rg(1)), in1).results
            car = rg_carries([q["S"] for q in r1])
            in2 = [dict(Y0=r1[k]["Y0"], PF=r1[k]["PF"], PR=r1[k]["PR"], CA=car[k][0], CB=car[k][1], wout=wB,
                        xT=fa1_xin(x_lat, x_ctx, k), modl=mlm, modc=mcm) for k in range(NCORES)]
            r2 = run(_prog("rgB", build_rgb), in2).results
            ys = [from_fm(q["yT"]) for q in r2]
            x_lat = np.concatenate([y[:1024] for y in ys], 0)
            x_ctx = np.concatenate([y[1024:] for y in ys], 0)
        last = (layer == 3)
        nctx = 0 if last else 32
        Wf = ffn_weights(f32(ffn_w_up)[layer], f32(ffn_conv_w)[layer], f32(ffn_conv_b)[layer], f32(ffn_w_down)[layer])
        per = ffn_inputs(x_lat, x_ctx, nctx)
        commonf = dict(modl=mlf, modc=mcf, gf=fm(f32(g_ffn)[layer]), **Wf)
        if last:
            commonf["gfin"] = fm(f32(g_final))
        inf = [dict(commonf, **per[k]) for k in range(NCORES)]
        rf = run(_prog(("ffn", nctx, last), lambda: build_ffn(nctx, final=last)), inf).results
        x_lat, xc_new = ffn_outputs(rf, nctx)
        if not last:
            x_ctx = xc_new
    return np.ascontiguousarray(x_lat.reshape(1, 8192, 2048).astype(np.float32))
```

```python
import contextlib
import numpy as np
import ml_dtypes
import concourse.bass as bass
import concourse.mybir as mybir
from concourse.bass_utils import run_bass_kernel_spmd

F32 = mybir.dt.float32
BF16 = mybir.dt.bfloat16
AF = mybir.ActivationFunctionType
ALU = mybir.AluOpType
EPS = 1e-6
NCORES = 8


class Buf:
    __slots__ = ("w", "r", "pr", "name")

    def __init__(self, name=""):
        self.w = {}
        self.r = {}
        self.pr = {}
        self.name = name


def _merge(dst, src):
    for k, (s, v) in src.items():
        if k not in dst or dst[k][1] < v:
            dst[k] = (s, v)


class Tk:
    NDS = 12

    def __init__(self, nc, es):
        self.nc = nc
        self.es = es
        self.eng = {"pe": nc.tensor, "act": nc.scalar, "dve": nc.vector, "pool": nc.gpsimd, "sp": nc.sync}
        self.sem = {}
        self.cnt = {}
        for e in ("pe", "act", "dve", "pool"):
            self.sem[e] = es.enter_context(nc.semaphore("s_" + e))
            self.cnt[e] = 0
        self.seen = {e: {} for e in self.eng}
        self.dsem = {}
        self.dk = {}
        for q in ("sp", "pool", "act"):
            self.dsem[q] = [es.enter_context(nc.semaphore(f"d_{q}{i}")) for i in range(self.NDS)]
            self.dk[q] = 0
        self.nbuf = 0

    def buf(self, name=""):
        self.nbuf += 1
        return Buf(name or f"b{self.nbuf}")

    def sbuf(self, name, shape, dt):
        return self.es.enter_context(self.nc.sbuf_tensor(name, shape, dt))

    def psum(self, name, shape, dt=F32):
        return self.es.enter_context(self.nc.psum_tensor(name, shape, dt))

    def _wait(self, e, need):
        seen = self.seen[e]
        for k, (s, v) in need.items():
            if seen.get(k, 0) >= v:
                continue
            self.eng[e].wait_ge(s, v)
            seen[k] = v

    def _needs(self, e, reads, writes, nowaw):
        need = {}
        for b in reads:
            _merge(need, b.w)
        for b in writes:
            _merge(need, b.r)
            _merge(need, b.pr)
            if not nowaw:
                _merge(need, b.w)
        if e == "pe":
            need.pop("pe", None)
        return need

    def _mark(self, key, ev, reads, writes):
        for b in reads:
            if key not in b.r or b.r[key][1] < ev[1]:
                b.r[key] = ev
        for b in writes:
            if b.r:
                b.pr = dict(b.r)
                b.r = {}
                b.w = {}
            b.w[key] = ev

    def op(self, e, fn, reads=(), writes=(), nowaw=False):
        self._wait(e, self._needs(e, reads, writes, nowaw))
        ins = fn()
        self.cnt[e] += 1
        ins.then_inc(self.sem[e], 1)
        ev = (self.sem[e], self.cnt[e])
        self._mark(e, ev, reads, writes)
        return ins

    def dma(self, q, out, in_, reads=(), writes=(), nowaw=False):
        need = self._needs(q, reads, writes, nowaw)
        k = self.dk[q]
        slot = k % self.NDS
        gen = k // self.NDS
        s = self.dsem[q][slot]
        key = f"d_{q}{slot}"
        if gen > 0:
            _merge(need, {key: (s, 16 * gen)})
        self._wait(q, need)
        ins = self.eng[q].dma_start(out=out, in_=in_)
        ins.then_inc(s, 16)
        self.dk[q] = k + 1
        ev = (s, 16 * (gen + 1))
        self._mark(key, ev, reads, writes)
        return ins

    def finish(self, q, bufs):
        need = {}
        for b in bufs:
            _merge(need, b.w)
        self._wait(q, need)


def new_nc():
    return bass.Bass("TRN2", target_bir_lowering=False)


def dram_in(nc, name, shape, dt=F32):
    return nc.dram_tensor(name, list(shape), dt, kind="ExternalInput").ap()


def dram_out(nc, name, shape, dt=F32):
    return nc.dram_tensor(name, list(shape), dt, kind="ExternalOutput").ap()


def run(nc, in_maps, trace=False):
    res = run_bass_kernel_spmd(nc, in_maps, core_ids=list(range(len(in_maps))), trace=trace)
    return res


D = 2048
KC = 16


def col_banks(W):
    return [(s, min(s + 512, W)) for s in range(0, W, 512)]


def fm(v):
    return np.ascontiguousarray(np.asarray(v).reshape(-1, 128).T)


def to_fm(x):
    T, F = x.shape
    return np.ascontiguousarray(x.T.reshape(F // 128, 128, T).transpose(1, 0, 2))


def from_fm(y):
    return np.ascontiguousarray(y.transpose(2, 1, 0).reshape(y.shape[2], -1))


def tile_w(w, nsplit=None):
    K, N = w.shape
    t = w.reshape(K // 128, 128, N // 128, 128)
    return np.ascontiguousarray(t.transpose(2, 1, 0, 3)).reshape(N // 128, 128, (K // 128) * 128)


class Common:
    pass


def emit_consts(tk, nc):
    ones = tk.sbuf("ones", [128, 128], F32)
    b = tk.buf("ones")
    tk.op("dve", lambda: nc.vector.memset(ones[:], 1.0), writes=[b])
    return ones, b


def emit_norm_mod(tk, nc, X, bX, H, bH, segs, W, mods, gvec, bC, ones, bOnes, PT, bPT, tmp, btmp, RS, bRS,
                  AB, bAB, hmask=None, sqb=None):
    banks = col_banks(W)
    names = [s[0] for s in segs]
    for i, nm in enumerate(names):
        m = mods[nm]
        tk.op("dve", lambda i=i, m=m: nc.vector.tensor_scalar(
            out=AB[:, i, :], in0=m[:, 1, :], scalar1=1.0, scalar2=None, op0=ALU.add),
            reads=[bC], writes=[bAB], nowaw=True)
        tk.op("dve", lambda i=i: nc.vector.tensor_tensor(
            out=AB[:, i, :], in0=AB[:, i, :], in1=gvec[:], op=ALU.mult),
            reads=[bC, bAB], writes=[bAB])
    sq_t, sq_b, sq_ones = (tmp, btmp, ones) if sqb is None else sqb
    for c in range(KC):
        s = c % 2
        tk.op("act", lambda c=c, s=s: nc.scalar.activation(out=sq_t[s][:, 0:W], in_=X[:, c, :], func=AF.Square),
              reads=[bX[c]], writes=[sq_b[s]])
        for (b0, b1) in banks:
            tk.op("pe", lambda c=c, s=s, b0=b0, b1=b1: nc.tensor.matmul(
                PT[:, b0:b1], lhsT=sq_ones[:], rhs=sq_t[s][:, b0:b1], start=(c == 0), stop=(c == KC - 1)),
                reads=[sq_b[s], bOnes], writes=[bPT], nowaw=(c > 0))
    tk.op("dve", lambda: nc.vector.tensor_scalar(out=RS[:, 0:W], in0=PT[:, 0:W], scalar1=1.0 / D, scalar2=EPS,
                                                  op0=ALU.mult, op1=ALU.add), reads=[bPT], writes=[bRS])
    tk.op("act", lambda: nc.scalar.activation(out=RS[:, 0:W], in_=RS[:, 0:W], func=AF.Sqrt), reads=[bRS], writes=[bRS])
    tk.op("dve", lambda: nc.vector.reciprocal(out=RS[:, 0:W], in_=RS[:, 0:W]), reads=[bRS], writes=[bRS])
    for c in range(KC):
        s = c % 2
        tk.op("dve", lambda c=c, s=s: nc.vector.tensor_tensor(out=tmp[s][:, 0:W], in0=X[:, c, :], in1=RS[:, 0:W],
                                                              op=ALU.mult),
              reads=[bX[c], bRS], writes=[btmp[s]])
        for i, (nm, s0, s1) in enumerate(segs):
            m = mods[nm]
            tk.op("act", lambda c=c, s=s, s0=s0, s1=s1, i=i, m=m: nc.scalar.activation(
                out=H[:, c, s0:s1], in_=tmp[s][:, s0:s1], func=AF.Identity,
                bias=m[:, 0, c:c + 1], scale=AB[:, i, c:c + 1]),
                reads=[btmp[s], bAB, bC], writes=[bH], nowaw=True)
    if hmask:
        edg, cols = hmask
        for k, col in enumerate(cols):
            tk.op("dve", lambda k=k, col=col: nc.vector.tensor_scalar(
                out=H[:, :, col:col + 1], in0=H[:, :, col:col + 1], scalar1=edg[:, k:k + 1], scalar2=None,
                op0=ALU.mult), reads=[bH, bC], writes=[bH])


class Linear:
    def __init__(self, tk, nc, name, nslots=3, kc=KC):
        self.tk, self.nc = tk, nc
        self.kc = kc
        self.WT = [tk.sbuf(f"{name}_w{i}", [128, kc * 128], BF16) for i in range(nslots)]
        self.bW = [tk.buf() for _ in range(nslots)]
        self.n = 0
        self.loaded = 0
        self.queue = []

    def plan(self, tiles):
        self.queue = list(tiles)
        for _ in range(len(self.WT) - 1):
            self._load()

    def _load(self):
        if self.loaded < len(self.queue):
            i = self.loaded
            s = i % len(self.WT)
            self.tk.dma("pool", self.WT[s][:], self.queue[i], writes=[self.bW[s]])
            self.loaded += 1

    def run(self, H, bH, W, PT, bPT, hreads=()):
        tk, nc = self.tk, self.nc
        bPTs = list(bPT) if isinstance(bPT, (list, tuple)) else [bPT]
        self._load()
        i = self.n
        s = i % len(self.WT)
        self.n += 1
        wt = self.WT[s]
        for c in range(self.kc):
            for (b0, b1) in col_banks(W):
                tk.op("pe", lambda c=c, b0=b0, b1=b1: nc.tensor.matmul(
                    PT[:, b0:b1], lhsT=wt[:, c * 128:(c + 1) * 128], rhs=H[:, c, b0:b1],
                    start=(c == 0), stop=(c == self.kc - 1)),
                    reads=[bH, self.bW[s]] + list(hreads), writes=bPTs, nowaw=(c > 0))


D = 2048
KC = 16
DFF = 5632
NP = 44
GP = 4
NG = NP // GP


def ffn_layout(NCTX):
    segs = [("lat", 0, 1026)]
    W = 1026
    if NCTX:
        segs.append(("ctx", 1026, 1026 + NCTX + 2))
        W += NCTX + 2
    return segs, W


def build_ffn(NCTX, final=False):
    segs, W = ffn_layout(NCTX)
    banks = [(s, min(s + 512, W)) for s in range(0, W, 512)]
    NOUT = 1024 + NCTX
    nc = new_nc()
    xT = dram_in(nc, "xT", [128, KC, W])
    edge = dram_in(nc, "edge", [128, 4])
    modl = dram_in(nc, "modl", [128, 3, KC])
    modc = dram_in(nc, "modc", [128, 3, KC])
    gfd = dram_in(nc, "gf", [128, KC])
    wup = dram_in(nc, "wup", [NP, 128, KC * 256])
    wdn = dram_in(nc, "wdn", [NG, 128, GP * D])
    cwd = dram_in(nc, "cw", [128, 2 * NP, 3])
    cbd = dram_in(nc, "cb", [128, 2 * NP])
    yT = dram_out(nc, "yT", [128, KC, NOUT])
    if final:
        gfind = dram_in(nc, "gfin", [128, KC])

    es = contextlib.ExitStack()
    with es:
        tk = Tk(nc, es)
        X = tk.sbuf("X", [128, KC, W], F32)
        H = tk.sbuf("H", [128, KC, W], BF16)
        Abuf = [tk.sbuf(f"A{i}", [128, GP, W], BF16) for i in range(2)]
        WU = [tk.sbuf(f"WU{i}", [128, KC * 256], BF16) for i in range(3)]
        WD = [tk.sbuf(f"WD{i}", [128, GP * D], BF16) for i in range(2)]
        UG = [tk.sbuf(f"UG{i}", [128, W], F32) for i in range(2)]
        UV = [tk.sbuf(f"UV{i}", [128, W], F32) for i in range(2)]
        TG = tk.sbuf("TG", [128, W], F32)
        TV = tk.sbuf("TV", [128, W], F32)
        SQ = [TG, TV]
        RS = UG[0]
        ones = tk.sbuf("ones", [128, 128], F32)
        onesb = tk.sbuf("onesb", [128, 128], BF16)
        SQB = [tk.sbuf(f"SQB{i}", [128, W], BF16) for i in range(2)]
        edg = tk.sbuf("edg", [128, 4], F32)
        ml = tk.sbuf("ml", [128, 3, KC], F32)
        mc = tk.sbuf("mc", [128, 3, KC], F32)
        gf = tk.sbuf("gfs", [128, KC], F32)
        AB = tk.sbuf("AB", [128, 2, KC], F32)
        cw = tk.sbuf("cws", [128, 2 * NP, 3], F32)
        cb = tk.sbuf("cbs", [128, 2 * NP], F32)
        PG = tk.psum("PG", [128, 1536])
        PV = tk.psum("PV", [128, 1536])
        PD = [tk.psum(f"PD{i}", [128, 512]) for i in range(2)]

        bX = [tk.buf(f"X{c}") for c in range(KC)]
        bH = tk.buf("H")
        bA = [tk.buf() for _ in range(2)]
        bWU = [tk.buf() for _ in range(3)]
        bWD = [tk.buf() for _ in range(2)]
        bUG = [tk.buf() for _ in range(2)]
        bUV = [tk.buf() for _ in range(2)]
        bTG, bTV = tk.buf(), tk.buf()
        bRS = bUG[0]
        bSQ = [bTG, bTV]
        bPG, bPV = tk.buf(), tk.buf()
        bPD = [tk.buf() for _ in range(2)]
        bC = tk.buf("consts")

        for c in range(KC):
            tk.dma("sp", X[:, c, :], xT[:, c, :], writes=[bX[c]])
        tk.dma("sp", edg[:], edge[:, :], writes=[bC], nowaw=True)
        tk.dma("sp", ml[:], modl[:, :, :], writes=[bC], nowaw=True)
        tk.dma("sp", mc[:], modc[:, :, :], writes=[bC], nowaw=True)
        tk.dma("sp", gf[:], gfd[:, :], writes=[bC], nowaw=True)
        tk.dma("sp", cw[:], cwd[:, :, :], writes=[bC], nowaw=True)
        tk.dma("sp", cb[:], cbd[:, :], writes=[bC], nowaw=True)

        def load_wu(p):
            tk.dma("pool", WU[p % 3][:], wup[p, :, :], reads=(bX if p < 2 else []), writes=[bWU[p % 3]])

        def load_wd(g):
            tk.dma("pool", WD[g % 2][:], wdn[g, :, :], reads=(bX if g == 0 else []), writes=[bWD[g % 2]])

        load_wu(0)
        load_wu(1)
        load_wd(0)

        tk.op("dve", lambda: nc.vector.memset(ones[:], 1.0), writes=[bC], nowaw=True)
        tk.op("dve", lambda: nc.vector.memset(onesb[:], 1.0), writes=[bC], nowaw=True)
        bSQB = [tk.buf(), tk.buf()]
        for i in range(2):
            tk.op("dve", lambda i=i: nc.vector.memset(Abuf[i][:], 0.0), writes=[bA[i]])
        bAB = tk.buf()
        for i, m in enumerate((ml, mc)):
            tk.op("dve", lambda i=i, m=m: nc.vector.tensor_scalar(
                out=AB[:, i, :], in0=m[:, 1, :], scalar1=1.0, scalar2=None, op0=ALU.add),
                reads=[bC], writes=[bAB], nowaw=True)
            tk.op("dve", lambda i=i: nc.vector.tensor_tensor(
                out=AB[:, i, :], in0=AB[:, i, :], in1=gf[:], op=ALU.mult),
                reads=[bC, bAB], writes=[bAB])

        for c in range(KC):
            s = c % 2
            tk.op("act", lambda c=c, s=s: nc.scalar.activation(out=SQB[s][:], in_=X[:, c, :], func=AF.Square),
                  reads=[bX[c]], writes=[bSQB[s]])
            for (b0, b1) in banks:
                tk.op("pe", lambda c=c, s=s, b0=b0, b1=b1: nc.tensor.matmul(
                    PG[:, b0:b1], lhsT=onesb[:], rhs=SQB[s][:, b0:b1], start=(c == 0), stop=(c == KC - 1)),
                    reads=[bSQB[s], bC], writes=[bPG], nowaw=(c > 0))
        tk.op("dve", lambda: nc.vector.tensor_scalar(out=RS[:], in0=PG[:, 0:W], scalar1=1.0 / D, scalar2=EPS,
                                                      op0=ALU.mult, op1=ALU.add), reads=[bPG], writes=[bRS])
        tk.op("act", lambda: nc.scalar.activation(out=RS[:], in_=RS[:], func=AF.Sqrt), reads=[bRS], writes=[bRS])
        tk.op("dve", lambda: nc.vector.reciprocal(out=RS[:], in_=RS[:]), reads=[bRS], writes=[bRS])
        for c in range(KC):
            s = c % 2
            tk.op("dve", lambda c=c, s=s: nc.vector.tensor_tensor(out=SQ[s][:], in0=X[:, c, :], in1=RS[:], op=ALU.mult),
                  reads=[bX[c], bRS], writes=[bSQ[s]])
            for (nm, s0, s1) in segs:
                i = 0 if nm == "lat" else 1
                m = ml if nm == "lat" else mc
                tk.op("act", lambda c=c, s=s, s0=s0, s1=s1, i=i, m=m: nc.scalar.activation(
                    out=H[:, c, s0:s1], in_=SQ[s][:, s0:s1], func=AF.Identity,
                    bias=m[:, 0, c:c + 1], scale=AB[:, i, c:c + 1]),
                    reads=[bSQ[s], bAB, bC], writes=[bH], nowaw=True)
        hcols = [0, 1025] + ([1026, W - 1] if NCTX else [])
        for k, col in enumerate(hcols):
            tk.op("dve", lambda k=k, col=col: nc.vector.tensor_scalar(
                out=H[:, :, col:col + 1], in0=H[:, :, col:col + 1], scalar1=edg[:, k:k + 1], scalar2=None,
                op0=ALU.mult), reads=[bH, bC], writes=[bH])

        def up_pair(p):
            g, j = divmod(p, GP)
            wu = WU[p % 3]
            s = p % 2
            for half, (PT, bPT, U, bU) in enumerate(((PG, bPG, UG[s], bUG[s]), (PV, bPV, UV[s], bUV[s]))):
                for c in range(KC):
                    for (b0, b1) in banks:
                        tk.op("pe", lambda c=c, b0=b0, b1=b1, PT=PT, half=half: nc.tensor.matmul(
                            PT[:, b0:b1], lhsT=wu[:, c * 256 + half * 128: c * 256 + half * 128 + 128],
                            rhs=H[:, c, b0:b1], start=(c == 0), stop=(c == KC - 1)),
                            reads=[bH, bWU[p % 3]], writes=[bPT], nowaw=(c > 0))
                tk.op("act", lambda PT=PT, U=U: nc.scalar.copy(out=U[:], in_=PT[:, 0:W]), reads=[bPT], writes=[bU])
            deferred = []
            for half, (U, bU, T, bT) in enumerate(((UG[s], bUG[s], TG, bTG), (UV[s], bUV[s], TV, bTV))):
                ch = p if half == 0 else NP + p
                deferred.append(lambda U=U, T=T, ch=ch, bU=bU, bT=bT: tk.op("dve", lambda: nc.vector.tensor_scalar(
                    out=T[:, 1:W - 1], in0=U[:, 1:W - 1], scalar1=cw[:, ch, 1:2], scalar2=cb[:, ch:ch + 1],
                    op0=ALU.mult, op1=ALU.add), reads=[bU, bC], writes=[bT]))
                deferred.append(lambda U=U, T=T, ch=ch, bU=bU, bT=bT: tk.op("dve", lambda: nc.vector.scalar_tensor_tensor(
                    out=T[:, 1:W - 1], in0=U[:, 0:W - 2], scalar=cw[:, ch, 0:1], in1=T[:, 1:W - 1],
                    op0=ALU.mult, op1=ALU.add), reads=[bU, bT, bC], writes=[bT]))
                deferred.append(lambda U=U, T=T, ch=ch, bU=bU, bT=bT: tk.op("dve", lambda: nc.vector.scalar_tensor_tensor(
                    out=T[:, 1:W - 1], in0=U[:, 2:W], scalar=cw[:, ch, 2:3], in1=T[:, 1:W - 1],
                    op0=ALU.mult, op1=ALU.add), reads=[bU, bT, bC], writes=[bT]))
            deferred.append(lambda: tk.op("act", lambda: nc.scalar.activation(
                out=TG[:, 1:W - 1], in_=TG[:, 1:W - 1], func=AF.Silu), reads=[bTG], writes=[bTG]))
            deferred.append(lambda g=g, j=j: tk.op("dve", lambda: nc.vector.tensor_tensor(
                out=Abuf[g % 2][:, j, 1:W - 1], in0=TG[:, 1:W - 1], in1=TV[:, 1:W - 1], op=ALU.mult),
                reads=[bTG, bTV], writes=[bA[g % 2]], nowaw=(j > 0)))
            return deferred

        dstate = {"n": 0}

        dbanks = [(1, 513), (513, 1025)]

        def down_part(g, part, deferred):
            items = [(bi, d) for bi in range(len(dbanks)) for d in range(KC)]
            per = (len(items) + GP - 1) // GP
            for (bi, d) in items[part * per:(part + 1) * per]:
                b0, b1 = dbanks[bi]
                k = dstate["n"] % 2
                dstate["n"] += 1
                for f in range(GP):
                    tk.op("pe", lambda f=f, d=d, b0=b0, b1=b1, k=k: nc.tensor.matmul(
                        PD[k][:, 0:b1 - b0], lhsT=WD[g % 2][:, f * D + d * 128: f * D + d * 128 + 128],
                        rhs=Abuf[g % 2][:, f, b0:b1], start=(f == 0), stop=(f == GP - 1)),
                        reads=[bA[g % 2], bWD[g % 2]], writes=[bPD[k]], nowaw=(f > 0))
                tk.op("dve", lambda d=d, b0=b0, b1=b1, k=k: nc.vector.scalar_tensor_tensor(
                    out=X[:, d, b0:b1], in0=PD[k][:, 0:b1 - b0], scalar=ml[:, 2, d:d + 1],
                    in1=X[:, d, b0:b1], op0=ALU.mult, op1=ALU.add),
                    reads=[bPD[k], bC, bX[d]], writes=[bX[d]])
                if deferred:
                    deferred.pop(0)()
            if NCTX and part == GP - 1:
                c0, c1 = 1027, 1027 + NCTX
                k = dstate["n"] % 2
                dstate["n"] += 1
                for d in range(KC):
                    for f in range(GP):
                        tk.op("pe", lambda f=f, d=d, k=k: nc.tensor.matmul(
                            PD[k][:, d * NCTX:(d + 1) * NCTX], lhsT=WD[g % 2][:, f * D + d * 128: f * D + d * 128 + 128],
                            rhs=Abuf[g % 2][:, f, c0:c1], start=(f == 0), stop=(f == GP - 1)),
                            reads=[bA[g % 2], bWD[g % 2]], writes=[bPD[k]], nowaw=not (d == 0 and f == 0))
                for d in range(KC):
                    tk.op("dve", lambda d=d, k=k: nc.vector.scalar_tensor_tensor(
                        out=X[:, d, c0:c1], in0=PD[k][:, d * NCTX:(d + 1) * NCTX], scalar=mc[:, 2, d:d + 1],
                        in1=X[:, d, c0:c1], op0=ALU.mult, op1=ALU.add),
                        reads=[bPD[k], bC, bX[d]], writes=[bX[d]])
                    if deferred:
                        deferred.pop(0)()

        prev_def = []
        for g in range(NG + 1):
            if 1 <= g < NG:
                load_wd(g)
            for j in range(GP):
                p = g * GP + j
                while prev_def:
                    prev_def.pop(0)()
                if g < NG:
                    if p + 2 < NP:
                        load_wu(p + 2)
                    prev_def = up_pair(p)
                if g >= 1:
                    down_part(g - 1, j, [])

        bOut = tk.buf()
        if final:
            gfin = tk.sbuf("gfins", [128, KC], F32)
            bGF = tk.buf()
            tk.dma("sp", gfin[:], gfind[:, :], writes=[bGF])
            for c in range(KC):
                s = c % 2
                tk.op("act", lambda c=c, s=s: nc.scalar.activation(out=SQ[s][:], in_=X[:, c, :], func=AF.Square),
                      reads=[bX[c]], writes=[bSQ[s]])
                for (b0, b1) in banks:
                    tk.op("pe", lambda c=c, s=s, b0=b0, b1=b1: nc.tensor.matmul(
                        PG[:, b0:b1], lhsT=ones[:], rhs=SQ[s][:, b0:b1], start=(c == 0), stop=(c == KC - 1)),
                        reads=[bSQ[s], bC], writes=[bPG], nowaw=(c > 0))
            tk.op("dve", lambda: nc.vector.tensor_scalar(out=RS[:], in0=PG[:, 0:W], scalar1=1.0 / D, scalar2=EPS,
                                                          op0=ALU.mult, op1=ALU.add), reads=[bPG], writes=[bRS])
            tk.op("act", lambda: nc.scalar.activation(out=RS[:], in_=RS[:], func=AF.Sqrt), reads=[bRS], writes=[bRS])
            tk.op("dve", lambda: nc.vector.reciprocal(out=RS[:], in_=RS[:]), reads=[bRS], writes=[bRS])
            for c in range(KC):
                s = c % 2
                tk.op("dve", lambda c=c, s=s: nc.vector.tensor_tensor(out=SQ[s][:], in0=X[:, c, :], in1=RS[:], op=ALU.mult),
                      reads=[bX[c], bRS], writes=[bSQ[s]])
                tk.op("act", lambda c=c, s=s: nc.scalar.activation(out=SQ[s][:], in_=SQ[s][:], func=AF.Identity,
                                                                   scale=gfin[:, c:c + 1]),
                      reads=[bSQ[s], bGF], writes=[bSQ[s]])
                tk.dma("sp", yT[:, c, 0:1024], SQ[s][:, 1:1025], reads=[bSQ[s]], writes=[bOut], nowaw=True)
        for c in range(KC if not final else 0):
            tk.dma("sp", yT[:, c, 0:1024], X[:, c, 1:1025], reads=[bX[c]], writes=[bOut], nowaw=True)
            if NCTX:
                tk.dma("sp", yT[:, c, 1024:1024 + NCTX], X[:, c, 1027:1027 + NCTX], reads=[bX[c]], writes=[bOut],
                       nowaw=True)
        tk.finish("sp", [bOut])
    return nc


def fm(v):
    return np.ascontiguousarray(v.reshape(KC, 128).T)


def ffn_weights(w_up, conv_w, conv_b, w_down):
    wu = w_up.reshape(KC, 128, 2, NP, 128)
    wu = np.ascontiguousarray(wu.transpose(3, 1, 0, 2, 4)).reshape(NP, 128, KC * 256)
    wd = w_down.reshape(NG, GP, 128, D)
    wd = np.ascontiguousarray(wd.transpose(0, 2, 1, 3)).reshape(NG, 128, GP * D)
    cw = np.ascontiguousarray(conv_w.reshape(3, 2 * NP, 128).transpose(2, 1, 0))
    cb = np.ascontiguousarray(conv_b.reshape(2 * NP, 128).T)
    return dict(wup=wu, wdn=wd, cw=cw, cb=cb)


def ffn_inputs(x_lat, x_ctx, NCTX):
    segs, W = ffn_layout(NCTX)
    outs = []
    for i in range(NCORES):
        cols = np.zeros((W, D), np.float32)
        lo, hi = i * 1024, (i + 1) * 1024
        cols[1:1025] = x_lat[lo:hi]
        e = np.zeros((128, 4), np.float32)
        if i > 0:
            cols[0] = x_lat[lo - 1]; e[:, 0] = 1
        if i < NCORES - 1:
            cols[1025] = x_lat[hi]; e[:, 1] = 1
        if NCTX:
            clo, chi = i * NCTX, (i + 1) * NCTX
            cols[1027:1027 + NCTX] = x_ctx[clo:chi]
            if i > 0:
                cols[1026] = x_ctx[clo - 1]; e[:, 2] = 1
            if i < NCORES - 1:
                cols[W - 1] = x_ctx[chi]; e[:, 3] = 1
        xT = np.ascontiguousarray(cols.T.reshape(KC, 128, W).transpose(1, 0, 2))
        outs.append(dict(xT=xT, edge=e))
    return outs


def ffn_outputs(results, NCTX):
    lat, ctx = [], []
    for r in results:
        y = r["yT"]
        t = y.transpose(2, 1, 0).reshape(y.shape[2], D)
        lat.append(t[:1024])
        if NCTX:
            ctx.append(t[1024:])
    return np.concatenate(lat, 0), (np.concatenate(ctx, 0) if NCTX else None)


W1 = 1056
NLAT = 1024
NCTX_FA = 32
SEGS1 = [("lat", 0, 1024), ("ctx", 1024, 1056)]
NOC = 18


def build_fa1(stage=9):
    W = W1
    nc = new_nc()
    xT = dram_in(nc, "xT", [128, KC, W])
    modl = dram_in(nc, "modl", [128, 3, KC])
    modc = dram_in(nc, "modc", [128, 3, KC])
    gd = dram_in(nc, "g", [128, KC])
    win = dram_in(nc, "win", [NOC, 128, KC * 128])
    cosd = dram_in(nc, "cosT", [128, NLAT])
    sind = dram_in(nc, "sinT", [128, NLAT])
    rmd = dram_in(nc, "Rm", [128, 128])
    csd = dram_in(nc, "CS", [128, 2 * 512])
    Zo = dram_out(nc, "Z", [W, 2048], BF16)
    qo = dram_out(nc, "qT", [128, 8, W], BF16)
    ko = dram_out(nc, "kT", [128, W], BF16)
    vo = dram_out(nc, "vT", [128, W], BF16)

    es = contextlib.ExitStack()
    with es:
        tk = Tk(nc, es)
        X = tk.sbuf("X", [128, KC, W], F32)
        H = tk.sbuf("H", [128, KC, W], BF16)
        FT = tk.sbuf("FT", [128, 8, W], BF16)
        tmp = [tk.sbuf(f"tmp{i}", [128, W], F32) for i in range(2)]
        RS = tk.sbuf("RS", [128, W], F32)
        ml = tk.sbuf("ml", [128, 3, KC], F32)
        mc = tk.sbuf("mc", [128, 3, KC], F32)
        g = tk.sbuf("gs", [128, KC], F32)
        AB = tk.sbuf("AB", [128, 2, KC], F32)
        COS = tk.sbuf("COS", [128, NLAT], F32)
        SIN = tk.sbuf("SIN", [128, NLAT], F32)
        RM = tk.sbuf("RM", [128, 128], BF16)
        CS = tk.sbuf("CSs", [128, 2 * 512], BF16)
        QR = [tk.sbuf(f"QR{i}", [128, NLAT], BF16) for i in range(2)]
        QO = [tk.sbuf(f"QO{i}", [128, W], BF16) for i in range(2)]
        ZT = [tk.sbuf(f"ZT{i}", [128, 2048], BF16) for i in range(2)]
        PA = tk.psum("PA", [128, 1536])
        PB = tk.psum("PB", [128, 1536])
        PR = tk.psum("PR", [128, 1024])

        bX = [tk.buf() for _ in range(KC)]
        bH, bFT, bRS, bAB, bC = tk.buf(), tk.buf(), tk.buf(), tk.buf(), tk.buf()
        btmp = [tk.buf() for _ in range(2)]
        bQR = [tk.buf() for _ in range(2)]
        bQO = [tk.buf() for _ in range(2)]
        bZT = [tk.buf() for _ in range(2)]
        bPA, bPB = tk.buf(), tk.buf()
        bPR = [tk.buf(), tk.buf()]

        for c in range(KC):
            tk.dma("sp", X[:, c, :], xT[:, c, :], writes=[bX[c]])
        for (dst, src) in ((ml[:], modl[:, :, :]), (mc[:], modc[:, :, :]), (g[:], gd[:, :]),
                           (COS[:], cosd[:, :]), (SIN[:], sind[:, :])):
            tk.dma("sp", dst, src, writes=[bC], nowaw=True)
        tk.dma("pool", RM[:], rmd[:, :], writes=[bC], nowaw=True)
        tk.dma("pool", CS[:], csd[:, :], writes=[bC], nowaw=True)
        lin = Linear(tk, nc, "win")
        lin.plan([win[i, :, :] for i in range(NOC)])
        ones, bOnes = emit_consts(tk, nc)
        onesb = tk.sbuf("onesb", [128, 128], BF16)
        SQB = [tk.sbuf(f"SQB{i}", [128, W], BF16) for i in range(2)]
        bSQB = [tk.buf(), tk.buf()]
        tk.op("dve", lambda: nc.vector.memset(onesb[:], 1.0), writes=[bOnes], nowaw=True)

        emit_norm_mod(tk, nc, X, bX, H, bH, SEGS1, W, {"lat": ml, "ctx": mc}, g, bC, ones, bOnes, PA, bPA,
                      tmp, btmp, RS, bRS, AB, bAB, sqb=(SQB, bSQB, onesb))

        outbuf = tk.buf()
        PTs = [(PA, bPA), (PB, bPB)]
        for oc in range(NOC):
            if stage <= 2 and oc >= 8:
                break
            if stage in (3, 31, 32, 33) and oc >= 9:
                break
            if stage == 4 and oc >= 17:
                break
            PT, bPT = PTs[oc % 2]
            lin.run(H, bH, W, PT, bPT)
            if oc < 8:
                tk.op("act", lambda oc=oc, PT=PT: nc.scalar.copy(out=FT[:, oc, :], in_=PT[:, 0:W]),
                      reads=[bPT], writes=[bFT], nowaw=True)
                if oc == 7 and stage >= 2:
                    tiles = [(t * 128, 128, 0) for t in range(8)] + [(W - 128, 128, 128 - NCTX_FA)]
                    for ti, (t0, nt, r0) in enumerate(tiles):
                        zt, bzt = ZT[ti % 2], bZT[ti % 2]
                        for gi in range(4):
                            hb = gi % 2
                            for cc in range(2):
                                tk.op("pe", lambda gi=gi, cc=cc, t0=t0, nt=nt, hb=hb: nc.tensor.matmul(
                                    PR[0:nt, hb * 512:(hb + 1) * 512], lhsT=FT[:, 2 * gi + cc, t0:t0 + nt],
                                    rhs=CS[:, cc * 512:(cc + 1) * 512], start=(cc == 0), stop=(cc == 1)),
                                    reads=[bFT, bC], writes=[bPR[hb]], nowaw=(cc > 0))
                            tk.op("act", lambda gi=gi, nt=nt, hb=hb, zt=zt: nc.scalar.copy(
                                out=zt[0:nt, gi * 512:(gi + 1) * 512], in_=PR[0:nt, hb * 512:(hb + 1) * 512]),
                                reads=[bPR[hb]], writes=[bzt], nowaw=(gi > 0))
                        tk.dma("sp", Zo[t0 + r0:t0 + nt, :], zt[r0:nt, :], reads=[bzt], writes=[outbuf], nowaw=True)
            elif oc < 17:
                s = oc % 2
                qr, bqr, qo_, bqo = QR[s], bQR[s], QO[s], bQO[s]
                tk.op("act", lambda qr=qr, PT=PT: nc.scalar.copy(out=qr[:], in_=PT[:, 0:NLAT]),
                      reads=[bPT], writes=[bqr])
                for hb in range(2):
                    tk.op("pe", lambda hb=hb, qr=qr: nc.tensor.matmul(
                        PR[:, hb * 512:(hb + 1) * 512], lhsT=RM[:], rhs=qr[:, hb * 512:(hb + 1) * 512],
                        start=True, stop=True), reads=[bqr, bC], writes=[bPR[hb]])
                if stage == 31:
                    continue
                tk.op("act", lambda PT=PT: nc.scalar.copy(out=tmp[0][:, 0:NLAT], in_=PT[:, 0:NLAT]),
                      reads=[bPT], writes=[btmp[0]])
                tk.op("act", lambda: nc.scalar.copy(out=tmp[1][:, 0:NLAT], in_=PR[:, 0:NLAT]),
                      reads=[bPR[0], bPR[1]], writes=[btmp[1]])
                tk.op("dve", lambda: nc.vector.tensor_tensor(
                    out=tmp[0][:, 0:NLAT], in0=tmp[0][:, 0:NLAT], in1=COS[:], op=ALU.mult),
                    reads=[btmp[0], bC], writes=[btmp[0]])
                tk.op("dve", lambda: nc.vector.tensor_tensor(
                    out=tmp[1][:, 0:NLAT], in0=tmp[1][:, 0:NLAT], in1=SIN[:], op=ALU.mult),
                    reads=[btmp[1], bC], writes=[btmp[1]])
                tk.op("dve", lambda qo_=qo_: nc.vector.tensor_tensor(
                    out=qo_[:, 0:NLAT], in0=tmp[0][:, 0:NLAT], in1=tmp[1][:, 0:NLAT], op=ALU.add),
                    reads=[btmp[0], btmp[1]], writes=[bqo])
                if stage == 32:
                    continue
                tk.op("act", lambda qo_=qo_, PT=PT: nc.scalar.copy(out=qo_[:, NLAT:W], in_=PT[:, NLAT:W]),
                      reads=[bPT], writes=[bqo], nowaw=True)
                if stage == 33:
                    continue
                dst = qo[:, oc - 8, :] if oc < 16 else ko[:, :]
                tk.dma("sp", dst, qo_[:], reads=[bqo], writes=[outbuf], nowaw=True)
            else:
                s = oc % 2
                qo_, bqo = QO[s], bQO[s]
                tk.op("act", lambda qo_=qo_, PT=PT: nc.scalar.copy(out=qo_[:], in_=PT[:, 0:W]),
                      reads=[bPT], writes=[bqo])
                tk.dma("sp", vo[:, :], qo_[:], reads=[bqo], writes=[outbuf], nowaw=True)
        tk.finish("sp", [outbuf])
    return nc


def rope_tables(core):
    t = np.arange(core * 1024, (core + 1) * 1024)
    row, col = t // 64, t % 64
    p = np.arange(128)
    dd = p % 64
    half = dd // 32
    within = dd % 32
    j = within % 16
    part = within // 16
    inv = (10000.0 ** (-np.arange(16, dtype=np.float32) / 16)).astype(np.float32)
    pos = np.where(half[:, None] == 0, row[None, :], col[None, :]).astype(np.float32)
    ang = pos * inv[j][:, None]
    return np.cos(ang).astype(np.float32), np.sin(ang).astype(np.float32)


def rope_rm():
    Rm = np.zeros((128, 128), np.float32)
    for pp in range(128):
        within = (pp % 64) % 32
        if within < 16:
            Rm[pp + 16, pp] = -1.0
        else:
            Rm[pp - 16, pp] = 1.0
    return Rm


def chan_dft_table():
    c = np.arange(256)
    ang = 2 * np.pi * np.outer(c, c) / 256.0
    C = (np.cos(ang) / 16.0).astype(np.float32)
    S = (np.sin(ang) / 16.0).astype(np.float32)
    CS = np.zeros((128, 2, 512), np.float32)
    for cc in range(2):
        CS[:, cc, 0:256] = C[cc * 128:(cc + 1) * 128]
        CS[:, cc, 256:512] = S[cc * 128:(cc + 1) * 128]
    return CS.reshape(128, 1024)


def fa1_xin(x_lat, x_ctx, core):
    cols = np.concatenate([x_lat[core * 1024:(core + 1) * 1024], x_ctx[core * NCTX_FA:(core + 1) * NCTX_FA]], 0)
    return to_fm(cols)


W_FA2 = 1056
NLAT = 1024
NCTX_FA = 32
NSEQ = 8192
NKB = 10
NVT = 12
KW = NKB * 128
SEGS_FA2 = [("lat", 0, 1024), ("ctx", 1024, 1056)]


def build_fa2():
    nc = new_nc()
    Zl = dram_in(nc, "Zl", [NSEQ, 2048], BF16)
    Zc = dram_in(nc, "Zc", [256, 2048], BF16)
    Tl = dram_in(nc, "Tl", [64, 128, 2, 1024], BF16)
    Tc = dram_in(nc, "Tc", [2, 128, 2, NCTX_FA], BF16)
    qd = dram_in(nc, "qT", [128, 8, W_FA2], BF16)
    kd = dram_in(nc, "Kd", [2, 128, KW + 256], BF16)
    vd = dram_in(nc, "Vw", [128, NVT, 128], BF16)
    md = dram_in(nc, "mask", [128, NKB, 384], BF16)
    sd = dram_in(nc, "sink", [128, 16])
    wout = dram_in(nc, "wout", [16, 128, KC * 128])
    xT = dram_in(nc, "xT", [128, KC, W_FA2])
    modl = dram_in(nc, "modl", [128, 3, KC])
    modc = dram_in(nc, "modc", [128, 3, KC])
    yT = dram_out(nc, "yT", [128, KC, W_FA2])

    es = contextlib.ExitStack()
    with es:
        tk = Tk(nc, es)
        CAT = tk.sbuf("CAT", [128, KC, W_FA2], BF16)
        Q = tk.sbuf("Q", [128, 8, W_FA2], BF16)
        ZB = [tk.sbuf(f"ZB{i}", [128, 2048], BF16) for i in range(4)]
        TB = [tk.sbuf(f"TB{i}", [128, 2, 512], BF16) for i in range(4)]
        KD = tk.sbuf("KD", [128, 2, KW + 256], BF16)
        VW = tk.sbuf("VW", [128, NVT, 128], BF16)
        VA = tk.sbuf("VA", [128, 2, 2, NVT, 128], BF16)
        MK = tk.sbuf("MK", [128, NKB, 384], BF16)
        SK = tk.sbuf("SK", [128, 16], F32)
        EB = [tk.sbuf(f"EB{i}", [128, 512], BF16) for i in range(4)]
        RD = tk.sbuf("RD", [128, W_FA2], F32)
        NUM = tk.sbuf("NUM", [128, W_FA2], F32)
        XB = [tk.sbuf(f"XB{i}", [128, W_FA2], F32) for i in range(3)]
        ml = tk.sbuf("ml", [128, 3, KC], F32)
        mc = tk.sbuf("mc", [128, 3, KC], F32)
        ALL = tk.psum("ALL", [128, 4096])

        def bank(b, n=1):
            return ALL[:, b * 512:(b + n) * 512]

        bBank = [tk.buf(f"bank{i}") for i in range(8)]
        bCAT, bQ, bKD, bVW, bVA, bMK, bSK, bC = (tk.buf() for _ in range(8))
        bZB = [tk.buf() for _ in range(4)]
        bTB = [tk.buf() for _ in range(4)]
        bEB = [tk.buf() for _ in range(4)]
        bRD, bNUM = tk.buf(), tk.buf()
        bXB = [tk.buf() for _ in range(3)]

        tk.dma("sp", Q[:], qd[:, :, :], writes=[bQ])
        for kv in range(2):
            tk.dma("sp", KD[:, kv, :], kd[kv, :, :], writes=[bKD], nowaw=True)
        tk.dma("sp", VW[:], vd[:, :, :], writes=[bVW])
        tk.dma("sp", MK[:], md[:, :, :], writes=[bMK])
        tk.dma("sp", SK[:], sd[:, :], writes=[bSK])
        tk.dma("sp", ml[:], modl[:, :, :], writes=[bC], nowaw=True)
        tk.dma("sp", mc[:], modc[:, :, :], writes=[bC], nowaw=True)
        lin = Linear(tk, nc, "wout")
        lin.plan([wout[i, :, :] for i in range(16)])

        for hb in range(2):
            for n in range(64):
                s = n % 4
                tk.dma("sp", ZB[s][:], Zl[n * 128:(n + 1) * 128, :], writes=[bZB[s]])
                tk.dma("sp", TB[s][:], Tl[n, :, :, hb * 512:(hb + 1) * 512], writes=[bTB[s]])
                for fc in range(8):
                    g, ch = divmod(fc, 2)
                    for pq in range(2):
                        tk.op("pe", lambda fc=fc, g=g, ch=ch, pq=pq, s=s, n=n: nc.tensor.matmul(
                            bank(fc), lhsT=ZB[s][:, g * 512 + pq * 256 + ch * 128: g * 512 + pq * 256 + ch * 128 + 128],
                            rhs=TB[s][:, pq, :], start=(n == 0 and pq == 0), stop=(n == 63 and pq == 1)),
                            reads=[bZB[s], bTB[s]], writes=[bBank[fc]], nowaw=not (n == 0 and pq == 0))
            for fc in range(8):
                tk.op("act", lambda fc=fc, hb=hb: nc.scalar.copy(out=CAT[:, fc, hb * 512:(hb + 1) * 512], in_=bank(fc)),
                      reads=[bBank[fc]], writes=[bCAT], nowaw=True)
        for n in range(2):
            s = n % 4
            tk.dma("sp", ZB[s][:], Zc[n * 128:(n + 1) * 128, :], writes=[bZB[s]])
            tk.dma("sp", TB[s][:, :, 0:NCTX_FA], Tc[n, :, :, :], writes=[bTB[s]])
            for fc in range(8):
                g, ch = divmod(fc, 2)
                for pq in range(2):
                    tk.op("pe", lambda fc=fc, g=g, ch=ch, pq=pq, s=s, n=n: nc.tensor.matmul(
                        ALL[:, fc * 512: fc * 512 + NCTX_FA],
                        lhsT=ZB[s][:, g * 512 + pq * 256 + ch * 128: g * 512 + pq * 256 + ch * 128 + 128],
                        rhs=TB[s][:, pq, 0:NCTX_FA], start=(n == 0 and pq == 0), stop=(n == 1 and pq == 1)),
                        reads=[bZB[s], bTB[s]], writes=[bBank[fc]], nowaw=not (n == 0 and pq == 0))
        for fc in range(8):
            tk.op("act", lambda fc=fc: nc.scalar.copy(out=CAT[:, fc, NLAT:W_FA2], in_=ALL[:, fc * 512: fc * 512 + NCTX_FA]),
                  reads=[bBank[fc]], writes=[bCAT], nowaw=True)

        tk.op("act", lambda: nc.scalar.activation(out=SK[:], in_=SK[:], func=AF.Exp), reads=[bSK], writes=[bSK])
        tk.op("dve", lambda: nc.vector.memset(VA[:], 1.0), writes=[bVA])
        for kv in range(2):
            tk.op("dve", lambda kv=kv: nc.vector.tensor_copy(out=VA[:, 0, kv, :, 0:64], in_=VW[:, :, kv * 64:(kv + 1) * 64]),
                  reads=[bVW], writes=[bVA])
            tk.op("dve", lambda kv=kv: nc.vector.tensor_copy(out=VA[:, 1, kv, :, 64:128], in_=VW[:, :, kv * 64:(kv + 1) * 64]),
                  reads=[bVW], writes=[bVA])

        ectr = {"n": 0}
        allwork = []
        cb_banks = col_banks(W_FA2)
        for h in range(16):
            ch, var = divmod(h, 2)
            base = var * 64
            dbase = 64 - base
            kv = h // 8
            pob = 2 + 3 * (h % 2)
            bPO = bBank[pob:pob + 3]
            work = []
            for cbk in range(2):
                for (c0, c1) in cb_banks:
                    work.append((NKB + cbk, KW + cbk * 128, c0, c1, None))
            for j in range(NKB):
                q0, q1 = max(0, j - 2) * 128, min(8, j + 1) * 128
                pieces = [(q0, q1)] if (q0 // 512 == (q1 - 1) // 512) else [(q0, 512), (512, q1)]
                for (c0, c1) in pieces:
                    work.append((j, j * 128, c0, c1, (j, c0 - (j - 2) * 128, c1 - (j - 2) * 128)))
            last = {}
            first = {}
            for wi, wk in enumerate(work):
                b = wk[2] // 512
                last[b] = wi
                first.setdefault(b, wi)
            hctx = dict(h=h, ch=ch, var=var, base=base, dbase=dbase, kv=kv, pob=pob, bPO=bPO, first=first, last=last,
                        nwork=len(work))
            for wi, wk in enumerate(work):
                allwork.append((hctx, wi, wk))

        def emit_front(hc, wi, wk):
            (vt, kc0, c0, c1, msk) = wk
            base, kv, ch = hc["base"], hc["kv"], hc["ch"]
            sb = ectr["n"] % 2
            e = ectr["n"] % 4
            ectr["n"] += 1
            n = c1 - c0
            tk.op("pe", lambda: nc.tensor.matmul(
                ALL[:, sb * 512: sb * 512 + n], lhsT=KD[base:base + 64, kv, kc0:kc0 + 128],
                rhs=Q[base:base + 64, ch, c0:c1], start=True, stop=True),
                reads=[bKD, bQ], writes=[bBank[sb]])
            tk.op("act", lambda: nc.scalar.activation(
                out=EB[e][:, 0:n], in_=ALL[:, sb * 512: sb * 512 + n], func=AF.Exp, scale=0.125),
                reads=[bBank[sb]], writes=[bEB[e]])
            if msk is not None:
                mj, m0, m1 = msk
                tk.op("dve", lambda: nc.vector.tensor_tensor(
                    out=EB[e][:, 0:n], in0=EB[e][:, 0:n], in1=MK[:, mj, m0:m1], op=ALU.mult),
                    reads=[bEB[e], bMK], writes=[bEB[e]])
            return e

        def emit_back(hc, wi, wk, e):
            (vt, kc0, c0, c1, msk) = wk
            base, dbase, kv, ch, var, pob, bPO, h = (hc[k] for k in ("base", "dbase", "kv", "ch", "var", "pob", "bPO", "h"))
            first, last = hc["first"], hc["last"]
            n = c1 - c0
            b = c0 // 512
            tk.op("pe", lambda: nc.tensor.matmul(
                ALL[:, pob * 512 + c0: pob * 512 + c1], lhsT=VA[:, var, kv, vt, :], rhs=EB[e][:, 0:n],
                start=(first[b] == wi), stop=(last[b] == wi)),
                reads=[bVA, bEB[e]], writes=[bPO[b]], nowaw=(first[b] != wi))
            if wi != hc["nwork"] - 1:
                return
            PO = ALL[:, pob * 512: pob * 512 + W_FA2]
            tk.op("dve", lambda: nc.vector.tensor_scalar(
                out=RD[base:base + 64, :], in0=PO[dbase:dbase + 64, :], scalar1=SK[dbase:dbase + 64, h:h + 1],
                scalar2=None, op0=ALU.add), reads=bPO + [bSK], writes=[bRD])
            tk.op("dve", lambda: nc.vector.reciprocal(out=RD[base:base + 64, :], in_=RD[base:base + 64, :]),
                  reads=[bRD], writes=[bRD])
            tk.op("act", lambda: nc.scalar.copy(out=NUM[base:base + 64, :], in_=PO[base:base + 64, :]),
                  reads=bPO, writes=[bNUM])
            tk.op("dve", lambda: nc.vector.tensor_tensor(
                out=CAT[base:base + 64, 8 + ch, :], in0=NUM[base:base + 64, :], in1=RD[base:base + 64, :],
                op=ALU.mult), reads=[bNUM, bRD], writes=[bCAT], nowaw=True)

        DEPTH = 2
        pend = []
        for (hc, wi, wk) in allwork:
            e = emit_front(hc, wi, wk)
            pend.append((hc, wi, wk, e))
            if len(pend) > DEPTH:
                emit_back(*pend.pop(0))
        while pend:
            emit_back(*pend.pop(0))

        bOut = tk.buf()
        for oc in range(16):
            pb = 3 * (oc % 2)
            PT = ALL[:, pb * 512: pb * 512 + 1536]
            xs = oc % 3
            tk.dma("sp", XB[xs][:], xT[:, oc, :], writes=[bXB[xs]])
            lin.run(CAT, bCAT, W_FA2, PT, bBank[pb:pb + 3])
            for (nm, s0, s1) in SEGS_FA2:
                m = ml if nm == "lat" else mc
                tk.op("dve", lambda oc=oc, s0=s0, s1=s1, m=m, PT=PT, xs=xs: nc.vector.scalar_tensor_tensor(
                    out=XB[xs][:, s0:s1], in0=PT[:, s0:s1], scalar=m[:, 2, oc:oc + 1], in1=XB[xs][:, s0:s1],
                    op0=ALU.mult, op1=ALU.add), reads=bBank[pb:pb + 3] + [bXB[xs], bC], writes=[bXB[xs]])
            tk.dma("sp", yT[:, oc, :], XB[xs][:], reads=[bXB[xs]], writes=[bOut], nowaw=True)
        tk.finish("sp", [bOut])
    return nc


def seq_tables(core):
    n = np.arange(NSEQ, dtype=np.int64)
    k = np.arange(core * 1024, (core + 1) * 1024, dtype=np.int64)
    ph = (np.outer(n, k) % NSEQ).astype(np.float64) * (2 * np.pi / NSEQ)
    sc = 1.0 / np.sqrt(NSEQ)
    T = np.stack([np.cos(ph) * sc, -np.sin(ph) * sc], 1)
    Tl = T.reshape(64, 128, 2, 1024).astype(ml_dtypes.bfloat16)
    n = np.arange(256, dtype=np.int64)
    k = np.arange(core * NCTX_FA, (core + 1) * NCTX_FA, dtype=np.int64)
    ph = (np.outer(n, k) % 256).astype(np.float64) * (2 * np.pi / 256)
    T = np.stack([np.cos(ph) / 16.0, -np.sin(ph) / 16.0], 1)
    Tc = T.reshape(2, 128, 2, NCTX_FA).astype(ml_dtypes.bfloat16)
    return Tl, Tc


def attn_mask(core):
    lo = core * 1024
    kk = np.arange(128)
    m = np.zeros((128, NKB, 384), np.float32)
    for j in range(NKB):
        kpos = lo - 128 + 128 * j + kk
        qpos = lo + (j - 2) * 128 + np.arange(384)
        ok = (np.abs(kpos[:, None] - qpos[None, :]) <= 128) & (kpos[:, None] >= 0) & (kpos[:, None] < NSEQ)
        m[:, j, :] = ok
    return m.astype(ml_dtypes.bfloat16)


def fa2_kv(kT_all, kcT, vT_all, vcT, core):
    lo, hi = core * 1024, (core + 1) * 1024
    kw = np.zeros((128, KW), kT_all.dtype)
    vw = np.zeros((128, KW), vT_all.dtype)
    a, b = max(lo - 128, 0), min(hi + 128, NSEQ)
    kw[:, a - (lo - 128): b - (lo - 128)] = kT_all[:, a:b]
    vw[:, a - (lo - 128): b - (lo - 128)] = vT_all[:, a:b]
    kfull = np.concatenate([kw, kcT], 1)
    Kd = np.stack([np.concatenate([kfull[kv * 64:(kv + 1) * 64]] * 2, 0) for kv in range(2)], 0)
    vfull = np.concatenate([vw, vcT], 1).T
    Vw = np.ascontiguousarray(vfull.reshape(NVT, 128, 128).transpose(1, 0, 2))
    return np.ascontiguousarray(Kd), Vw


W_RG = 1062
NCTX_RG = 32
SEGS_RG = [("lat", 0, 1027), ("ctx", 1027, 1062)]
VALID = [(2, 1026), (1029, 1061)]
HALO = [0, 1, 1026, 1027, 1028, 1061]
NOUT_RG = 1024 + NCTX_RG
LRU_C = 8.0
NT = 11


def rev(t2d):
    return bass.AP(tensor=t2d.tensor, offset=t2d.offset + (t2d.ap[-1][1] - 1) * t2d.ap[-1][0],
                   ap=[list(t2d.ap[0]), [-t2d.ap[-1][0], t2d.ap[-1][1]]])


def build_rg(phase):
    nc = new_nc()
    xT = dram_in(nc, "xT", [128, KC, W_RG])
    edge = dram_in(nc, "edge", [128, 6])
    modl = dram_in(nc, "modl", [128, 3, KC])
    modc = dram_in(nc, "modc", [128, 3, KC])
    gd = dram_in(nc, "g", [128, KC])
    wall = dram_in(nc, "wall", [32 if phase == 1 else 48, 128, KC * 128])
    cwd = dram_in(nc, "cw", [128, KC, 4])
    cbd = dram_in(nc, "cb", [128, KC])
    wgd = dram_in(nc, "wg", [128, 64 * 256])
    gbd = dram_in(nc, "gb", [128, 64])
    lamd = dram_in(nc, "lam", [128, 32])
    if phase == 2:
        cad = dram_in(nc, "CA", [128, 64, 16])
        cbd2 = dram_in(nc, "CB", [128, 64, 16])
        yT = dram_out(nc, "yT", [128, KC, NOUT_RG])
    else:
        So = dram_out(nc, "S", [128, 2, 2, 2, KC])
        Y0o = dram_out(nc, "Y0", [128, KC, NOUT_RG], BF16)
        PFo = dram_out(nc, "PF", [128, KC, NOUT_RG], BF16)
        PRo = dram_out(nc, "PR", [128, KC, NOUT_RG], BF16)

    es = contextlib.ExitStack()
    with es:
        tk = Tk(nc, es)
        H = tk.sbuf("H", [128, KC, W_RG], BF16)
        XCB = tk.sbuf("XCB", [128, KC, W_RG], BF16)
        if phase == 2:
            Y = tk.sbuf("Y", [128, KC, W_RG], BF16)
        else:
            OB = [tk.sbuf(f"OB{i}", [128, 3, W_RG], BF16) for i in range(2)]
            bOB = [tk.buf() for _ in range(2)]
            PC = [tk.sbuf(f"PC{i}", [128, W_RG], F32) for i in range(2)]
            bPC = [tk.buf() for _ in range(2)]
        WG = tk.sbuf("WG", [128, 64 * 256], BF16)
        T = [tk.sbuf(f"T{i}", [128, W_RG], F32) for i in range(NT)]
        XT = [tk.sbuf(f"XT{i}", [128, W_RG], F32) for i in range(4)]
        XB = [tk.sbuf(f"XB{i}", [128, W_RG], F32) for i in range(2)]
        edg = tk.sbuf("edg", [128, 6], F32)
        ml = tk.sbuf("ml", [128, 3, KC], F32)
        mc = tk.sbuf("mc", [128, 3, KC], F32)
        g = tk.sbuf("gs", [128, KC], F32)
        AB = tk.sbuf("AB", [128, 2, KC], F32)
        cw = tk.sbuf("cws", [128, KC, 4], F32)
        cb = tk.sbuf("cbs", [128, KC], F32)
        nb = tk.sbuf("nb", [128, 64], F32)
        c8 = tk.sbuf("c8", [128, 32], F32)
        c16 = tk.sbuf("c16", [128, 32], F32)
        S = tk.sbuf("Ssum", [128, 2, 2, 2, KC], F32)
        H0 = tk.sbuf("H0", [128, 64], F32)
        if phase == 2:
            CA = tk.sbuf("CAs", [128, 64, 16], F32)
            CB = tk.sbuf("CBs", [128, 64, 16], F32)
            CT = tk.sbuf("CT", [128, 16], F32)
        PA = tk.psum("PA", [128, 1536])
        PB = tk.psum("PB", [128, 1536])

        bH, bXCB, bY, bWG, bC, bAB, bS, bH0 = (tk.buf() for _ in range(8))
        bT = [tk.buf() for _ in range(NT)]
        bXT = [tk.buf() for _ in range(4)]
        bXB = [tk.buf() for _ in range(2)]
        bPA, bPB = tk.buf(), tk.buf()
        PS = [(PA, bPA), (PB, bPB)]
        pctr = {"n": 0}

        def next_ps():
            p = PS[pctr["n"] % 2]
            pctr["n"] += 1
            return p

        for (dst, src) in ((edg[:], edge[:, :]), (ml[:], modl[:, :, :]), (mc[:], modc[:, :, :]), (g[:], gd[:, :]),
                           (cw[:], cwd[:, :, :]), (cb[:], cbd[:, :]), (nb[:], gbd[:, :]), (c8[:], lamd[:, :])):
            tk.dma("sp", dst, src, writes=[bC], nowaw=True)
        if phase == 2:
            tk.dma("sp", CA[:], cad[:, :, :], writes=[bC], nowaw=True)
            tk.dma("sp", CB[:], cbd2[:, :, :], writes=[bC], nowaw=True)
        tk.dma("pool", WG[:], wgd[:, :], writes=[bWG])
        lin = Linear(tk, nc, "wall", nslots=2)
        order = [0, 1]
        for c in range(KC):
            if c + 2 < KC:
                order.append(c + 2)
            order.append(16 + c)
        if phase == 2:
            order += list(range(32, 48))
        lin.plan([wall[i, :, :] for i in order])
        ones, bOnes = emit_consts(tk, nc)
        onesb = tk.sbuf("onesb", [128, 128], BF16)
        SQB = [tk.sbuf(f"SQB{i}", [128, W_RG], BF16) for i in range(2)]
        bSQB = [tk.buf(), tk.buf()]
        tk.op("dve", lambda: nc.vector.memset(onesb[:], 1.0), writes=[bOnes], nowaw=True)
        bXCBc_init = []
        tk.op("dve", lambda: nc.vector.memset(XCB[:], 0.0), writes=[bXCB])
        tk.op("dve", lambda: nc.vector.memset(T[9][:], 0.0), writes=[bT[9]])
        ZERO = T[9]
        tk.op("act", lambda: nc.scalar.activation(out=c8[:], in_=c8[:], func=AF.Exp, scale=-1.0), reads=[bC], writes=[bC])
        tk.op("act", lambda: nc.scalar.activation(out=c8[:], in_=c8[:], func=AF.Ln, bias=1.0), reads=[bC], writes=[bC])
        tk.op("dve", lambda: nc.vector.tensor_scalar(out=c16[:], in0=c8[:], scalar1=-2.0 * LRU_C, scalar2=None, op0=ALU.mult),
              reads=[bC], writes=[bC])
        tk.op("dve", lambda: nc.vector.tensor_scalar(out=c8[:], in0=c8[:], scalar1=-LRU_C, scalar2=None, op0=ALU.mult),
              reads=[bC], writes=[bC])

        if phase == 2:
            for idx in range(64):
                tk.op("dve", lambda idx=idx: nc.vector.tensor_tensor_scan(
                    out=CT[:], data0=CA[:, idx, :], data1=CB[:, idx, :], initial=0.0, op0=ALU.mult, op1=ALU.add),
                    reads=[bC], writes=[bH0])
                tk.op("dve", lambda idx=idx: nc.vector.tensor_copy(out=H0[:, idx:idx + 1], in_=CT[:, 15:16]),
                      reads=[bH0], writes=[bH0])

        class XS_:
            n = 0

        def xload(c):
            s = XS_.n % 2
            XS_.n += 1
            tk.dma("sp", XB[s][:], xT[:, c, :], writes=[bXB[s]])
            return XB[s], bXB[s]

        class XView:
            cur = {}

        banks = col_banks(W_RG)
        mods = {"lat": ml, "ctx": mc}
        for i, (nm, _, _) in enumerate(SEGS_RG):
            m = mods[nm]
            tk.op("dve", lambda i=i, m=m: nc.vector.tensor_scalar(
                out=AB[:, i, :], in0=m[:, 1, :], scalar1=1.0, scalar2=None, op0=ALU.add),
                reads=[bC], writes=[bAB], nowaw=True)
            tk.op("dve", lambda i=i: nc.vector.tensor_tensor(
                out=AB[:, i, :], in0=AB[:, i, :], in1=g[:], op=ALU.mult), reads=[bC, bAB], writes=[bAB])
        for c in range(KC):
            xb, bxb = xload(c)
            s = c % 2
            tk.op("act", lambda xb=xb, s=s: nc.scalar.activation(out=SQB[s][:], in_=xb[:], func=AF.Square),
                  reads=[bxb], writes=[bSQB[s]])
            for (b0, b1) in banks:
                tk.op("pe", lambda c=c, s=s, b0=b0, b1=b1: nc.tensor.matmul(
                    PA[:, b0:b1], lhsT=onesb[:], rhs=SQB[s][:, b0:b1], start=(c == 0), stop=(c == KC - 1)),
                    reads=[bSQB[s], bOnes], writes=[bPA], nowaw=(c > 0))
        RS, bRS = T[2], bT[2]
        tk.op("dve", lambda: nc.vector.tensor_scalar(out=RS[:], in0=PA[:, 0:W_RG], scalar1=1.0 / D, scalar2=EPS,
                                                      op0=ALU.mult, op1=ALU.add), reads=[bPA], writes=[bRS])
        tk.op("act", lambda: nc.scalar.activation(out=RS[:], in_=RS[:], func=AF.Sqrt), reads=[bRS], writes=[bRS])
        tk.op("dve", lambda: nc.vector.reciprocal(out=RS[:], in_=RS[:]), reads=[bRS], writes=[bRS])
        for c in range(KC):
            xb, bxb = xload(c)
            s = c % 2
            tk.op("dve", lambda xb=xb, s=s: nc.vector.tensor_tensor(out=T[s][:], in0=xb[:], in1=RS[:], op=ALU.mult),
                  reads=[bxb, bRS], writes=[bT[s]])
            for i, (nm, s0, s1) in enumerate(SEGS_RG):
                m = mods[nm]
                tk.op("act", lambda c=c, s=s, s0=s0, s1=s1, i=i, m=m: nc.scalar.activation(
                    out=H[:, c, s0:s1], in_=T[s][:, s0:s1], func=AF.Identity,
                    bias=m[:, 0, c:c + 1], scale=AB[:, i, c:c + 1]),
                    reads=[bT[s], bAB, bC], writes=[bH], nowaw=True)
        for k, col in enumerate(HALO):
            tk.op("dve", lambda k=k, col=col: nc.vector.tensor_scalar(
                out=H[:, :, col:col + 1], in0=H[:, :, col:col + 1], scalar1=edg[:, k:k + 1], scalar2=None,
                op0=ALU.mult), reads=[bH, bC], writes=[bH])

        def xs_chunk(c):
            PT, bPT = next_ps()
            lin.run(H, bH, W_RG, PT, bPT)
            s = c % 2
            xs, bxs = XT[s], bXT[s]
            xc, bxc = XT[2 + s], bXT[2 + s]
            tk.op("act", lambda PT=PT, xs=xs: nc.scalar.copy(out=xs[:], in_=PT[:, 0:W_RG]), reads=[bPT], writes=[bxs])
            tk.op("dve", lambda c=c, xs=xs, xc=xc: nc.vector.tensor_scalar(
                out=xc[:, 2:W_RG - 1], in0=xs[:, 0:W_RG - 3], scalar1=cw[:, c, 0:1], scalar2=cb[:, c:c + 1],
                op0=ALU.mult, op1=ALU.add), reads=[bxs, bC], writes=[bxc])
            for k in range(1, 4):
                last = (k == 3)
                dst = XCB[:, c, 2:W_RG - 1] if last else xc[:, 2:W_RG - 1]
                tk.op("dve", lambda c=c, xs=xs, xc=xc, k=k, dst=dst: nc.vector.scalar_tensor_tensor(
                    out=dst, in0=xs[:, k:W_RG - 3 + k], scalar=cw[:, c, k:k + 1], in1=xc[:, 2:W_RG - 1],
                    op0=ALU.mult, op1=ALU.add), reads=[bxs, bxc, bC],
                    writes=([bXCBc[c]] if last else [bxc]), nowaw=last)

        bXCBc = [tk.buf() for _ in range(KC)]
        xs_chunk(0)
        xs_chunk(1)

        bOutA = tk.buf()
        if phase == 1:
            for d in range(2):
                tk.op("dve", lambda d=d: nc.vector.memset(PC[d][:], 0.0), writes=[bPC[d]])
        def sigmoid_from_psum(PT, bPT, dst, bdst, bias_ap):
            tk.op("act", lambda: nc.scalar.activation(out=dst[:], in_=PT[:, 0:W_RG], func=AF.Sigmoid, bias=bias_ap),
                  reads=[bPT, bC], writes=[bdst])

        for c in range(KC):
            hb = c // 2
            if c + 2 < KC:
                xs_chunk(c + 2)
            HS = []
            for d in range(2):
                tr, btr = T[3 * d + 0], bT[3 * d + 0]
                ti, bti = T[3 * d + 1], bT[3 * d + 1]
                ta, bta = T[3 * d + 2], bT[3 * d + 2]
                hs, bhs = T[6 + d], bT[6 + d]
                for typ, (dst, bdst) in enumerate(((tr, btr), (ti, bti))):
                    PT, bPT = next_ps()
                    widx = (d * 2 + typ) * 16 + c
                    for ic in range(2):
                        for (b0, b1) in banks:
                            tk.op("pe", lambda ic=ic, b0=b0, b1=b1, PT=PT, widx=widx: nc.tensor.matmul(
                                PT[:, b0:b1], lhsT=WG[:, widx * 256 + ic * 128: widx * 256 + ic * 128 + 128],
                                rhs=XCB[:, hb * 2 + ic, b0:b1], start=(ic == 0), stop=(ic == 1)),
                                reads=[bXCB, bXCBc[hb * 2 + ic], bWG], writes=[bPT], nowaw=(ic > 0))
                    sigmoid_from_psum(PT, bPT, dst, bdst, nb[:, widx:widx + 1])
                tk.op("act", lambda d=d, tr=tr, ta=ta: nc.scalar.activation(
                    out=ta[:], in_=tr[:], func=AF.Exp, scale=c8[:, d * 16 + c: d * 16 + c + 1]),
                    reads=[btr, bC], writes=[bta])
                tk.op("act", lambda d=d, tr=tr: nc.scalar.activation(
                    out=tr[:], in_=tr[:], func=AF.Exp, scale=c16[:, d * 16 + c: d * 16 + c + 1]),
                    reads=[btr, bC], writes=[btr])
                tk.op("act", lambda tr=tr: nc.scalar.activation(out=tr[:], in_=tr[:], func=AF.Ln, scale=-1.0, bias=1.0),
                      reads=[btr], writes=[btr])
                tk.op("act", lambda tr=tr: nc.scalar.activation(out=tr[:], in_=tr[:], func=AF.Exp, scale=0.5),
                      reads=[btr], writes=[btr])
                tk.op("dve", lambda ti=ti: nc.vector.tensor_tensor(out=ti[:], in0=ti[:], in1=XCB[:, c, :], op=ALU.mult),
                      reads=[bti, bXCB, bXCBc[c]], writes=[bti])
                tk.op("dve", lambda ti=ti, tr=tr: nc.vector.tensor_tensor(out=ti[:], in0=ti[:], in1=tr[:], op=ALU.mult),
                      reads=[bti, btr], writes=[bti])
                for si, (v0, v1) in enumerate(VALID):
                    idx = (d * 2 + si) * 16 + c
                    init = 0.0 if phase == 1 else H0[:, idx:idx + 1]
                    o_, a_, g_ = hs[:, v0:v1], ta[:, v0:v1], ti[:, v0:v1]
                    if d == 1:
                        o_, a_, g_ = rev(o_), rev(a_), rev(g_)
                    tk.op("dve", lambda o_=o_, a_=a_, g_=g_, init=init: nc.vector.tensor_tensor_scan(
                        out=o_, data0=a_, data1=g_, initial=init, op0=ALU.mult, op1=ALU.add),
                        reads=[bta, bti, bH0], writes=[bhs], nowaw=(si > 0))
                    if phase == 1:
                        endc = (v1 - 1) if d == 0 else v0
                        tk.op("dve", lambda hs=hs, d=d, si=si, endc=endc: nc.vector.tensor_copy(
                            out=S[:, d, si, 1, c:c + 1], in_=hs[:, endc:endc + 1]), reads=[bhs], writes=[bS], nowaw=True)
                        z_ = ZERO[:, v0:v1]
                        p_ = PC[d][:, v0:v1]
                        if d == 1:
                            z_, p_ = rev(z_), rev(p_)
                        tk.op("dve", lambda p_=p_, a_=a_, z_=z_: nc.vector.tensor_tensor_scan(
                            out=p_, data0=a_, data1=z_, initial=1.0, op0=ALU.mult, op1=ALU.add),
                            reads=[bta, bT[9]], writes=[bPC[d]], nowaw=(si > 0))
                        tk.op("dve", lambda d=d, si=si, endc=endc: nc.vector.tensor_copy(
                            out=S[:, d, si, 0, c:c + 1], in_=PC[d][:, endc:endc + 1]), reads=[bPC[d]], writes=[bS], nowaw=True)
                HS.append((hs, bhs))
            if True:
                PT, bPT = next_ps()
                lin.run(H, bH, W_RG, PT, bPT)
                gt_, bgt = T[8], bT[8]
                t1, bt1 = T[10], bT[10]
                tk.op("act", lambda PT=PT: nc.scalar.copy(out=gt_[:], in_=PT[:, 0:W_RG]), reads=[bPT], writes=[bgt])
                tk.op("dve", lambda: nc.vector.tensor_tensor(out=t1[:], in0=gt_[:], in1=gt_[:], op=ALU.mult),
                      reads=[bgt], writes=[bt1])
                tk.op("dve", lambda: nc.vector.tensor_scalar(out=t1[:], in0=t1[:], scalar1=0.044715, scalar2=1.0,
                                                              op0=ALU.mult, op1=ALU.add), reads=[bt1], writes=[bt1])
                tk.op("dve", lambda: nc.vector.tensor_tensor(out=t1[:], in0=t1[:], in1=gt_[:], op=ALU.mult),
                      reads=[bt1, bgt], writes=[bt1])
                tk.op("act", lambda: nc.scalar.activation(out=t1[:], in_=t1[:], func=AF.Sigmoid, scale=1.5957691216),
                      reads=[bt1], writes=[bt1])
                tk.op("dve", lambda: nc.vector.tensor_tensor(out=gt_[:], in0=gt_[:], in1=t1[:], op=ALU.mult),
                      reads=[bt1, bgt], writes=[bgt])
                (h0_, bh0_), (h1_, bh1_) = HS
                tk.op("dve", lambda: nc.vector.tensor_tensor(out=h0_[:], in0=h0_[:], in1=h1_[:], op=ALU.add),
                      reads=[bh0_, bh1_], writes=[bh0_])
                if phase == 2:
                    tk.op("dve", lambda c=c: nc.vector.tensor_tensor(out=Y[:, c, :], in0=h0_[:], in1=gt_[:], op=ALU.mult),
                          reads=[bh0_, bgt], writes=[bY], nowaw=True)
                else:
                    ob, bob = OB[c % 2], bOB[c % 2]
                    tk.op("pool", lambda ob=ob: nc.gpsimd.tensor_tensor(out=ob[:, 0, :], in0=h0_[:], in1=gt_[:], op=ALU.mult),
                          reads=[bh0_, bgt], writes=[bob])
                    for d in range(2):
                        tk.op("pool", lambda ob=ob, d=d: nc.gpsimd.tensor_tensor(
                            out=ob[:, 1 + d, :], in0=PC[d][:], in1=gt_[:], op=ALU.mult),
                            reads=[bPC[d], bgt], writes=[bob], nowaw=True)
                    for k, dst in enumerate((Y0o, PFo, PRo)):
                        tk.dma("sp", dst[:, c, 0:1024], ob[:, k, 2:1026], reads=[bob], writes=[bOutA], nowaw=True)
                        tk.dma("sp", dst[:, c, 1024:NOUT_RG], ob[:, k, 1029:1061], reads=[bob], writes=[bOutA], nowaw=True)

        bOut = tk.buf()
        if phase == 1:
            tk.dma("sp", So[:, :, :, :, :], S[:], reads=[bS], writes=[bOut])
            tk.finish("sp", [bOutA])
        else:
            for oc in range(KC):
                PT, bPT = next_ps()
                xb, bxb = xload(oc)
                lin.run(Y, bY, W_RG, PT, bPT)
                for (nm, s0, s1) in SEGS_RG:
                    m = ml if nm == "lat" else mc
                    tk.op("dve", lambda oc=oc, s0=s0, s1=s1, m=m, PT=PT, xb=xb: nc.vector.scalar_tensor_tensor(
                        out=xb[:, s0:s1], in0=PT[:, s0:s1], scalar=m[:, 2, oc:oc + 1], in1=xb[:, s0:s1],
                        op0=ALU.mult, op1=ALU.add), reads=[bPT, bxb, bC], writes=[bxb])
                tk.dma("sp", yT[:, oc, 0:1024], xb[:, 2:1026], reads=[bxb], writes=[bOut], nowaw=True)
                tk.dma("sp", yT[:, oc, 1024:NOUT_RG], xb[:, 1029:1061], reads=[bxb], writes=[bOut], nowaw=True)
        tk.finish("sp", [bOut])
    return nc


def rg_xin(x_lat, x_ctx, core):
    cols = np.zeros((W_RG, D), np.float32)
    e = np.zeros((128, 6), np.float32)
    lo, hi = core * 1024, (core + 1) * 1024
    cols[2:1026] = x_lat[lo:hi]
    if core > 0:
        cols[0:2] = x_lat[lo - 2:lo]; e[:, 0:2] = 1
    if core < NCORES - 1:
        cols[1026] = x_lat[hi]; e[:, 2] = 1
    clo, chi = core * NCTX_RG, (core + 1) * NCTX_RG
    cols[1029:1061] = x_ctx[clo:chi]
    if core > 0:
        cols[1027:1029] = x_ctx[clo - 2:clo]; e[:, 3:5] = 1
    if core < NCORES - 1:
        cols[1061] = x_ctx[chi]; e[:, 5] = 1
    return to_fm(cols), e


def rg_weights(w_in, conv_w, conv_b, w_a, b_a, w_i, b_i, lam, w_out):
    wall = np.concatenate([tile_w(w_in[:, D:]), tile_w(w_in[:, :D]), tile_w(w_out)], 0)
    cw = np.ascontiguousarray(conv_w.reshape(4, KC, 128).transpose(2, 1, 0))
    cb = fm(conv_b)
    wg = np.zeros((128, 64, 256), np.float32)
    gb = np.zeros((128, 64), np.float32)
    for d in range(2):
        for typ, (wm, bm) in enumerate(((w_a, b_a), (w_i, b_i))):
            for c in range(KC):
                hb, jc = divmod(c, 2)
                idx = (d * 2 + typ) * 16 + c
                blk = wm[d, hb][:, jc * 128:(jc + 1) * 128]
                wg[:, idx, :] = blk.reshape(2, 128, 128).transpose(1, 0, 2).reshape(128, 256)
                gb[:, idx] = bm[d, c * 128:(c + 1) * 128]
    lamf = np.concatenate([fm(lam[0]), fm(lam[1])], 1)
    return dict(wall=wall, cw=cw, cb=cb, wg=wg.reshape(128, 64 * 256), gb=gb, lam=lamf)


def rg_carries(S_all):
    out = []
    for i in range(NCORES):
        CA = np.ones((128, 2, 2, KC, 16), np.float32)
        CB = np.zeros((128, 2, 2, KC, 16), np.float32)
        chains = {
            (0, 1): [(1, j) for j in range(i)],
            (0, 0): [(1, j) for j in range(NCORES)] + [(0, j) for j in range(i)],
            (1, 1): [(1, j) for j in range(NCORES - 1, i, -1)],
            (1, 0): [(1, j) for j in range(NCORES - 1, -1, -1)] + [(0, j) for j in range(NCORES - 1, i, -1)],
        }
        for (d, seg), ch in chains.items():
            pad = 16 - len(ch)
            for n, (sseg, j) in enumerate(ch):
                CA[:, d, seg, :, pad + n] = S_all[j][:, d, sseg, 0, :]
                CB[:, d, seg, :, pad + n] = S_all[j][:, d, sseg, 1, :]
        out.append((CA.reshape(128, 64, 16), CB.reshape(128, 64, 16)))
    return out


W_B = 1056
SEGS_B = [("lat", 0, 1024), ("ctx", 1024, 1056)]


def build_rgb():
    nc = new_nc()
    Wb = W_B
    y0d = dram_in(nc, "Y0", [128, KC, Wb], BF16)
    pfd = dram_in(nc, "PF", [128, KC, Wb], BF16)
    prd = dram_in(nc, "PR", [128, KC, Wb], BF16)
    cad = dram_in(nc, "CA", [128, 64, 16])
    cbd2 = dram_in(nc, "CB", [128, 64, 16])
    wout = dram_in(nc, "wout", [16, 128, KC * 128])
    xT = dram_in(nc, "xT", [128, KC, Wb])
    modl = dram_in(nc, "modl", [128, 3, KC])
    modc = dram_in(nc, "modc", [128, 3, KC])
    yT = dram_out(nc, "yT", [128, KC, Wb])
    es = contextlib.ExitStack()
    with es:
        tk = Tk(nc, es)
        Y0 = tk.sbuf("Y0s", [128, KC, Wb], BF16)
        PF = tk.sbuf("PFs", [128, KC, Wb], BF16)
        PR = tk.sbuf("PRs", [128, KC, Wb], BF16)
        Y = tk.sbuf("Y", [128, KC, Wb], BF16)
        TT = [tk.sbuf(f"TT{i}", [128, Wb], F32) for i in range(2)]
        XB = [tk.sbuf(f"XB{i}", [128, Wb], F32) for i in range(3)]
        CA = tk.sbuf("CAs", [128, 64, 16], F32)
        CB = tk.sbuf("CBs", [128, 64, 16], F32)
        CT = tk.sbuf("CT", [128, 16], F32)
        H0 = tk.sbuf("H0", [128, 64], F32)
        ml = tk.sbuf("ml", [128, 3, KC], F32)
        mc = tk.sbuf("mc", [128, 3, KC], F32)
        PA = tk.psum("PA", [128, 1536])
        PB = tk.psum("PB", [128, 1536])
        bIn = [tk.buf() for _ in range(KC)]
        bC, bH0, bY = tk.buf(), tk.buf(), tk.buf()
        bTT = [tk.buf() for _ in range(2)]
        bXB = [tk.buf() for _ in range(3)]
        bPA, bPB = tk.buf(), tk.buf()
        tk.dma("sp", CA[:], cad[:, :, :], writes=[bC], nowaw=True)
        tk.dma("sp", CB[:], cbd2[:, :, :], writes=[bC], nowaw=True)
        tk.dma("sp", ml[:], modl[:, :, :], writes=[bC], nowaw=True)
        tk.dma("sp", mc[:], modc[:, :, :], writes=[bC], nowaw=True)
        for c in range(KC):
            tk.dma("sp", Y0[:, c, :], y0d[:, c, :], writes=[bIn[c]], nowaw=True)
            tk.dma("sp", PF[:, c, :], pfd[:, c, :], writes=[bIn[c]], nowaw=True)
            tk.dma("sp", PR[:, c, :], prd[:, c, :], writes=[bIn[c]], nowaw=True)
        lin = Linear(tk, nc, "wout")
        lin.plan([wout[i, :, :] for i in range(16)])
        for idx in range(64):
            tk.op("dve", lambda idx=idx: nc.vector.tensor_tensor_scan(
                out=CT[:], data0=CA[:, idx, :], data1=CB[:, idx, :], initial=0.0, op0=ALU.mult, op1=ALU.add),
                reads=[bC], writes=[bH0])
            tk.op("dve", lambda idx=idx: nc.vector.tensor_copy(out=H0[:, idx:idx + 1], in_=CT[:, 15:16]),
                  reads=[bH0], writes=[bH0])
        for c in range(KC):
            t, bt = TT[c % 2], bTT[c % 2]
            for si, (nm, s0, s1) in enumerate(SEGS_B):
                i_f = (0 * 2 + si) * 16 + c
                i_r = (1 * 2 + si) * 16 + c
                tk.op("dve", lambda c=c, s0=s0, s1=s1, i_f=i_f, t=t: nc.vector.scalar_tensor_tensor(
                    out=t[:, s0:s1], in0=PF[:, c, s0:s1], scalar=H0[:, i_f:i_f + 1], in1=Y0[:, c, s0:s1],
                    op0=ALU.mult, op1=ALU.add), reads=[bIn[c], bH0], writes=[bt], nowaw=(si > 0))
                tk.op("dve", lambda c=c, s0=s0, s1=s1, i_r=i_r, t=t: nc.vector.scalar_tensor_tensor(
                    out=Y[:, c, s0:s1], in0=PR[:, c, s0:s1], scalar=H0[:, i_r:i_r + 1], in1=t[:, s0:s1],
                    op0=ALU.mult, op1=ALU.add), reads=[bIn[c], bH0, bt], writes=[bY], nowaw=True)
        bOut = tk.buf()
        PS = [(PA, bPA), (PB, bPB)]
        for oc in range(KC):
            PT, bPT = PS[oc % 2]
            xs = oc % 3
            tk.dma("sp", XB[xs][:], xT[:, oc, :], writes=[bXB[xs]])
            lin.run(Y, bY, Wb, PT, bPT)
            for (nm, s0, s1) in SEGS_B:
                m = ml if nm == "lat" else mc
                tk.op("dve", lambda oc=oc, s0=s0, s1=s1, m=m, PT=PT, xs=xs: nc.vector.scalar_tensor_tensor(
                    out=XB[xs][:, s0:s1], in0=PT[:, s0:s1], scalar=m[:, 2, oc:oc + 1], in1=XB[xs][:, s0:s1],
                    op0=ALU.mult, op1=ALU.add), reads=[bPT, bXB[xs], bC], writes=[bXB[xs]])
            tk.dma("sp", yT[:, oc, :], XB[xs][:], reads=[bXB[xs]], writes=[bOut], nowaw=True)
        tk.finish("sp", [bOut])
    return nc


NMT = 48


def build_mod():
    nc = new_nc()
    sT = dram_in(nc, "sT", [128, KC, 2])
    wm = dram_in(nc, "wm", [NMT, 128, KC * 128])
    bmd = dram_in(nc, "bm", [128, NMT])
    mo = dram_out(nc, "mo", [128, NMT, 2])
    es = contextlib.ExitStack()
    with es:
        tk = Tk(nc, es)
        S = tk.sbuf("Ssilu", [128, KC, 2], F32)
        BM = tk.sbuf("BM", [128, NMT], F32)
        OUT = tk.sbuf("OUT", [128, NMT, 2], F32)
        WT = [tk.sbuf(f"WT{i}", [128, KC * 128], F32) for i in range(4)]
        ALL = tk.psum("ALL", [128, 4096])
        bS, bBM, bOUT = tk.buf(), tk.buf(), tk.buf()
        bWT = [tk.buf() for _ in range(4)]
        bB = [tk.buf() for _ in range(8)]
        tk.dma("sp", S[:], sT[:, :, :], writes=[bS])
        tk.dma("sp", BM[:], bmd[:, :], writes=[bBM])
        tk.op("act", lambda: nc.scalar.activation(out=S[:], in_=S[:], func=AF.Silu), reads=[bS], writes=[bS])
        for t in range(NMT):
            s = t % 4
            tk.dma("sp", WT[s][:], wm[t, :, :], writes=[bWT[s]])
            b = t % 8
            for c in range(KC):
                tk.op("pe", lambda c=c, s=s, b=b: nc.tensor.matmul(
                    ALL[:, b * 512: b * 512 + 2], lhsT=WT[s][:, c * 128:(c + 1) * 128], rhs=S[:, c, :],
                    start=(c == 0), stop=(c == KC - 1)), reads=[bWT[s], bS], writes=[bB[b]], nowaw=(c > 0))
            tk.op("act", lambda t=t, b=b: nc.scalar.activation(
                out=OUT[:, t, :], in_=ALL[:, b * 512: b * 512 + 2], func=AF.Identity, bias=BM[:, t:t + 1]),
                reads=[bB[b], bBM], writes=[bOUT], nowaw=True)
        bo = tk.buf()
        tk.dma("sp", mo[:, :, :], OUT[:], reads=[bOUT], writes=[bo])
        tk.finish("sp", [bo])
    return nc


def mod_inputs(c, c_ctx, w_mod, b_mod):
    sT = np.ascontiguousarray(np.stack([fm(c.reshape(-1)), fm(c_ctx.reshape(-1))], 2))
    ins = []
    for core in range(NCORES):
        tiles, bias = [], []
        for l in range(4):
            cols = slice(core * 1536, (core + 1) * 1536)
            tiles.append(tile_w(w_mod[l][:, cols]))
            bias.append(b_mod[l][cols].reshape(12, 128).T)
        ins.append(dict(sT=sT, wm=np.concatenate(tiles, 0), bm=np.ascontiguousarray(np.concatenate(bias, 1))))
    return ins


def mod_outputs(results):
    full = np.zeros((4, 2, 12288), np.float32)
    for core, r in enumerate(results):
        mo = r["mo"]
        for l in range(4):
            blk = mo[:, l * 12:(l + 1) * 12, :]
            for s in range(2):
                full[l, s, core * 1536:(core + 1) * 1536] = blk[:, :, s].T.reshape(-1)
    return full.reshape(4, 2, 6, 2048)


def _modtile(m3):
    return np.ascontiguousarray(np.stack([fm(m3[0]), fm(m3[1]), fm(m3[2])], 1))


_PROGS = {}


def _prog(key, fn):
    if key not in _PROGS:
        _PROGS[key] = fn()
    return _PROGS[key]


def kernel(x, c, ctx, c_ctx, w_mod, b_mod, g_mix, g_ffn, fa_w_in, fa_w_out, attn_sink,
           rg_w_in, rg_conv_w, rg_conv_b, rg_w_a, rg_b_a, rg_w_i, rg_b_i, rg_lambda, rg_w_out,
           ffn_w_up, ffn_conv_w, ffn_conv_b, ffn_w_down, g_final):
    f32 = lambda a: np.asarray(a, dtype=np.float32)
    x_lat = f32(x)[0]
    x_ctx = f32(ctx)[0]
    r = run(_prog("mod", build_mod), mod_inputs(f32(c), f32(c_ctx), f32(w_mod), f32(b_mod))).results
    mod = mod_outputs(r)
    seqtab = {}
    out = None
    for layer in range(4):
        i = layer // 2
        mlm, mcm = _modtile(mod[layer, 0, 0:3]), _modtile(mod[layer, 1, 0:3])
        mlf, mcf = _modtile(mod[layer, 0, 3:6]), _modtile(mod[layer, 1, 3:6])
        gm = fm(f32(g_mix)[layer])
        if layer % 2 == 0:
            w_in, w_out = f32(fa_w_in)[i], f32(fa_w_out)[i]
            common1 = dict(modl=mlm, modc=mcm, g=gm, win=tile_w(w_in), Rm=rope_rm(), CS=chan_dft_table())
            in1 = []
            for k in range(NCORES):
                c_, s_ = rope_tables(k)
                in1.append(dict(common1, xT=fa1_xin(x_lat, x_ctx, k), cosT=c_, sinT=s_))
            r1 = run(_prog("fa1", build_fa1), in1).results
            Zl = np.concatenate([q["Z"][:1024] for q in r1], 0)
            Zc = np.concatenate([q["Z"][1024:] for q in r1], 0)
            kT_all = np.concatenate([q["kT"][:, :1024] for q in r1], 1)
            kcT = np.concatenate([q["kT"][:, 1024:] for q in r1], 1)
            vT_all = np.concatenate([q["vT"][:, :1024] for q in r1], 1)
            vcT = np.concatenate([q["vT"][:, 1024:] for q in r1], 1)
            wo_t = tile_w(w_out)
            sk = np.ascontiguousarray(np.tile(f32(attn_sink)[i][None, :], (128, 1)))
            in2 = []
            for k in range(NCORES):
                if k not in seqtab:
                    seqtab[k] = (seq_tables(k), attn_mask(k))
                (Tl, Tc), msk = seqtab[k]
                Kd, Vw = fa2_kv(kT_all, kcT, vT_all, vcT, k)
                in2.append(dict(Zl=Zl, Zc=Zc, Tl=Tl, Tc=Tc, qT=r1[k]["qT"], Kd=Kd, Vw=Vw, mask=msk, sink=sk,
                                wout=wo_t, xT=fa1_xin(x_lat, x_ctx, k), modl=mlm, modc=mcm))
            r2 = run(_prog("fa2", build_fa2), in2).results
            ys = [from_fm(q["yT"]) for q in r2]
            x_lat = np.concatenate([y[:1024] for y in ys], 0)
            x_ctx = np.concatenate([y[1024:] for y in ys], 0)
        else:
            Wt = rg_weights(f32(rg_w_in)[i], f32(rg_conv_w)[i], f32(rg_conv_b)[i], f32(rg_w_a)[i], f32(rg_b_a)[i],
                            f32(rg_w_i)[i], f32(rg_b_i)[i], f32(rg_lambda)[i], f32(rg_w_out)[i])
            common = dict(modl=mlm, modc=mcm, g=gm, **Wt)
            xin = [rg_xin(x_lat, x_ctx, k) for k in range(NCORES)]
            wA = np.ascontiguousarray(Wt["wall"][:32])
            wB = np.ascontiguousarray(Wt["wall"][32:48])
            in1 = [dict(common, wall=wA, xT=xin[k][0], edge=xin[k][1]) for k in range(NCORES)]
            r1 = run(_prog("rgA", lambda: build_rg(1)), in1).results
            car = rg_carries([q["S"] for q in r1])
            in2 = [dict(Y0=r1[k]["Y0"], PF=r1[k]["PF"], PR=r1[k]["PR"], CA=car[k][0], CB=car[k][1], wout=wB,
                        xT=fa1_xin(x_lat, x_ctx, k), modl=mlm, modc=mcm) for k in range(NCORES)]
            r2 = run(_prog("rgB", build_rgb), in2).results
            ys = [from_fm(q["yT"]) for q in r2]
            x_lat = np.concatenate([y[:1024] for y in ys], 0)
            x_ctx = np.concatenate([y[1024:] for y in ys], 0)
        last = (layer == 3)
        nctx = 0 if last else 32
        Wf = ffn_weights(f32(ffn_w_up)[layer], f32(ffn_conv_w)[layer], f32(ffn_conv_b)[layer], f32(ffn_w_down)[layer])
        per = ffn_inputs(x_lat, x_ctx, nctx)
        commonf = dict(modl=mlf, modc=mcf, gf=fm(f32(g_ffn)[layer]), **Wf)
        if last:
            commonf["gfin"] = fm(f32(g_final))
        inf = [dict(commonf, **per[k]) for k in range(NCORES)]
        rf = run(_prog(("ffn", nctx, last), lambda: build_ffn(nctx, final=last)), inf).results
        x_lat, xc_new = ffn_outputs(rf, nctx)
        if not last:
            x_ctx = xc_new
    return np.ascontiguousarray(x_lat.reshape(1, 8192, 2048).astype(np.float32))
```

```python
import contextlib
import numpy as np
import ml_dtypes
import concourse.bass as bass
import concourse.mybir as mybir
from concourse.bass_utils import run_bass_kernel_spmd

F32 = mybir.dt.float32
BF16 = mybir.dt.bfloat16
AF = mybir.ActivationFunctionType
ALU = mybir.AluOpType
EPS = 1e-6
NCORES = 8


class Buf:
    __slots__ = ("w", "r", "pr", "name")

    def __init__(self, name=""):
        self.w = {}
        self.r = {}
        self.pr = {}
        self.name = name


def _merge(dst, src):
    for k, (s, v) in src.items():
        if k not in dst or dst[k][1] < v:
            dst[k] = (s, v)


class Tk:
    NDS = 12

    def __init__(self, nc, es):
        self.nc = nc
        self.es = es
        self.eng = {"pe": nc.tensor, "act": nc.scalar, "dve": nc.vector, "pool": nc.gpsimd, "sp": nc.sync}
        self.sem = {}
        self.cnt = {}
        for e in ("pe", "act", "dve", "pool"):
            self.sem[e] = es.enter_context(nc.semaphore("s_" + e))
            self.cnt[e] = 0
        self.seen = {e: {} for e in self.eng}
        self.dsem = {}
        self.dk = {}
        for q in ("sp", "pool", "act"):
            self.dsem[q] = [es.enter_context(nc.semaphore(f"d_{q}{i}")) for i in range(self.NDS)]
            self.dk[q] = 0
        self.nbuf = 0

    def buf(self, name=""):
        self.nbuf += 1
        return Buf(name or f"b{self.nbuf}")

    def sbuf(self, name, shape, dt):
        return self.es.enter_context(self.nc.sbuf_tensor(name, shape, dt))

    def psum(self, name, shape, dt=F32):
        return self.es.enter_context(self.nc.psum_tensor(name, shape, dt))

    def _wait(self, e, need):
        seen = self.seen[e]
        for k, (s, v) in need.items():
            if seen.get(k, 0) >= v:
                continue
            self.eng[e].wait_ge(s, v)
            seen[k] = v

    def _needs(self, e, reads, writes, nowaw):
        need = {}
        for b in reads:
            _merge(need, b.w)
        for b in writes:
            _merge(need, b.r)
            _merge(need, b.pr)
            if not nowaw:
                _merge(need, b.w)
        if e == "pe":
            need.pop("pe", None)
        return need

    def _mark(self, key, ev, reads, writes):
        for b in reads:
            if key not in b.r or b.r[key][1] < ev[1]:
                b.r[key] = ev
        for b in writes:
            if b.r:
                b.pr = dict(b.r)
                b.r = {}
                b.w = {}
            b.w[key] = ev

    def op(self, e, fn, reads=(), writes=(), nowaw=False):
        self._wait(e, self._needs(e, reads, writes, nowaw))
        ins = fn()
        self.cnt[e] += 1
        ins.then_inc(self.sem[e], 1)
        ev = (self.sem[e], self.cnt[e])
        self._mark(e, ev, reads, writes)
        return ins

    def dma(self, q, out, in_, reads=(), writes=(), nowaw=False):
        need = self._needs(q, reads, writes, nowaw)
        k = self.dk[q]
        slot = k % self.NDS
        gen = k // self.NDS
        s = self.dsem[q][slot]
        key = f"d_{q}{slot}"
        if gen > 0:
            _merge(need, {key: (s, 16 * gen)})
        self._wait(q, need)
        ins = self.eng[q].dma_start(out=out, in_=in_)
        ins.then_inc(s, 16)
        self.dk[q] = k + 1
        ev = (s, 16 * (gen + 1))
        self._mark(key, ev, reads, writes)
        return ins

    def finish(self, q, bufs):
        need = {}
        for b in bufs:
            _merge(need, b.w)
        self._wait(q, need)


def new_nc():
    return bass.Bass("TRN2", target_bir_lowering=False)


def dram_in(nc, name, shape, dt=F32):
    return nc.dram_tensor(name, list(shape), dt, kind="ExternalInput").ap()


def dram_out(nc, name, shape, dt=F32):
    return nc.dram_tensor(name, list(shape), dt, kind="ExternalOutput").ap()


def run(nc, in_maps, trace=False):
    res = run_bass_kernel_spmd(nc, in_maps, core_ids=list(range(len(in_maps))), trace=trace)
    return res


D = 2048
KC = 16


def col_banks(W):
    return [(s, min(s + 512, W)) for s in range(0, W, 512)]


def fm(v):
    return np.ascontiguousarray(np.asarray(v).reshape(-1, 128).T)


def to_fm(x):
    T, F = x.shape
    return np.ascontiguousarray(x.T.reshape(F // 128, 128, T).transpose(1, 0, 2))


def from_fm(y):
    return np.ascontiguousarray(y.transpose(2, 1, 0).reshape(y.shape[2], -1))


def tile_w(w, nsplit=None):
    K, N = w.shape
    t = w.reshape(K // 128, 128, N // 128, 128)
    return np.ascontiguousarray(t.transpose(2, 1, 0, 3)).reshape(N // 128, 128, (K // 128) * 128)


class Common:
    pass


def emit_consts(tk, nc):
    ones = tk.sbuf("ones", [128, 128], F32)
    b = tk.buf("ones")
    tk.op("dve", lambda: nc.vector.memset(ones[:], 1.0), writes=[b])
    return ones, b


def emit_norm_mod(tk, nc, X, bX, H, bH, segs, W, mods, gvec, bC, ones, bOnes, PT, bPT, tmp, btmp, RS, bRS,
                  AB, bAB, hmask=None, sqb=None):
    banks = col_banks(W)
    names = [s[0] for s in segs]
    for i, nm in enumerate(names):
        m = mods[nm]
        tk.op("dve", lambda i=i, m=m: nc.vector.tensor_scalar(
            out=AB[:, i, :], in0=m[:, 1, :], scalar1=1.0, scalar2=None, op0=ALU.add),
            reads=[bC], writes=[bAB], nowaw=True)
        tk.op("dve", lambda i=i: nc.vector.tensor_tensor(
            out=AB[:, i, :], in0=AB[:, i, :], in1=gvec[:], op=ALU.mult),
            reads=[bC, bAB], writes=[bAB])
    sq_t, sq_b, sq_ones = (tmp, btmp, ones) if sqb is None else sqb
    for c in range(KC):
        s = c % 2
        tk.op("act", lambda c=c, s=s: nc.scalar.activation(out=sq_t[s][:, 0:W], in_=X[:, c, :], func=AF.Square),
              reads=[bX[c]], writes=[sq_b[s]])
        for (b0, b1) in banks:
            tk.op("pe", lambda c=c, s=s, b0=b0, b1=b1: nc.tensor.matmul(
                PT[:, b0:b1], lhsT=sq_ones[:], rhs=sq_t[s][:, b0:b1], start=(c == 0), stop=(c == KC - 1)),
                reads=[sq_b[s], bOnes], writes=[bPT], nowaw=(c > 0))
    tk.op("dve", lambda: nc.vector.tensor_scalar(out=RS[:, 0:W], in0=PT[:, 0:W], scalar1=1.0 / D, scalar2=EPS,
                                                  op0=ALU.mult, op1=ALU.add), reads=[bPT], writes=[bRS])
    tk.op("act", lambda: nc.scalar.activation(out=RS[:, 0:W], in_=RS[:, 0:W], func=AF.Sqrt), reads=[bRS], writes=[bRS])
    tk.op("dve", lambda: nc.vector.reciprocal(out=RS[:, 0:W], in_=RS[:, 0:W]), reads=[bRS], writes=[bRS])
    for c in range(KC):
        s = c % 2
        tk.op("dve", lambda c=c, s=s: nc.vector.tensor_tensor(out=tmp[s][:, 0:W], in0=X[:, c, :], in1=RS[:, 0:W],
                                                              op=ALU.mult),
              reads=[bX[c], bRS], writes=[btmp[s]])
        for i, (nm, s0, s1) in enumerate(segs):
            m = mods[nm]
            tk.op("act", lambda c=c, s=s, s0=s0, s1=s1, i=i, m=m: nc.scalar.activation(
                out=H[:, c, s0:s1], in_=tmp[s][:, s0:s1], func=AF.Identity,
                bias=m[:, 0, c:c + 1], scale=AB[:, i, c:c + 1]),
                reads=[btmp[s], bAB, bC], writes=[bH], nowaw=True)
    if hmask:
        edg, cols = hmask
        for k, col in enumerate(cols):
            tk.op("dve", lambda k=k, col=col: nc.vector.tensor_scalar(
                out=H[:, :, col:col + 1], in0=H[:, :, col:col + 1], scalar1=edg[:, k:k + 1], scalar2=None,
                op0=ALU.mult), reads=[bH, bC], writes=[bH])


class Linear:
    def __init__(self, tk, nc, name, nslots=3, kc=KC):
        self.tk, self.nc = tk, nc
        self.kc = kc
        self.WT = [tk.sbuf(f"{name}_w{i}", [128, kc * 128], BF16) for i in range(nslots)]
        self.bW = [tk.buf() for _ in range(nslots)]
        self.n = 0
        self.loaded = 0
        self.queue = []

    def plan(self, tiles):
        self.queue = list(tiles)
        for _ in range(len(self.WT) - 1):
            self._load()

    def _load(self):
        if self.loaded < len(self.queue):
            i = self.loaded
            s = i % len(self.WT)
            self.tk.dma("pool", self.WT[s][:], self.queue[i], writes=[self.bW[s]])
            self.loaded += 1

    def run(self, H, bH, W, PT, bPT, hreads=()):
        tk, nc = self.tk, self.nc
        bPTs = list(bPT) if isinstance(bPT, (list, tuple)) else [bPT]
        self._load()
        i = self.n
        s = i % len(self.WT)
        self.n += 1
        wt = self.WT[s]
        for c in range(self.kc):
            for (b0, b1) in col_banks(W):
                tk.op("pe", lambda c=c, b0=b0, b1=b1: nc.tensor.matmul(
                    PT[:, b0:b1], lhsT=wt[:, c * 128:(c + 1) * 128], rhs=H[:, c, b0:b1],
                    start=(c == 0), stop=(c == self.kc - 1)),
                    reads=[bH, self.bW[s]] + list(hreads), writes=bPTs, nowaw=(c > 0))


D = 2048
KC = 16
DFF = 5632
NP = 44
GP = 4
NG = NP // GP


def ffn_layout(NCTX):
    segs = [("lat", 0, 1026)]
    W = 1026
    if NCTX:
        segs.append(("ctx", 1026, 1026 + NCTX + 2))
        W += NCTX + 2
    return segs, W


def build_ffn(NCTX, final=False):
    segs, W = ffn_layout(NCTX)
    banks = [(s, min(s + 512, W)) for s in range(0, W, 512)]
    NOUT = 1024 + NCTX
    nc = new_nc()
    xT = dram_in(nc, "xT", [128, KC, W])
    edge = dram_in(nc, "edge", [128, 4])
    modl = dram_in(nc, "modl", [128, 3, KC])
    modc = dram_in(nc, "modc", [128, 3, KC])
    gfd = dram_in(nc, "gf", [128, KC])
    wup = dram_in(nc, "wup", [NP, 128, KC * 256])
    wdn = dram_in(nc, "wdn", [NG, 128, GP * D])
    cwd = dram_in(nc, "cw", [128, 2 * NP, 3])
    cbd = dram_in(nc, "cb", [128, 2 * NP])
    yT = dram_out(nc, "yT", [128, KC, NOUT])
    if final:
        gfind = dram_in(nc, "gfin", [128, KC])

    es = contextlib.ExitStack()
    with es:
        tk = Tk(nc, es)
        X = tk.sbuf("X", [128, KC, W], F32)
        H = tk.sbuf("H", [128, KC, W], BF16)
        Abuf = [tk.sbuf(f"A{i}", [128, GP, W], BF16) for i in range(2)]
        WU = [tk.sbuf(f"WU{i}", [128, KC * 256], BF16) for i in range(3)]
        WD = [tk.sbuf(f"WD{i}", [128, GP * D], BF16) for i in range(2)]
        UG = [tk.sbuf(f"UG{i}", [128, W], F32) for i in range(2)]
        UV = [tk.sbuf(f"UV{i}", [128, W], F32) for i in range(2)]
        TG = tk.sbuf("TG", [128, W], F32)
        TV = tk.sbuf("TV", [128, W], F32)
        SQ = [TG, TV]
        RS = UG[0]
        ones = tk.sbuf("ones", [128, 128], F32)
        onesb = tk.sbuf("onesb", [128, 128], BF16)
        SQB = [tk.sbuf(f"SQB{i}", [128, W], BF16) for i in range(2)]
        edg = tk.sbuf("edg", [128, 4], F32)
        ml = tk.sbuf("ml", [128, 3, KC], F32)
        mc = tk.sbuf("mc", [128, 3, KC], F32)
        gf = tk.sbuf("gfs", [128, KC], F32)
        AB = tk.sbuf("AB", [128, 2, KC], F32)
        cw = tk.sbuf("cws", [128, 2 * NP, 3], F32)
        cb = tk.sbuf("cbs", [128, 2 * NP], F32)
        PG = tk.psum("PG", [128, 1536])
        PV = tk.psum("PV", [128, 1536])
        PD = [tk.psum(f"PD{i}", [128, 512]) for i in range(2)]

        bX = [tk.buf(f"X{c}") for c in range(KC)]
        bH = tk.buf("H")
        bA = [tk.buf() for _ in range(2)]
        bWU = [tk.buf() for _ in range(3)]
        bWD = [tk.buf() for _ in range(2)]
        bUG = [tk.buf() for _ in range(2)]
        bUV = [tk.buf() for _ in range(2)]
        bTG, bTV = tk.buf(), tk.buf()
        bRS = bUG[0]
        bSQ = [bTG, bTV]
        bPG, bPV = tk.buf(), tk.buf()
        bPD = [tk.buf() for _ in range(2)]
        bC = tk.buf("consts")

        for c in range(KC):
            tk.dma("sp", X[:, c, :], xT[:, c, :], writes=[bX[c]])
        tk.dma("sp", edg[:], edge[:, :], writes=[bC], nowaw=True)
        tk.dma("sp", ml[:], modl[:, :, :], writes=[bC], nowaw=True)
        tk.dma("sp", mc[:], modc[:, :, :], writes=[bC], nowaw=True)
        tk.dma("sp", gf[:], gfd[:, :], writes=[bC], nowaw=True)
        tk.dma("sp", cw[:], cwd[:, :, :], writes=[bC], nowaw=True)
        tk.dma("sp", cb[:], cbd[:, :], writes=[bC], nowaw=True)

        def load_wu(p):
            tk.dma("pool", WU[p % 3][:], wup[p, :, :], reads=(bX if p < 2 else []), writes=[bWU[p % 3]])

        def load_wd(g):
            tk.dma("pool", WD[g % 2][:], wdn[g, :, :], reads=(bX if g == 0 else []), writes=[bWD[g % 2]])

        load_wu(0)
        load_wu(1)
        load_wd(0)

        tk.op("dve", lambda: nc.vector.memset(ones[:], 1.0), writes=[bC], nowaw=True)
        tk.op("dve", lambda: nc.vector.memset(onesb[:], 1.0), writes=[bC], nowaw=True)
        bSQB = [tk.buf(), tk.buf()]
        for i in range(2):
            tk.op("dve", lambda i=i: nc.vector.memset(Abuf[i][:], 0.0), writes=[bA[i]])
        bAB = tk.buf()
        for i, m in enumerate((ml, mc)):
            tk.op("dve", lambda i=i, m=m: nc.vector.tensor_scalar(
                out=AB[:, i, :], in0=m[:, 1, :], scalar1=1.0, scalar2=None, op0=ALU.add),
                reads=[bC], writes=[bAB], nowaw=True)
            tk.op("dve", lambda i=i: nc.vector.tensor_tensor(
                out=AB[:, i, :], in0=AB[:, i, :], in1=gf[:], op=ALU.mult),
                reads=[bC, bAB], writes=[bAB])

        for c in range(KC):
            s = c % 2
            tk.op("act", lambda c=c, s=s: nc.scalar.activation(out=SQB[s][:], in_=X[:, c, :], func=AF.Square),
                  reads=[bX[c]], writes=[bSQB[s]])
            for (b0, b1) in banks:
                tk.op("pe", lambda c=c, s=s, b0=b0, b1=b1: nc.tensor.matmul(
                    PG[:, b0:b1], lhsT=onesb[:], rhs=SQB[s][:, b0:b1], start=(c == 0), stop=(c == KC - 1)),
                    reads=[bSQB[s], bC], writes=[bPG], nowaw=(c > 0))
        tk.op("dve", lambda: nc.vector.tensor_scalar(out=RS[:], in0=PG[:, 0:W], scalar1=1.0 / D, scalar2=EPS,
                                                      op0=ALU.mult, op1=ALU.add), reads=[bPG], writes=[bRS])
        tk.op("act", lambda: nc.scalar.activation(out=RS[:], in_=RS[:], func=AF.Sqrt), reads=[bRS], writes=[bRS])
        tk.op("dve", lambda: nc.vector.reciprocal(out=RS[:], in_=RS[:]), reads=[bRS], writes=[bRS])
        for c in range(KC):
            s = c % 2
            tk.op("dve", lambda c=c, s=s: nc.vector.tensor_tensor(out=SQ[s][:], in0=X[:, c, :], in1=RS[:], op=ALU.mult),
                  reads=[bX[c], bRS], writes=[bSQ[s]])
            for (nm, s0, s1) in segs:
                i = 0 if nm == "lat" else 1
                m = ml if nm == "lat" else mc
                tk.op("act", lambda c=c, s=s, s0=s0, s1=s1, i=i, m=m: nc.scalar.activation(
                    out=H[:, c, s0:s1], in_=SQ[s][:, s0:s1], func=AF.Identity,
                    bias=m[:, 0, c:c + 1], scale=AB[:, i, c:c + 1]),
                    reads=[bSQ[s], bAB, bC], writes=[bH], nowaw=True)
        hcols = [0, 1025] + ([1026, W - 1] if NCTX else [])
        for k, col in enumerate(hcols):
            tk.op("dve", lambda k=k, col=col: nc.vector.tensor_scalar(
                out=H[:, :, col:col + 1], in0=H[:, :, col:col + 1], scalar1=edg[:, k:k + 1], scalar2=None,
                op0=ALU.mult), reads=[bH, bC], writes=[bH])

        def up_pair(p):
            g, j = divmod(p, GP)
            wu = WU[p % 3]
            s = p % 2
            for half, (PT, bPT, U, bU) in enumerate(((PG, bPG, UG[s], bUG[s]), (PV, bPV, UV[s], bUV[s]))):
                for c in range(KC):
                    for (b0, b1) in banks:
                        tk.op("pe", lambda c=c, b0=b0, b1=b1, PT=PT, half=half: nc.tensor.matmul(
                            PT[:, b0:b1], lhsT=wu[:, c * 256 + half * 128: c * 256 + half * 128 + 128],
                            rhs=H[:, c, b0:b1], start=(c == 0), stop=(c == KC - 1)),
                            reads=[bH, bWU[p % 3]], writes=[bPT], nowaw=(c > 0))
                tk.op("act", lambda PT=PT, U=U: nc.scalar.copy(out=U[:], in_=PT[:, 0:W]), reads=[bPT], writes=[bU])
            deferred = []
            for half, (U, bU, T, bT) in enumerate(((UG[s], bUG[s], TG, bTG), (UV[s], bUV[s], TV, bTV))):
                ch = p if half == 0 else NP + p
                deferred.append(lambda U=U, T=T, ch=ch, bU=bU, bT=bT: tk.op("dve", lambda: nc.vector.tensor_scalar(
                    out=T[:, 1:W - 1], in0=U[:, 1:W - 1], scalar1=cw[:, ch, 1:2], scalar2=cb[:, ch:ch + 1],
                    op0=ALU.mult, op1=ALU.add), reads=[bU, bC], writes=[bT]))
                deferred.append(lambda U=U, T=T, ch=ch, bU=bU, bT=bT: tk.op("dve", lambda: nc.vector.scalar_tensor_tensor(
                    out=T[:, 1:W - 1], in0=U[:, 0:W - 2], scalar=cw[:, ch, 0:1], in1=T[:, 1:W - 1],
                    op0=ALU.mult, op1=ALU.add), reads=[bU, bT, bC], writes=[bT]))
                deferred.append(lambda U=U, T=T, ch=ch, bU=bU, bT=bT: tk.op("dve", lambda: nc.vector.scalar_tensor_tensor(
                    out=T[:, 1:W - 1], in0=U[:, 2:W], scalar=cw[:, ch, 2:3], in1=T[:, 1:W - 1],
                    op0=ALU.mult, op1=ALU.add), reads=[bU, bT, bC], writes=[bT]))
            deferred.append(lambda: tk.op("act", lambda: nc.scalar.activation(
                out=TG[:, 1:W - 1], in_=TG[:, 1:W - 1], func=AF.Silu), reads=[bTG], writes=[bTG]))
            deferred.append(lambda g=g, j=j: tk.op("dve", lambda: nc.vector.tensor_tensor(
                out=Abuf[g % 2][:, j, 1:W - 1], in0=TG[:, 1:W - 1], in1=TV[:, 1:W - 1], op=ALU.mult),
                reads=[bTG, bTV], writes=[bA[g % 2]], nowaw=(j > 0)))
            return deferred

        dstate = {"n": 0}

        dbanks = [(1, 513), (513, 1025)]

        def down_part(g, part, deferred):
            items = [(bi, d) for bi in range(len(dbanks)) for d in range(KC)]
            per = (len(items) + GP - 1) // GP
            for (bi, d) in items[part * per:(part + 1) * per]:
                b0, b1 = dbanks[bi]
                k = dstate["n"] % 2
                dstate["n"] += 1
                for f in range(GP):
                    tk.op("pe", lambda f=f, d=d, b0=b0, b1=b1, k=k: nc.tensor.matmul(
                        PD[k][:, 0:b1 - b0], lhsT=WD[g % 2][:, f * D + d * 128: f * D + d * 128 + 128],
                        rhs=Abuf[g % 2][:, f, b0:b1], start=(f == 0), stop=(f == GP - 1)),
                        reads=[bA[g % 2], bWD[g % 2]], writes=[bPD[k]], nowaw=(f > 0))
                tk.op("dve", lambda d=d, b0=b0, b1=b1, k=k: nc.vector.scalar_tensor_tensor(
                    out=X[:, d, b0:b1], in0=PD[k][:, 0:b1 - b0], scalar=ml[:, 2, d:d + 1],
                    in1=X[:, d, b0:b1], op0=ALU.mult, op1=ALU.add),
                    reads=[bPD[k], bC, bX[d]], writes=[bX[d]])
                if deferred:
                    deferred.pop(0)()
            if NCTX and part == GP - 1:
                c0, c1 = 1027, 1027 + NCTX
                k = dstate["n"] % 2
                dstate["n"] += 1
                for d in range(KC):
                    for f in range(GP):
                        tk.op("pe", lambda f=f, d=d, k=k: nc.tensor.matmul(
                            PD[k][:, d * NCTX:(d + 1) * NCTX], lhsT=WD[g % 2][:, f * D + d * 128: f * D + d * 128 + 128],
                            rhs=Abuf[g % 2][:, f, c0:c1], start=(f == 0), stop=(f == GP - 1)),
                            reads=[bA[g % 2], bWD[g % 2]], writes=[bPD[k]], nowaw=not (d == 0 and f == 0))
                for d in range(KC):
                    tk.op("dve", lambda d=d, k=k: nc.vector.scalar_tensor_tensor(
                        out=X[:, d, c0:c1], in0=PD[k][:, d * NCTX:(d + 1) * NCTX], scalar=mc[:, 2, d:d + 1],
                        in1=X[:, d, c0:c1], op0=ALU.mult, op1=ALU.add),
                        reads=[bPD[k], bC, bX[d]], writes=[bX[d]])
                    if deferred:
                        deferred.pop(0)()

        prev_def = []
        for g in range(NG + 1):
            if 1 <= g < NG:
                load_wd(g)
            for j in range(GP):
                p = g * GP + j
                while prev_def:
                    prev_def.pop(0)()
                if g < NG:
                    if p + 2 < NP:
                        load_wu(p + 2)
                    prev_def = up_pair(p)
                if g >= 1:
                    down_part(g - 1, j, [])

        bOut = tk.buf()
        if final:
            gfin = tk.sbuf("gfins", [128, KC], F32)
            bGF = tk.buf()
            tk.dma("sp", gfin[:], gfind[:, :], writes=[bGF])
            for c in range(KC):
                s = c % 2
                tk.op("act", lambda c=c, s=s: nc.scalar.activation(out=SQ[s][:], in_=X[:, c, :], func=AF.Square),
                      reads=[bX[c]], writes=[bSQ[s]])
                for (b0, b1) in banks:
                    tk.op("pe", lambda c=c, s=s, b0=b0, b1=b1: nc.tensor.matmul(
                        PG[:, b0:b1], lhsT=ones[:], rhs=SQ[s][:, b0:b1], start=(c == 0), stop=(c == KC - 1)),
                        reads=[bSQ[s], bC], writes=[bPG], nowaw=(c > 0))
            tk.op("dve", lambda: nc.vector.tensor_scalar(out=RS[:], in0=PG[:, 0:W], scalar1=1.0 / D, scalar2=EPS,
                                                          op0=ALU.mult, op1=ALU.add), reads=[bPG], writes=[bRS])
            tk.op("act", lambda: nc.scalar.activation(out=RS[:], in_=RS[:], func=AF.Sqrt), reads=[bRS], writes=[bRS])
            tk.op("dve", lambda: nc.vector.reciprocal(out=RS[:], in_=RS[:]), reads=[bRS], writes=[bRS])
            for c in range(KC):
                s = c % 2
                tk.op("dve", lambda c=c, s=s: nc.vector.tensor_tensor(out=SQ[s][:], in0=X[:, c, :], in1=RS[:], op=ALU.mult),
                      reads=[bX[c], bRS], writes=[bSQ[s]])
                tk.op("act", lambda c=c, s=s: nc.scalar.activation(out=SQ[s][:], in_=SQ[s][:], func=AF.Identity,
                                                                   scale=gfin[:, c:c + 1]),
                      reads=[bSQ[s], bGF], writes=[bSQ[s]])
                tk.dma("sp", yT[:, c, 0:1024], SQ[s][:, 1:1025], reads=[bSQ[s]], writes=[bOut], nowaw=True)
        for c in range(KC if not final else 0):
            tk.dma("sp", yT[:, c, 0:1024], X[:, c, 1:1025], reads=[bX[c]], writes=[bOut], nowaw=True)
            if NCTX:
                tk.dma("sp", yT[:, c, 1024:1024 + NCTX], X[:, c, 1027:1027 + NCTX], reads=[bX[c]], writes=[bOut],
                       nowaw=True)
        tk.finish("sp", [bOut])
    return nc


def fm(v):
    return np.ascontiguousarray(v.reshape(KC, 128).T)


def ffn_weights(w_up, conv_w, conv_b, w_down):
    wu = w_up.reshape(KC, 128, 2, NP, 128)
    wu = np.ascontiguousarray(wu.transpose(3, 1, 0, 2, 4)).reshape(NP, 128, KC * 256)
    wd = w_down.reshape(NG, GP, 128, D)
    wd = np.ascontiguousarray(wd.transpose(0, 2, 1, 3)).reshape(NG, 128, GP * D)
    cw = np.ascontiguousarray(conv_w.reshape(3, 2 * NP, 128).transpose(2, 1, 0))
    cb = np.ascontiguousarray(conv_b.reshape(2 * NP, 128).T)
    return dict(wup=wu, wdn=wd, cw=cw, cb=cb)


def ffn_inputs(x_lat, x_ctx, NCTX):
    segs, W = ffn_layout(NCTX)
    outs = []
    for i in range(NCORES):
        cols = np.zeros((W, D), np.float32)
        lo, hi = i * 1024, (i + 1) * 1024
        cols[1:1025] = x_lat[lo:hi]
        e = np.zeros((128, 4), np.float32)
        if i > 0:
            cols[0] = x_lat[lo - 1]; e[:, 0] = 1
        if i < NCORES - 1:
            cols[1025] = x_lat[hi]; e[:, 1] = 1
        if NCTX:
            clo, chi = i * NCTX, (i + 1) * NCTX
            cols[1027:1027 + NCTX] = x_ctx[clo:chi]
            if i > 0:
                cols[1026] = x_ctx[clo - 1]; e[:, 2] = 1
            if i < NCORES - 1:
                cols[W - 1] = x_ctx[chi]; e[:, 3] = 1
        xT = np.ascontiguousarray(cols.T.reshape(KC, 128, W).transpose(1, 0, 2))
        outs.append(dict(xT=xT, edge=e))
    return outs


def ffn_outputs(results, NCTX):
    lat, ctx = [], []
    for r in results:
        y = r["yT"]
        t = y.transpose(2, 1, 0).reshape(y.shape[2], D)
        lat.append(t[:1024])
        if NCTX:
            ctx.append(t[1024:])
    return np.concatenate(lat, 0), (np.concatenate(ctx, 0) if NCTX else None)


W1 = 1056
NLAT = 1024
NCTX_FA = 32
SEGS1 = [("lat", 0, 1024), ("ctx", 1024, 1056)]
NOC = 18


def build_fa1(stage=9):
    W = W1
    nc = new_nc()
    xT = dram_in(nc, "xT", [128, KC, W])
    modl = dram_in(nc, "modl", [128, 3, KC])
    modc = dram_in(nc, "modc", [128, 3, KC])
    gd = dram_in(nc, "g", [128, KC])
    win = dram_in(nc, "win", [NOC, 128, KC * 128])
    cosd = dram_in(nc, "cosT", [128, NLAT])
    sind = dram_in(nc, "sinT", [128, NLAT])
    rmd = dram_in(nc, "Rm", [128, 128])
    csd = dram_in(nc, "CS", [128, 2 * 512])
    Zo = dram_out(nc, "Z", [W, 2048], BF16)
    qo = dram_out(nc, "qT", [128, 8, W], BF16)
    ko = dram_out(nc, "kT", [128, W], BF16)
    vo = dram_out(nc, "vT", [128, W], BF16)

    es = contextlib.ExitStack()
    with es:
        tk = Tk(nc, es)
        X = tk.sbuf("X", [128, KC, W], F32)
        H = tk.sbuf("H", [128, KC, W], BF16)
        FT = tk.sbuf("FT", [128, 8, W], BF16)
        tmp = [tk.sbuf(f"tmp{i}", [128, W], F32) for i in range(2)]
        RS = tk.sbuf("RS", [128, W], F32)
        ml = tk.sbuf("ml", [128, 3, KC], F32)
        mc = tk.sbuf("mc", [128, 3, KC], F32)
        g = tk.sbuf("gs", [128, KC], F32)
        AB = tk.sbuf("AB", [128, 2, KC], F32)
        COS = tk.sbuf("COS", [128, NLAT], F32)
        SIN = tk.sbuf("SIN", [128, NLAT], F32)
        RM = tk.sbuf("RM", [128, 128], BF16)
        CS = tk.sbuf("CSs", [128, 2 * 512], BF16)
        QR = [tk.sbuf(f"QR{i}", [128, NLAT], BF16) for i in range(2)]
        QO = [tk.sbuf(f"QO{i}", [128, W], BF16) for i in range(2)]
        ZT = [tk.sbuf(f"ZT{i}", [128, 2048], BF16) for i in range(2)]
        PA = tk.psum("PA", [128, 1536])
        PB = tk.psum("PB", [128, 1536])
        PR = tk.psum("PR", [128, 1024])

        bX = [tk.buf() for _ in range(KC)]
        bH, bFT, bRS, bAB, bC = tk.buf(), tk.buf(), tk.buf(), tk.buf(), tk.buf()
        btmp = [tk.buf() for _ in range(2)]
        bQR = [tk.buf() for _ in range(2)]
        bQO = [tk.buf() for _ in range(2)]
        bZT = [tk.buf() for _ in range(2)]
        bPA, bPB = tk.buf(), tk.buf()
        bPR = [tk.buf(), tk.buf()]

        for c in range(KC):
            tk.dma("sp", X[:, c, :], xT[:, c, :], writes=[bX[c]])
        for (dst, src) in ((ml[:], modl[:, :, :]), (mc[:], modc[:, :, :]), (g[:], gd[:, :]),
                           (COS[:], cosd[:, :]), (SIN[:], sind[:, :])):
            tk.dma("sp", dst, src, writes=[bC], nowaw=True)
        tk.dma("pool", RM[:], rmd[:, :], writes=[bC], nowaw=True)
        tk.dma("pool", CS[:], csd[:, :], writes=[bC], nowaw=True)
        lin = Linear(tk, nc, "win")
        lin.plan([win[i, :, :] for i in range(NOC)])
        ones, bOnes = emit_consts(tk, nc)
        onesb = tk.sbuf("onesb", [128, 128], BF16)
        SQB = [tk.sbuf(f"SQB{i}", [128, W], BF16) for i in range(2)]
        bSQB = [tk.buf(), tk.buf()]
        tk.op("dve", lambda: nc.vector.memset(onesb[:], 1.0), writes=[bOnes], nowaw=True)

        emit_norm_mod(tk, nc, X, bX, H, bH, SEGS1, W, {"lat": ml, "ctx": mc}, g, bC, ones, bOnes, PA, bPA,
                      tmp, btmp, RS, bRS, AB, bAB, sqb=(SQB, bSQB, onesb))

        outbuf = tk.buf()
        PTs = [(PA, bPA), (PB, bPB)]
        for oc in range(NOC):
            if stage <= 2 and oc >= 8:
                break
            if stage in (3, 31, 32, 33) and oc >= 9:
                break
            if stage == 4 and oc >= 17:
                break
            PT, bPT = PTs[oc % 2]
            lin.run(H, bH, W, PT, bPT)
            if oc < 8:
                tk.op("act", lambda oc=oc, PT=PT: nc.scalar.copy(out=FT[:, oc, :], in_=PT[:, 0:W]),
                      reads=[bPT], writes=[bFT], nowaw=True)
                if oc == 7 and stage >= 2:
                    tiles = [(t * 128, 128, 0) for t in range(8)] + [(W - 128, 128, 128 - NCTX_FA)]
                    for ti, (t0, nt, r0) in enumerate(tiles):
                        zt, bzt = ZT[ti % 2], bZT[ti % 2]
                        for gi in range(4):
                            hb = gi % 2
                            for cc in range(2):
                                tk.op("pe", lambda gi=gi, cc=cc, t0=t0, nt=nt, hb=hb: nc.tensor.matmul(
                                    PR[0:nt, hb * 512:(hb + 1) * 512], lhsT=FT[:, 2 * gi + cc, t0:t0 + nt],
                                    rhs=CS[:, cc * 512:(cc + 1) * 512], start=(cc == 0), stop=(cc == 1)),
                                    reads=[bFT, bC], writes=[bPR[hb]], nowaw=(cc > 0))
                            tk.op("act", lambda gi=gi, nt=nt, hb=hb, zt=zt: nc.scalar.copy(
                                out=zt[0:nt, gi * 512:(gi + 1) * 512], in_=PR[0:nt, hb * 512:(hb + 1) * 512]),
                                reads=[bPR[hb]], writes=[bzt], nowaw=(gi > 0))
                        tk.dma("sp", Zo[t0 + r0:t0 + nt, :], zt[r0:nt, :], reads=[bzt], writes=[outbuf], nowaw=True)
            elif oc < 17:
                s = oc % 2
                qr, bqr, qo_, bqo = QR[s], bQR[s], QO[s], bQO[s]
                tk.op("act", lambda qr=qr, PT=PT: nc.scalar.copy(out=qr[:], in_=PT[:, 0:NLAT]),
                      reads=[bPT], writes=[bqr])
                for hb in range(2):
                    tk.op("pe", lambda hb=hb, qr=qr: nc.tensor.matmul(
                        PR[:, hb * 512:(hb + 1) * 512], lhsT=RM[:], rhs=qr[:, hb * 512:(hb + 1) * 512],
                        start=True, stop=True), reads=[bqr, bC], writes=[bPR[hb]])
                if stage == 31:
                    continue
                tk.op("act", lambda PT=PT: nc.scalar.copy(out=tmp[0][:, 0:NLAT], in_=PT[:, 0:NLAT]),
                      reads=[bPT], writes=[btmp[0]])
                tk.op("act", lambda: nc.scalar.copy(out=tmp[1][:, 0:NLAT], in_=PR[:, 0:NLAT]),
                      reads=[bPR[0], bPR[1]], writes=[btmp[1]])
                tk.op("dve", lambda: nc.vector.tensor_tensor(
                    out=tmp[0][:, 0:NLAT], in0=tmp[0][:, 0:NLAT], in1=COS[:], op=ALU.mult),
                    reads=[btmp[0], bC], writes=[btmp[0]])
                tk.op("dve", lambda: nc.vector.tensor_tensor(
                    out=tmp[1][:, 0:NLAT], in0=tmp[1][:, 0:NLAT], in1=SIN[:], op=ALU.mult),
                    reads=[btmp[1], bC], writes=[btmp[1]])
                tk.op("dve", lambda qo_=qo_: nc.vector.tensor_tensor(
                    out=qo_[:, 0:NLAT], in0=tmp[0][:, 0:NLAT], in1=tmp[1][:, 0:NLAT], op=ALU.add),
                    reads=[btmp[0], btmp[1]], writes=[bqo])
                if stage == 32:
                    continue
                tk.op("act", lambda qo_=qo_, PT=PT: nc.scalar.copy(out=qo_[:, NLAT:W], in_=PT[:, NLAT:W]),
                      reads=[bPT], writes=[bqo], nowaw=True)
                if stage == 33:
                    continue
                dst = qo[:, oc - 8, :] if oc < 16 else ko[:, :]
                tk.dma("sp", dst, qo_[:], reads=[bqo], writes=[outbuf], nowaw=True)
            else:
                s = oc % 2
                qo_, bqo = QO[s], bQO[s]
                tk.op("act", lambda qo_=qo_, PT=PT: nc.scalar.copy(out=qo_[:], in_=PT[:, 0:W]),
                      reads=[bPT], writes=[bqo])
                tk.dma("sp", vo[:, :], qo_[:], reads=[bqo], writes=[outbuf], nowaw=True)
        tk.finish("sp", [outbuf])
    return nc


def rope_tables(core):
    t = np.arange(core * 1024, (core + 1) * 1024)
    row, col = t // 64, t % 64
    p = np.arange(128)
    dd = p % 64
    half = dd // 32
    within = dd % 32
    j = within % 16
    part = within // 16
    inv = (10000.0 ** (-np.arange(16, dtype=np.float32) / 16)).astype(np.float32)
    pos = np.where(half[:, None] == 0, row[None, :], col[None, :]).astype(np.float32)
    ang = pos * inv[j][:, None]
    return np.cos(ang).astype(np.float32), np.sin(ang).astype(np.float32)


def rope_rm():
    Rm = np.zeros((128, 128), np.float32)
    for pp in range(128):
        within = (pp % 64) % 32
        if within < 16:
            Rm[pp + 16, pp] = -1.0
        else:
            Rm[pp - 16, pp] = 1.0
    return Rm


def chan_dft_table():
    c = np.arange(256)
    ang = 2 * np.pi * np.outer(c, c) / 256.0
    C = (np.cos(ang) / 16.0).astype(np.float32)
    S = (np.sin(ang) / 16.0).astype(np.float32)
    CS = np.zeros((128, 2, 512), np.float32)
    for cc in range(2):
        CS[:, cc, 0:256] = C[cc * 128:(cc + 1) * 128]
        CS[:, cc, 256:512] = S[cc * 128:(cc + 1) * 128]
    return CS.reshape(128, 1024)


def fa1_xin(x_lat, x_ctx, core):
    cols = np.concatenate([x_lat[core * 1024:(core + 1) * 1024], x_ctx[core * NCTX_FA:(core + 1) * NCTX_FA]], 0)
    return to_fm(cols)


W_FA2 = 1056
NLAT = 1024
NCTX_FA = 32
NSEQ = 8192
NKB = 10
NVT = 12
KW = NKB * 128
SEGS_FA2 = [("lat", 0, 1024), ("ctx", 1024, 1056)]


def build_fa2():
    nc = new_nc()
    Zl = dram_in(nc, "Zl", [NSEQ, 2048], BF16)
    Zc = dram_in(nc, "Zc", [256, 2048], BF16)
    Tl = dram_in(nc, "Tl", [64, 128, 2, 1024], BF16)
    Tc = dram_in(nc, "Tc", [2, 128, 2, NCTX_FA], BF16)
    qd = dram_in(nc, "qT", [128, 8, W_FA2], BF16)
    kd = dram_in(nc, "Kd", [2, 128, KW + 256], BF16)
    vd = dram_in(nc, "Vw", [128, NVT, 128], BF16)
    md = dram_in(nc, "mask", [128, NKB, 384], BF16)
    sd = dram_in(nc, "sink", [128, 16])
    wout = dram_in(nc, "wout", [16, 128, KC * 128])
    xT = dram_in(nc, "xT", [128, KC, W_FA2])
    modl = dram_in(nc, "modl", [128, 3, KC])
    modc = dram_in(nc, "modc", [128, 3, KC])
    yT = dram_out(nc, "yT", [128, KC, W_FA2])

    es = contextlib.ExitStack()
    with es:
        tk = Tk(nc, es)
        CAT = tk.sbuf("CAT", [128, KC, W_FA2], BF16)
        Q = tk.sbuf("Q", [128, 8, W_FA2], BF16)
        ZB = [tk.sbuf(f"ZB{i}", [128, 2048], BF16) for i in range(4)]
        TB = [tk.sbuf(f"TB{i}", [128, 2, 512], BF16) for i in range(4)]
        KD = tk.sbuf("KD", [128, 2, KW + 256], BF16)
        VW = tk.sbuf("VW", [128, NVT, 128], BF16)
        VA = tk.sbuf("VA", [128, 2, 2, NVT, 128], BF16)
        MK = tk.sbuf("MK", [128, NKB, 384], BF16)
        SK = tk.sbuf("SK", [128, 16], F32)
        EB = [tk.sbuf(f"EB{i}", [128, 512], BF16) for i in range(4)]
        RD = tk.sbuf("RD", [128, W_FA2], F32)
        NUM = tk.sbuf("NUM", [128, W_FA2], F32)
        XB = [tk.sbuf(f"XB{i}", [128, W_FA2], F32) for i in range(3)]
        ml = tk.sbuf("ml", [128, 3, KC], F32)
        mc = tk.sbuf("mc", [128, 3, KC], F32)
        ALL = tk.psum("ALL", [128, 4096])

        def bank(b, n=1):
            return ALL[:, b * 512:(b + n) * 512]

        bBank = [tk.buf(f"bank{i}") for i in range(8)]
        bCAT, bQ, bKD, bVW, bVA, bMK, bSK, bC = (tk.buf() for _ in range(8))
        bZB = [tk.buf() for _ in range(4)]
        bTB = [tk.buf() for _ in range(4)]
        bEB = [tk.buf() for _ in range(4)]
        bRD, bNUM = tk.buf(), tk.buf()
        bXB = [tk.buf() for _ in range(3)]

        tk.dma("sp", Q[:], qd[:, :, :], writes=[bQ])
        for kv in range(2):
            tk.dma("sp", KD[:, kv, :], kd[kv, :, :], writes=[bKD], nowaw=True)
        tk.dma("sp", VW[:], vd[:, :, :], writes=[bVW])
        tk.dma("sp", MK[:], md[:, :, :], writes=[bMK])
        tk.dma("sp", SK[:], sd[:, :], writes=[bSK])
        tk.dma("sp", ml[:], modl[:, :, :], writes=[bC], nowaw=True)
        tk.dma("sp", mc[:], modc[:, :, :], writes=[bC], nowaw=True)
        lin = Linear(tk, nc, "wout")
        lin.plan([wout[i, :, :] for i in range(16)])

        for hb in range(2):
            for n in range(64):
                s = n % 4
                tk.dma("sp", ZB[s][:], Zl[n * 128:(n + 1) * 128, :], writes=[bZB[s]])
                tk.dma("sp", TB[s][:], Tl[n, :, :, hb * 512:(hb + 1) * 512], writes=[bTB[s]])
                for fc in range(8):
                    g, ch = divmod(fc, 2)
                    for pq in range(2):
                        tk.op("pe", lambda fc=fc, g=g, ch=ch, pq=pq, s=s, n=n: nc.tensor.matmul(
                            bank(fc), lhsT=ZB[s][:, g * 512 + pq * 256 + ch * 128: g * 512 + pq * 256 + ch * 128 + 128],
                            rhs=TB[s][:, pq, :], start=(n == 0 and pq == 0), stop=(n == 63 and pq == 1)),
                            reads=[bZB[s], bTB[s]], writes=[bBank[fc]], nowaw=not (n == 0 and pq == 0))
            for fc in range(8):
                tk.op("act", lambda fc=fc, hb=hb: nc.scalar.copy(out=CAT[:, fc, hb * 512:(hb + 1) * 512], in_=bank(fc)),
                      reads=[bBank[fc]], writes=[bCAT], nowaw=True)
        for n in range(2):
            s = n % 4
            tk.dma("sp", ZB[s][:], Zc[n * 128:(n + 1) * 128, :], writes=[bZB[s]])
            tk.dma("sp", TB[s][:, :, 0:NCTX_FA], Tc[n, :, :, :], writes=[bTB[s]])
            for fc in range(8):
                g, ch = divmod(fc, 2)
                for pq in range(2):
                    tk.op("pe", lambda fc=fc, g=g, ch=ch, pq=pq, s=s, n=n: nc.tensor.matmul(
                        ALL[:, fc * 512: fc * 512 + NCTX_FA],
                        lhsT=ZB[s][:, g * 512 + pq * 256 + ch * 128: g * 512 + pq * 256 + ch * 128 + 128],
                        rhs=TB[s][:, pq, 0:NCTX_FA], start=(n == 0 and pq == 0), stop=(n == 1 and pq == 1)),
                        reads=[bZB[s], bTB[s]], writes=[bBank[fc]], nowaw=not (n == 0 and pq == 0))
        for fc in range(8):
            tk.op("act", lambda fc=fc: nc.scalar.copy(out=CAT[:, fc, NLAT:W_FA2], in_=ALL[:, fc * 512: fc * 512 + NCTX_FA]),
                  reads=[bBank[fc]], writes=[bCAT], nowaw=True)

        tk.op("act", lambda: nc.scalar.activation(out=SK[:], in_=SK[:], func=AF.Exp), reads=[bSK], writes=[bSK])
        tk.op("dve", lambda: nc.vector.memset(VA[:], 1.0), writes=[bVA])
        for kv in range(2):
            tk.op("dve", lambda kv=kv: nc.vector.tensor_copy(out=VA[:, 0, kv, :, 0:64], in_=VW[:, :, kv * 64:(kv + 1) * 64]),
                  reads=[bVW], writes=[bVA])
            tk.op("dve", lambda kv=kv: nc.vector.tensor_copy(out=VA[:, 1, kv, :, 64:128], in_=VW[:, :, kv * 64:(kv + 1) * 64]),
                  reads=[bVW], writes=[bVA])

        ectr = {"n": 0}
        allwork = []
        cb_banks = col_banks(W_FA2)
        for h in range(16):
            ch, var = divmod(h, 2)
            base = var * 64
            dbase = 64 - base
            kv = h // 8
            pob = 2 + 3 * (h % 2)
            bPO = bBank[pob:pob + 3]
            work = []
            for cbk in range(2):
                for (c0, c1) in cb_banks:
                    work.append((NKB + cbk, KW + cbk * 128, c0, c1, None))
            for j in range(NKB):
                q0, q1 = max(0, j - 2) * 128, min(8, j + 1) * 128
                pieces = [(q0, q1)] if (q0 // 512 == (q1 - 1) // 512) else [(q0, 512), (512, q1)]
                for (c0, c1) in pieces:
                    work.append((j, j * 128, c0, c1, (j, c0 - (j - 2) * 128, c1 - (j - 2) * 128)))
            last = {}
            first = {}
            for wi, wk in enumerate(work):
                b = wk[2] // 512
                last[b] = wi
                first.setdefault(b, wi)
            hctx = dict(h=h, ch=ch, var=var, base=base, dbase=dbase, kv=kv, pob=pob, bPO=bPO, first=first, last=last,
                        nwork=len(work))
            for wi, wk in enumerate(work):
                allwork.append((hctx, wi, wk))

        def emit_front(hc, wi, wk):
            (vt, kc0, c0, c1, msk) = wk
            base, kv, ch = hc["base"], hc["kv"], hc["ch"]
            sb = ectr["n"] % 2
            e = ectr["n"] % 4
            ectr["n"] += 1
            n = c1 - c0
            tk.op("pe", lambda: nc.tensor.matmul(
                ALL[:, sb * 512: sb * 512 + n], lhsT=KD[base:base + 64, kv, kc0:kc0 + 128],
                rhs=Q[base:base + 64, ch, c0:c1], start=True, stop=True),
                reads=[bKD, bQ], writes=[bBank[sb]])
            tk.op("act", lambda: nc.scalar.activation(
                out=EB[e][:, 0:n], in_=ALL[:, sb * 512: sb * 512 + n], func=AF.Exp, scale=0.125),
                reads=[bBank[sb]], writes=[bEB[e]])
            if msk is not None:
                mj, m0, m1 = msk
                tk.op("dve", lambda: nc.vector.tensor_tensor(
                    out=EB[e][:, 0:n], in0=EB[e][:, 0:n], in1=MK[:, mj, m0:m1], op=ALU.mult),
                    reads=[bEB[e], bMK], writes=[bEB[e]])
            return e

        def emit_back(hc, wi, wk, e):
            (vt, kc0, c0, c1, msk) = wk
            base, dbase, kv, ch, var, pob, bPO, h = (hc[k] for k in ("base", "dbase", "kv", "ch", "var", "pob", "bPO", "h"))
            first, last = hc["first"], hc["last"]
            n = c1 - c0
            b = c0 // 512
            tk.op("pe", lambda: nc.tensor.matmul(
                ALL[:, pob * 512 + c0: pob * 512 + c1], lhsT=VA[:, var, kv, vt, :], rhs=EB[e][:, 0:n],
                start=(first[b] == wi), stop=(last[b] == wi)),
                reads=[bVA, bEB[e]], writes=[bPO[b]], nowaw=(first[b] != wi))
            if wi != hc["nwork"] - 1:
                return
            PO = ALL[:, pob * 512: pob * 512 + W_FA2]
            tk.op("dve", lambda: nc.vector.tensor_scalar(
                out=RD[base:base + 64, :], in0=PO[dbase:dbase + 64, :], scalar1=SK[dbase:dbase + 64, h:h + 1],
                scalar2=None, op0=ALU.add), reads=bPO + [bSK], writes=[bRD])
            tk.op("dve", lambda: nc.vector.reciprocal(out=RD[base:base + 64, :], in_=RD[base:base + 64, :]),
                  reads=[bRD], writes=[bRD])
            tk.op("act", lambda: nc.scalar.copy(out=NUM[base:base + 64, :], in_=PO[base:base + 64, :]),
                  reads=bPO, writes=[bNUM])
            tk.op("dve", lambda: nc.vector.tensor_tensor(
                out=CAT[base:base + 64, 8 + ch, :], in0=NUM[base:base + 64, :], in1=RD[base:base + 64, :],
                op=ALU.mult), reads=[bNUM, bRD], writes=[bCAT], nowaw=True)

        DEPTH = 2
        pend = []
        for (hc, wi, wk) in allwork:
            e = emit_front(hc, wi, wk)
            pend.append((hc, wi, wk, e))
            if len(pend) > DEPTH:
                emit_back(*pend.pop(0))
        while pend:
            emit_back(*pend.pop(0))

        bOut = tk.buf()
        for oc in range(16):
            pb = 3 * (oc % 2)
            PT = ALL[:, pb * 512: pb * 512 + 1536]
            xs = oc % 3
            tk.dma("sp", XB[xs][:], xT[:, oc, :], writes=[bXB[xs]])
            lin.run(CAT, bCAT, W_FA2, PT, bBank[pb:pb + 3])
            for (nm, s0, s1) in SEGS_FA2:
                m = ml if nm == "lat" else mc
                tk.op("dve", lambda oc=oc, s0=s0, s1=s1, m=m, PT=PT, xs=xs: nc.vector.scalar_tensor_tensor(
                    out=XB[xs][:, s0:s1], in0=PT[:, s0:s1], scalar=m[:, 2, oc:oc + 1], in1=XB[xs][:, s0:s1],
                    op0=ALU.mult, op1=ALU.add), reads=bBank[pb:pb + 3] + [bXB[xs], bC], writes=[bXB[xs]])
            tk.dma("sp", yT[:, oc, :], XB[xs][:], reads=[bXB[xs]], writes=[bOut], nowaw=True)
        tk.finish("sp", [bOut])
    return nc


def seq_tables(core):
    n = np.arange(NSEQ, dtype=np.int64)
    k = np.arange(core * 1024, (core + 1) * 1024, dtype=np.int64)
    ph = (np.outer(n, k) % NSEQ).astype(np.float64) * (2 * np.pi / NSEQ)
    sc = 1.0 / np.sqrt(NSEQ)
    T = np.stack([np.cos(ph) * sc, -np.sin(ph) * sc], 1)
    Tl = T.reshape(64, 128, 2, 1024).astype(ml_dtypes.bfloat16)
    n = np.arange(256, dtype=np.int64)
    k = np.arange(core * NCTX_FA, (core + 1) * NCTX_FA, dtype=np.int64)
    ph = (np.outer(n, k) % 256).astype(np.float64) * (2 * np.pi / 256)
    T = np.stack([np.cos(ph) / 16.0, -np.sin(ph) / 16.0], 1)
    Tc = T.reshape(2, 128, 2, NCTX_FA).astype(ml_dtypes.bfloat16)
    return Tl, Tc


def attn_mask(core):
    lo = core * 1024
    kk = np.arange(128)
    m = np.zeros((128, NKB, 384), np.float32)
    for j in range(NKB):
        kpos = lo - 128 + 128 * j + kk
        qpos = lo + (j - 2) * 128 + np.arange(384)
        ok = (np.abs(kpos[:, None] - qpos[None, :]) <= 128) & (kpos[:, None] >= 0) & (kpos[:, None] < NSEQ)
        m[:, j, :] = ok
    return m.astype(ml_dtypes.bfloat16)


def fa2_kv(kT_all, kcT, vT_all, vcT, core):
    lo, hi = core * 1024, (core + 1) * 1024
    kw = np.zeros((128, KW), kT_all.dtype)
    vw = np.zeros((128, KW), vT_all.dtype)
    a, b = max(lo - 128, 0), min(hi + 128, NSEQ)
    kw[:, a - (lo - 128): b - (lo - 128)] = kT_all[:, a:b]
    vw[:, a - (lo - 128): b - (lo - 128)] = vT_all[:, a:b]
    kfull = np.concatenate([kw, kcT], 1)
    Kd = np.stack([np.concatenate([kfull[kv * 64:(kv + 1) * 64]] * 2, 0) for kv in range(2)], 0)
    vfull = np.concatenate([vw, vcT], 1).T
    Vw = np.ascontiguousarray(vfull.reshape(NVT, 128, 128).transpose(1, 0, 2))
    return np.ascontiguousarray(Kd), Vw


W_RG = 1062
NCTX_RG = 32
SEGS_RG = [("lat", 0, 1027), ("ctx", 1027, 1062)]
VALID = [(2, 1026), (1029, 1061)]
HALO = [0, 1, 1026, 1027, 1028, 1061]
NOUT_RG = 1024 + NCTX_RG
LRU_C = 8.0
NT = 11


def rev(t2d):
    return bass.AP(tensor=t2d.tensor, offset=t2d.offset + (t2d.ap[-1][1] - 1) * t2d.ap[-1][0],
                   ap=[list(t2d.ap[0]), [-t2d.ap[-1][0], t2d.ap[-1][1]]])


def build_rg(phase):
    nc = new_nc()
    xT = dram_in(nc, "xT", [128, KC, W_RG])
    edge = dram_in(nc, "edge", [128, 6])
    modl = dram_in(nc, "modl", [128, 3, KC])
    modc = dram_in(nc, "modc", [128, 3, KC])
    gd = dram_in(nc, "g", [128, KC])
    wall = dram_in(nc, "wall", [32 if phase == 1 else 48, 128, KC * 128])
    cwd = dram_in(nc, "cw", [128, KC, 4])
    cbd = dram_in(nc, "cb", [128, KC])
    wgd = dram_in(nc, "wg", [128, 64 * 256])
    gbd = dram_in(nc, "gb", [128, 64])
    lamd = dram_in(nc, "lam", [128, 32])
    if phase == 2:
        cad = dram_in(nc, "CA", [128, 64, 16])
        cbd2 = dram_in(nc, "CB", [128, 64, 16])
        yT = dram_out(nc, "yT", [128, KC, NOUT_RG])
    else:
        So = dram_out(nc, "S", [128, 2, 2, 2, KC])
        Y0o = dram_out(nc, "Y0", [128, KC, NOUT_RG], BF16)
        PFo = dram_out(nc, "PF", [128, KC, NOUT_RG], BF16)
        PRo = dram_out(nc, "PR", [128, KC, NOUT_RG], BF16)

    es = contextlib.ExitStack()
    with es:
        tk = Tk(nc, es)
        H = tk.sbuf("H", [128, KC, W_RG], BF16)
        XCB = tk.sbuf("XCB", [128, KC, W_RG], BF16)
        if phase == 2:
            Y = tk.sbuf("Y", [128, KC, W_RG], BF16)
        else:
            OB = [tk.sbuf(f"OB{i}", [128, 3, W_RG], BF16) for i in range(2)]
            bOB = [tk.buf() for _ in range(2)]
            PC = [tk.sbuf(f"PC{i}", [128, W_RG], F32) for i in range(2)]
            bPC = [tk.buf() for _ in range(2)]
        WG = tk.sbuf("WG", [128, 64 * 256], BF16)
        T = [tk.sbuf(f"T{i}", [128, W_RG], F32) for i in range(NT)]
        XT = [tk.sbuf(f"XT{i}", [128, W_RG], F32) for i in range(4)]
        XB = [tk.sbuf(f"XB{i}", [128, W_RG], F32) for i in range(2)]
        edg = tk.sbuf("edg", [128, 6], F32)
        ml = tk.sbuf("ml", [128, 3, KC], F32)
        mc = tk.sbuf("mc", [128, 3, KC], F32)
        g = tk.sbuf("gs", [128, KC], F32)
        AB = tk.sbuf("AB", [128, 2, KC], F32)
        cw = tk.sbuf("cws", [128, KC, 4], F32)
        cb = tk.sbuf("cbs", [128, KC], F32)
        nb = tk.sbuf("nb", [128, 64], F32)
        c8 = tk.sbuf("c8", [128, 32], F32)
        c16 = tk.sbuf("c16", [128, 32], F32)
        S = tk.sbuf("Ssum", [128, 2, 2, 2, KC], F32)
        H0 = tk.sbuf("H0", [128, 64], F32)
        if phase == 2:
            CA = tk.sbuf("CAs", [128, 64, 16], F32)
            CB = tk.sbuf("CBs", [128, 64, 16], F32)
            CT = tk.sbuf("CT", [128, 16], F32)
        PA = tk.psum("PA", [128, 1536])
        PB = tk.psum("PB", [128, 1536])

        bH, bXCB, bY, bWG, bC, bAB, bS, bH0 = (tk.buf() for _ in range(8))
        bT = [tk.buf() for _ in range(NT)]
        bXT = [tk.buf() for _ in range(4)]
        bXB = [tk.buf() for _ in range(2)]
        bPA, bPB = tk.buf(), tk.buf()
        PS = [(PA, bPA), (PB, bPB)]
        pctr = {"n": 0}

        def next_ps():
            p = PS[pctr["n"] % 2]
            pctr["n"] += 1
            return p

        for (dst, src) in ((edg[:], edge[:, :]), (ml[:], modl[:, :, :]), (mc[:], modc[:, :, :]), (g[:], gd[:, :]),
                           (cw[:], cwd[:, :, :]), (cb[:], cbd[:, :]), (nb[:], gbd[:, :]), (c8[:], lamd[:, :])):
            tk.dma("sp", dst, src, writes=[bC], nowaw=True)
        if phase == 2:
            tk.dma("sp", CA[:], cad[:, :, :], writes=[bC], nowaw=True)
            tk.dma("sp", CB[:], cbd2[:, :, :], writes=[bC], nowaw=True)
        tk.dma("pool", WG[:], wgd[:, :], writes=[bWG])
        lin = Linear(tk, nc, "wall", nslots=2)
        order = [0, 1]
        for c in range(KC):
            if c + 2 < KC:
                order.append(c + 2)
            order.append(16 + c)
        if phase == 2:
            order += list(range(32, 48))
        lin.plan([wall[i, :, :] for i in order])
        ones, bOnes = emit_consts(tk, nc)
        onesb = tk.sbuf("onesb", [128, 128], BF16)
        SQB = [tk.sbuf(f"SQB{i}", [128, W_RG], BF16) for i in range(2)]
        bSQB = [tk.buf(), tk.buf()]
        tk.op("dve", lambda: nc.vector.memset(onesb[:], 1.0), writes=[bOnes], nowaw=True)
        bXCBc_init = []
        tk.op("dve", lambda: nc.vector.memset(XCB[:], 0.0), writes=[bXCB])
        tk.op("dve", lambda: nc.vector.memset(T[9][:], 0.0), writes=[bT[9]])
        ZERO = T[9]
        tk.op("act", lambda: nc.scalar.activation(out=c8[:], in_=c8[:], func=AF.Exp, scale=-1.0), reads=[bC], writes=[bC])
        tk.op("act", lambda: nc.scalar.activation(out=c8[:], in_=c8[:], func=AF.Ln, bias=1.0), reads=[bC], writes=[bC])
        tk.op("dve", lambda: nc.vector.tensor_scalar(out=c16[:], in0=c8[:], scalar1=-2.0 * LRU_C, scalar2=None, op0=ALU.mult),
              reads=[bC], writes=[bC])
        tk.op("dve", lambda: nc.vector.tensor_scalar(out=c8[:], in0=c8[:], scalar1=-LRU_C, scalar2=None, op0=ALU.mult),
              reads=[bC], writes=[bC])

        if phase == 2:
            for idx in range(64):
                tk.op("dve", lambda idx=idx: nc.vector.tensor_tensor_scan(
                    out=CT[:], data0=CA[:, idx, :], data1=CB[:, idx, :], initial=0.0, op0=ALU.mult, op1=ALU.add),
                    reads=[bC], writes=[bH0])
                tk.op("dve", lambda idx=idx: nc.vector.tensor_copy(out=H0[:, idx:idx + 1], in_=CT[:, 15:16]),
                      reads=[bH0], writes=[bH0])

        class XS_:
            n = 0

        def xload(c):
            s = XS_.n % 2
            XS_.n += 1
            tk.dma("sp", XB[s][:], xT[:, c, :], writes=[bXB[s]])
            return XB[s], bXB[s]

        class XView:
            cur = {}

        banks = col_banks(W_RG)
        mods = {"lat": ml, "ctx": mc}
        for i, (nm, _, _) in enumerate(SEGS_RG):
            m = mods[nm]
            tk.op("dve", lambda i=i, m=m: nc.vector.tensor_scalar(
                out=AB[:, i, :], in0=m[:, 1, :], scalar1=1.0, scalar2=None, op0=ALU.add),
                reads=[bC], writes=[bAB], nowaw=True)
            tk.op("dve", lambda i=i: nc.vector.tensor_tensor(
                out=AB[:, i, :], in0=AB[:, i, :], in1=g[:], op=ALU.mult), reads=[bC, bAB], writes=[bAB])
        for c in range(KC):
            xb, bxb = xload(c)
            s = c % 2
            tk.op("act", lambda xb=xb, s=s: nc.scalar.activation(out=SQB[s][:], in_=xb[:], func=AF.Square),
                  reads=[bxb], writes=[bSQB[s]])
            for (b0, b1) in banks:
                tk.op("pe", lambda c=c, s=s, b0=b0, b1=b1: nc.tensor.matmul(
                    PA[:, b0:b1], lhsT=onesb[:], rhs=SQB[s][:, b0:b1], start=(c == 0), stop=(c == KC - 1)),
                    reads=[bSQB[s], bOnes], writes=[bPA], nowaw=(c > 0))
        RS, bRS = T[2], bT[2]
        tk.op("dve", lambda: nc.vector.tensor_scalar(out=RS[:], in0=PA[:, 0:W_RG], scalar1=1.0 / D, scalar2=EPS,
                                                      op0=ALU.mult, op1=ALU.add), reads=[bPA], writes=[bRS])
        tk.op("act", lambda: nc.scalar.activation(out=RS[:], in_=RS[:], func=AF.Sqrt), reads=[bRS], writes=[bRS])
        tk.op("dve", lambda: nc.vector.reciprocal(out=RS[:], in_=RS[:]), reads=[bRS], writes=[bRS])
        for c in range(KC):
            xb, bxb = xload(c)
            s = c % 2
            tk.op("dve", lambda xb=xb, s=s: nc.vector.tensor_tensor(out=T[s][:], in0=xb[:], in1=RS[:], op=ALU.mult),
                  reads=[bxb, bRS], writes=[bT[s]])
            for i, (nm, s0, s1) in enumerate(SEGS_RG):
                m = mods[nm]
                tk.op("act", lambda c=c, s=s, s0=s0, s1=s1, i=i, m=m: nc.scalar.activation(
                    out=H[:, c, s0:s1], in_=T[s][:, s0:s1], func=AF.Identity,
                    bias=m[:, 0, c:c + 1], scale=AB[:, i, c:c + 1]),
                    reads=[bT[s], bAB, bC], writes=[bH], nowaw=True)
        for k, col in enumerate(HALO):
            tk.op("dve", lambda k=k, col=col: nc.vector.tensor_scalar(
                out=H[:, :, col:col + 1], in0=H[:, :, col:col + 1], scalar1=edg[:, k:k + 1], scalar2=None,
                op0=ALU.mult), reads=[bH, bC], writes=[bH])

        def xs_chunk(c):
            PT, bPT = next_ps()
            lin.run(H, bH, W_RG, PT, bPT)
            s = c % 2
            xs, bxs = XT[s], bXT[s]
            xc, bxc = XT[2 + s], bXT[2 + s]
            tk.op("act", lambda PT=PT, xs=xs: nc.scalar.copy(out=xs[:], in_=PT[:, 0:W_RG]), reads=[bPT], writes=[bxs])
            tk.op("dve", lambda c=c, xs=xs, xc=xc: nc.vector.tensor_scalar(
                out=xc[:, 2:W_RG - 1], in0=xs[:, 0:W_RG - 3], scalar1=cw[:, c, 0:1], scalar2=cb[:, c:c + 1],
                op0=ALU.mult, op1=ALU.add), reads=[bxs, bC], writes=[bxc])
            for k in range(1, 4):
                last = (k == 3)
                dst = XCB[:, c, 2:W_RG - 1] if last else xc[:, 2:W_RG - 1]
                tk.op("dve", lambda c=c, xs=xs, xc=xc, k=k, dst=dst: nc.vector.scalar_tensor_tensor(
                    out=dst, in0=xs[:, k:W_RG - 3 + k], scalar=cw[:, c, k:k + 1], in1=xc[:, 2:W_RG - 1],
                    op0=ALU.mult, op1=ALU.add), reads=[bxs, bxc, bC],
                    writes=([bXCBc[c]] if last else [bxc]), nowaw=last)

        bXCBc = [tk.buf() for _ in range(KC)]
        xs_chunk(0)
        xs_chunk(1)

        bOutA = tk.buf()
        if phase == 1:
            for d in range(2):
                tk.op("dve", lambda d=d: nc.vector.memset(PC[d][:], 0.0), writes=[bPC[d]])
        def sigmoid_from_psum(PT, bPT, dst, bdst, bias_ap):
            tk.op("act", lambda: nc.scalar.activation(out=dst[:], in_=PT[:, 0:W_RG], func=AF.Sigmoid, bias=bias_ap),
                  reads=[bPT, bC], writes=[bdst])

        for c in range(KC):
            hb = c // 2
            if c + 2 < KC:
                xs_chunk(c + 2)
            HS = []
            for d in range(2):
                tr, btr = T[3 * d + 0], bT[3 * d + 0]
                ti, bti = T[3 * d + 1], bT[3 * d + 1]
                ta, bta = T[3 * d + 2], bT[3 * d + 2]
                hs, bhs = T[6 + d], bT[6 + d]
                for typ, (dst, bdst) in enumerate(((tr, btr), (ti, bti))):
                    PT, bPT = next_ps()
                    widx = (d * 2 + typ) * 16 + c
                    for ic in range(2):
                        for (b0, b1) in banks:
                            tk.op("pe", lambda ic=ic, b0=b0, b1=b1, PT=PT, widx=widx: nc.tensor.matmul(
                                PT[:, b0:b1], lhsT=WG[:, widx * 256 + ic * 128: widx * 256 + ic * 128 + 128],
                                rhs=XCB[:, hb * 2 + ic, b0:b1], start=(ic == 0), stop=(ic == 1)),
                                reads=[bXCB, bXCBc[hb * 2 + ic], bWG], writes=[bPT], nowaw=(ic > 0))
                    sigmoid_from_psum(PT, bPT, dst, bdst, nb[:, widx:widx + 1])
                tk.op("act", lambda d=d, tr=tr, ta=ta: nc.scalar.activation(
                    out=ta[:], in_=tr[:], func=AF.Exp, scale=c8[:, d * 16 + c: d * 16 + c + 1]),
                    reads=[btr, bC], writes=[bta])
                tk.op("act", lambda d=d, tr=tr: nc.scalar.activation(
                    out=tr[:], in_=tr[:], func=AF.Exp, scale=c16[:, d * 16 + c: d * 16 + c + 1]),
                    reads=[btr, bC], writes=[btr])
                tk.op("act", lambda tr=tr: nc.scalar.activation(out=tr[:], in_=tr[:], func=AF.Ln, scale=-1.0, bias=1.0),
                      reads=[btr], writes=[btr])
                tk.op("act", lambda tr=tr: nc.scalar.activation(out=tr[:], in_=tr[:], func=AF.Exp, scale=0.5),
                      reads=[btr], writes=[btr])
                tk.op("dve", lambda ti=ti: nc.vector.tensor_tensor(out=ti[:], in0=ti[:], in1=XCB[:, c, :], op=ALU.mult),
                      reads=[bti, bXCB, bXCBc[c]], writes=[bti])
                tk.op("dve", lambda ti=ti, tr=tr: nc.vector.tensor_tensor(out=ti[:], in0=ti[:], in1=tr[:], op=ALU.mult),
                      reads=[bti, btr], writes=[bti])
                for si, (v0, v1) in enumerate(VALID):
                    idx = (d * 2 + si) * 16 + c
                    init = 0.0 if phase == 1 else H0[:, idx:idx + 1]
                    o_, a_, g_ = hs[:, v0:v1], ta[:, v0:v1], ti[:, v0:v1]
                    if d == 1:
                        o_, a_, g_ = rev(o_), rev(a_), rev(g_)
                    tk.op("dve", lambda o_=o_, a_=a_, g_=g_, init=init: nc.vector.tensor_tensor_scan(
                        out=o_, data0=a_, data1=g_, initial=init, op0=ALU.mult, op1=ALU.add),
                        reads=[bta, bti, bH0], writes=[bhs], nowaw=(si > 0))
                    if phase == 1:
                        endc = (v1 - 1) if d == 0 else v0
                        tk.op("dve", lambda hs=hs, d=d, si=si, endc=endc: nc.vector.tensor_copy(
                            out=S[:, d, si, 1, c:c + 1], in_=hs[:, endc:endc + 1]), reads=[bhs], writes=[bS], nowaw=True)
                        z_ = ZERO[:, v0:v1]
                        p_ = PC[d][:, v0:v1]
                        if d == 1:
                            z_, p_ = rev(z_), rev(p_)
                        tk.op("dve", lambda p_=p_, a_=a_, z_=z_: nc.vector.tensor_tensor_scan(
                            out=p_, data0=a_, data1=z_, initial=1.0, op0=ALU.mult, op1=ALU.add),
                            reads=[bta, bT[9]], writes=[bPC[d]], nowaw=(si > 0))
                        tk.op("dve", lambda d=d, si=si, endc=endc: nc.vector.tensor_copy(
                            out=S[:, d, si, 0, c:c + 1], in_=PC[d][:, endc:endc + 1]), reads=[bPC[d]], writes=[bS], nowaw=True)
                HS.append((hs, bhs))
            if True:
                PT, bPT = next_ps()
                lin.run(H, bH, W_RG, PT, bPT)
                gt_, bgt = T[8], bT[8]
                t1, bt1 = T[10], bT[10]
                tk.op("act", lambda PT=PT: nc.scalar.copy(out=gt_[:], in_=PT[:, 0:W_RG]), reads=[bPT], writes=[bgt])
                tk.op("dve", lambda: nc.vector.tensor_tensor(out=t1[:], in0=gt_[:], in1=gt_[:], op=ALU.mult),
                      reads=[bgt], writes=[bt1])
                tk.op("dve", lambda: nc.vector.tensor_scalar(out=t1[:], in0=t1[:], scalar1=0.044715, scalar2=1.0,
                                                              op0=ALU.mult, op1=ALU.add), reads=[bt1], writes=[bt1])
                tk.op("dve", lambda: nc.vector.tensor_tensor(out=t1[:], in0=t1[:], in1=gt_[:], op=ALU.mult),
                      reads=[bt1, bgt], writes=[bt1])
                tk.op("act", lambda: nc.scalar.activation(out=t1[:], in_=t1[:], func=AF.Sigmoid, scale=1.5957691216),
                      reads=[bt1], writes=[bt1])
                tk.op("dve", lambda: nc.vector.tensor_tensor(out=gt_[:], in0=gt_[:], in1=t1[:], op=ALU.mult),
                      reads=[bt1, bgt], writes=[bgt])
                (h0_, bh0_), (h1_, bh1_) = HS
                tk.op("dve", lambda: nc.vector.tensor_tensor(out=h0_[:], in0=h0_[:], in1=h1_[:], op=ALU.add),
                      reads=[bh0_, bh1_], writes=[bh0_])
                if phase == 2:
                    tk.op("dve", lambda c=c: nc.vector.tensor_tensor(out=Y[:, c, :], in0=h0_[:], in1=gt_[:], op=ALU.mult),
                          reads=[bh0_, bgt], writes=[bY], nowaw=True)
                else:
                    ob, bob = OB[c % 2], bOB[c % 2]
                    tk.op("pool", lambda ob=ob: nc.gpsimd.tensor_tensor(out=ob[:, 0, :], in0=h0_[:], in1=gt_[:], op=ALU.mult),
                          reads=[bh0_, bgt], writes=[bob])
                    for d in range(2):
                        tk.op("pool", lambda ob=ob, d=d: nc.gpsimd.tensor_tensor(
                            out=ob[:, 1 + d, :], in0=PC[d][:], in1=gt_[:], op=ALU.mult),
                            reads=[bPC[d], bgt], writes=[bob], nowaw=True)
                    for k, dst in enumerate((Y0o, PFo, PRo)):
                        tk.dma("sp", dst[:, c, 0:1024], ob[:, k, 2:1026], reads=[bob], writes=[bOutA], nowaw=True)
                        tk.dma("sp", dst[:, c, 1024:NOUT_RG], ob[:, k, 1029:1061], reads=[bob], writes=[bOutA], nowaw=True)

        bOut = tk.buf()
        if phase == 1:
            tk.dma("sp", So[:, :, :, :, :], S[:], reads=[bS], writes=[bOut])
            tk.finish("sp", [bOutA])
        else:
            for oc in range(KC):
                PT, bPT = next_ps()
                xb, bxb = xload(oc)
                lin.run(Y, bY, W_RG, PT, bPT)
                for (nm, s0, s1) in SEGS_RG:
                    m = ml if nm == "lat" else mc
                    tk.op("dve", lambda oc=oc, s0=s0, s1=s1, m=m, PT=PT, xb=xb: nc.vector.scalar_tensor_tensor(
                        out=xb[:, s0:s1], in0=PT[:, s0:s1], scalar=m[:, 2, oc:oc + 1], in1=xb[:, s0:s1],
                        op0=ALU.mult, op1=ALU.add), reads=[bPT, bxb, bC], writes=[bxb])
                tk.dma("sp", yT[:, oc, 0:1024], xb[:, 2:1026], reads=[bxb], writes=[bOut], nowaw=True)
                tk.dma("sp", yT[:, oc, 1024:NOUT_RG], xb[:, 1029:1061], reads=[bxb], writes=[bOut], nowaw=True)
        tk.finish("sp", [bOut])
    return nc


def rg_xin(x_lat, x_ctx, core):
    cols = np.zeros((W_RG, D), np.float32)
    e = np.zeros((128, 6), np.float32)
    lo, hi = core * 1024, (core + 1) * 1024
    cols[2:1026] = x_lat[lo:hi]
    if core > 0:
        cols[0:2] = x_lat[lo - 2:lo]; e[:, 0:2] = 1
    if core < NCORES - 1:
        cols[1026] = x_lat[hi]; e[:, 2] = 1
    clo, chi = core * NCTX_RG, (core + 1) * NCTX_RG
    cols[1029:1061] = x_ctx[clo:chi]
    if core > 0:
        cols[1027:1029] = x_ctx[clo - 2:clo]; e[:, 3:5] = 1
    if core < NCORES - 1:
        cols[1061] = x_ctx[chi]; e[:, 5] = 1
    return to_fm(cols), e


def rg_weights(w_in, conv_w, conv_b, w_a, b_a, w_i, b_i, lam, w_out):
    wall = np.concatenate([tile_w(w_in[:, D:]), tile_w(w_in[:, :D]), tile_w(w_out)], 0)
    cw = np.ascontiguousarray(conv_w.reshape(4, KC, 128).transpose(2, 1, 0))
    cb = fm(conv_b)
    wg = np.zeros((128, 64, 256), np.float32)
    gb = np.zeros((128, 64), np.float32)
    for d in range(2):
        for typ, (wm, bm) in enumerate(((w_a, b_a), (w_i, b_i))):
            for c in range(KC):
                hb, jc = divmod(c, 2)
                idx = (d * 2 + typ) * 16 + c
                blk = wm[d, hb][:, jc * 128:(jc + 1) * 128]
                wg[:, idx, :] = blk.reshape(2, 128, 128).transpose(1, 0, 2).reshape(128, 256)
                gb[:, idx] = bm[d, c * 128:(c + 1) * 128]
    lamf = np.concatenate([fm(lam[0]), fm(lam[1])], 1)
    return dict(wall=wall, cw=cw, cb=cb, wg=wg.reshape(128, 64 * 256), gb=gb, lam=lamf)


def rg_carries(S_all):
    out = []
    for i in range(NCORES):
        CA = np.ones((128, 2, 2, KC, 16), np.float32)
        CB = np.zeros((128, 2, 2, KC, 16), np.float32)
        chains = {
            (0, 1): [(1, j) for j in range(i)],
            (0, 0): [(1, j) for j in range(NCORES)] + [(0, j) for j in range(i)],
            (1, 1): [(1, j) for j in range(NCORES - 1, i, -1)],
            (1, 0): [(1, j) for j in range(NCORES - 1, -1, -1)] + [(0, j) for j in range(NCORES - 1, i, -1)],
        }
        for (d, seg), ch in chains.items():
            pad = 16 - len(ch)
            for n, (sseg, j) in enumerate(ch):
                CA[:, d, seg, :, pad + n] = S_all[j][:, d, sseg, 0, :]
                CB[:, d, seg, :, pad + n] = S_all[j][:, d, sseg, 1, :]
        CA[..., 0] = 0.0
        out.append((CA.reshape(128, 64, 16), CB.reshape(128, 64, 16)))
    return out


W_B = 1056
SEGS_B = [("lat", 0, 1024), ("ctx", 1024, 1056)]


def build_rgb():
    nc = new_nc()
    Wb = W_B
    y0d = dram_in(nc, "Y0", [128, KC, Wb], BF16)
    pfd = dram_in(nc, "PF", [128, KC, Wb], BF16)
    prd = dram_in(nc, "PR", [128, KC, Wb], BF16)
    cad = dram_in(nc, "CA", [128, 64, 16])
    cbd2 = dram_in(nc, "CB", [128, 64, 16])
    wout = dram_in(nc, "wout", [16, 128, KC * 128])
    xT = dram_in(nc, "xT", [128, KC, Wb])
    modl = dram_in(nc, "modl", [128, 3, KC])
    modc = dram_in(nc, "modc", [128, 3, KC])
    yT = dram_out(nc, "yT", [128, KC, Wb])
    es = contextlib.ExitStack()
    with es:
        tk = Tk(nc, es)
        Y0 = tk.sbuf("Y0s", [128, KC, Wb], BF16)
        PF = tk.sbuf("PFs", [128, KC, Wb], BF16)
        PR = tk.sbuf("PRs", [128, KC, Wb], BF16)
        Y = tk.sbuf("Y", [128, KC, Wb], BF16)
        TT = [tk.sbuf(f"TT{i}", [128, Wb], F32) for i in range(2)]
        XB = [tk.sbuf(f"XB{i}", [128, Wb], F32) for i in range(3)]
        CA = tk.sbuf("CAs", [128, 1024], F32)
        CB = tk.sbuf("CBs", [128, 1024], F32)
        CT = tk.sbuf("CT", [128, 1024], F32)
        H0 = tk.sbuf("H0", [128, 64], F32)
        ml = tk.sbuf("ml", [128, 3, KC], F32)
        mc = tk.sbuf("mc", [128, 3, KC], F32)
        PA = tk.psum("PA", [128, 1536])
        PB = tk.psum("PB", [128, 1536])
        bIn = [tk.buf() for _ in range(KC)]
        bC, bH0, bY = tk.buf(), tk.buf(), tk.buf()
        bTT = [tk.buf() for _ in range(2)]
        bXB = [tk.buf() for _ in range(3)]
        bPA, bPB = tk.buf(), tk.buf()
        tk.dma("sp", CA[:], cad.rearrange("p a b -> p (a b)"), writes=[bC], nowaw=True)
        tk.dma("sp", CB[:], cbd2.rearrange("p a b -> p (a b)"), writes=[bC], nowaw=True)
        tk.dma("sp", ml[:], modl[:, :, :], writes=[bC], nowaw=True)
        tk.dma("sp", mc[:], modc[:, :, :], writes=[bC], nowaw=True)
        for c in range(KC):
            tk.dma("sp", Y0[:, c, :], y0d[:, c, :], writes=[bIn[c]], nowaw=True)
            tk.dma("sp", PF[:, c, :], pfd[:, c, :], writes=[bIn[c]], nowaw=True)
            tk.dma("sp", PR[:, c, :], prd[:, c, :], writes=[bIn[c]], nowaw=True)
        lin = Linear(tk, nc, "wout")
        lin.plan([wout[i, :, :] for i in range(16)])
        tk.op("dve", lambda: nc.vector.tensor_tensor_scan(
            out=CT[:], data0=CA[:], data1=CB[:], initial=0.0, op0=ALU.mult, op1=ALU.add),
            reads=[bC], writes=[bH0])
        ct2 = CT[:, 0:1024]
        last_of_chain = bass.AP(tensor=ct2.tensor, offset=ct2.offset + 15, ap=[list(ct2.ap[0]), [16, 64]])
        tk.op("dve", lambda: nc.vector.tensor_copy(out=H0[:], in_=last_of_chain), reads=[bH0], writes=[bH0])
        for c in range(KC):
            t, bt = TT[c % 2], bTT[c % 2]
            for si, (nm, s0, s1) in enumerate(SEGS_B):
                i_f = (0 * 2 + si) * 16 + c
                i_r = (1 * 2 + si) * 16 + c
                tk.op("dve", lambda c=c, s0=s0, s1=s1, i_f=i_f, t=t: nc.vector.scalar_tensor_tensor(
                    out=t[:, s0:s1], in0=PF[:, c, s0:s1], scalar=H0[:, i_f:i_f + 1], in1=Y0[:, c, s0:s1],
                    op0=ALU.mult, op1=ALU.add), reads=[bIn[c], bH0], writes=[bt], nowaw=(si > 0))
                tk.op("dve", lambda c=c, s0=s0, s1=s1, i_r=i_r, t=t: nc.vector.scalar_tensor_tensor(
                    out=Y[:, c, s0:s1], in0=PR[:, c, s0:s1], scalar=H0[:, i_r:i_r + 1], in1=t[:, s0:s1],
                    op0=ALU.mult, op1=ALU.add), reads=[bIn[c], bH0, bt], writes=[bY], nowaw=True)
        bOut = tk.buf()
        PS = [(PA, bPA), (PB, bPB)]
        for oc in range(KC):
            PT, bPT = PS[oc % 2]
            xs = oc % 3
            tk.dma("sp", XB[xs][:], xT[:, oc, :], writes=[bXB[xs]])
            lin.run(Y, bY, Wb, PT, bPT)
            for (nm, s0, s1) in SEGS_B:
                m = ml if nm == "lat" else mc
                tk.op("dve", lambda oc=oc, s0=s0, s1=s1, m=m, PT=PT, xs=xs: nc.vector.scalar_tensor_tensor(
                    out=XB[xs][:, s0:s1], in0=PT[:, s0:s1], scalar=m[:, 2, oc:oc + 1], in1=XB[xs][:, s0:s1],
                    op0=ALU.mult, op1=ALU.add), reads=[bPT, bXB[xs], bC], writes=[bXB[xs]])
            tk.dma("sp", yT[:, oc, :], XB[xs][:], reads=[bXB[xs]], writes=[bOut], nowaw=True)
        tk.finish("sp", [bOut])
    return nc


NMT = 48


def build_mod():
    nc = new_nc()
    sT = dram_in(nc, "sT", [128, KC, 2])
    wm = dram_in(nc, "wm", [NMT, 128, KC * 128])
    bmd = dram_in(nc, "bm", [128, NMT])
    mo = dram_out(nc, "mo", [128, NMT, 2])
    es = contextlib.ExitStack()
    with es:
        tk = Tk(nc, es)
        S = tk.sbuf("Ssilu", [128, KC, 2], F32)
        BM = tk.sbuf("BM", [128, NMT], F32)
        OUT = tk.sbuf("OUT", [128, NMT, 2], F32)
        WT = [tk.sbuf(f"WT{i}", [128, KC * 128], F32) for i in range(4)]
        ALL = tk.psum("ALL", [128, 4096])
        bS, bBM, bOUT = tk.buf(), tk.buf(), tk.buf()
        bWT = [tk.buf() for _ in range(4)]
        bB = [tk.buf() for _ in range(8)]
        tk.dma("sp", S[:], sT[:, :, :], writes=[bS])
        tk.dma("sp", BM[:], bmd[:, :], writes=[bBM])
        tk.op("act", lambda: nc.scalar.activation(out=S[:], in_=S[:], func=AF.Silu), reads=[bS], writes=[bS])
        for t in range(NMT):
            s = t % 4
            tk.dma("sp", WT[s][:], wm[t, :, :], writes=[bWT[s]])
            b = t % 8
            for c in range(KC):
                tk.op("pe", lambda c=c, s=s, b=b: nc.tensor.matmul(
                    ALL[:, b * 512: b * 512 + 2], lhsT=WT[s][:, c * 128:(c + 1) * 128], rhs=S[:, c, :],
                    start=(c == 0), stop=(c == KC - 1)), reads=[bWT[s], bS], writes=[bB[b]], nowaw=(c > 0))
            tk.op("act", lambda t=t, b=b: nc.scalar.activation(
                out=OUT[:, t, :], in_=ALL[:, b * 512: b * 512 + 2], func=AF.Identity, bias=BM[:, t:t + 1]),
                reads=[bB[b], bBM], writes=[bOUT], nowaw=True)
        bo = tk.buf()
        tk.dma("sp", mo[:, :, :], OUT[:], reads=[bOUT], writes=[bo])
        tk.finish("sp", [bo])
    return nc


def mod_inputs(c, c_ctx, w_mod, b_mod):
    sT = np.ascontiguousarray(np.stack([fm(c.reshape(-1)), fm(c_ctx.reshape(-1))], 2))
    ins = []
    for core in range(NCORES):
        tiles, bias = [], []
        for l in range(4):
            cols = slice(core * 1536, (core + 1) * 1536)
            tiles.append(tile_w(w_mod[l][:, cols]))
            bias.append(b_mod[l][cols].reshape(12, 128).T)
        ins.append(dict(sT=sT, wm=np.concatenate(tiles, 0), bm=np.ascontiguousarray(np.concatenate(bias, 1))))
    return ins


def mod_outputs(results):
    full = np.zeros((4, 2, 12288), np.float32)
    for core, r in enumerate(results):
        mo = r["mo"]
        for l in range(4):
            blk = mo[:, l * 12:(l + 1) * 12, :]
            for s in range(2):
                full[l, s, core * 1536:(core + 1) * 1536] = blk[:, :, s].T.reshape(-1)
    return full.reshape(4, 2, 6, 2048)


def _modtile(m3):
    return np.ascontiguousarray(np.stack([fm(m3[0]), fm(m3[1]), fm(m3[2])], 1))


_PROGS = {}


def _prog(key, fn):
    if key not in _PROGS:
        _PROGS[key] = fn()
    return _PROGS[key]


def kernel(x, c, ctx, c_ctx, w_mod, b_mod, g_mix, g_ffn, fa_w_in, fa_w_out, attn_sink,
           rg_w_in, rg_conv_w, rg_conv_b, rg_w_a, rg_b_a, rg_w_i, rg_b_i, rg_lambda, rg_w_out,
           ffn_w_up, ffn_conv_w, ffn_conv_b, ffn_w_down, g_final):
    f32 = lambda a: np.asarray(a, dtype=np.float32)
    x_lat = f32(x)[0]
    x_ctx = f32(ctx)[0]
    r = run(_prog("mod", build_mod), mod_inputs(f32(c), f32(c_ctx), f32(w_mod), f32(b_mod))).results
    mod = mod_outputs(r)
    seqtab = {}
    out = None
    for layer in range(4):
        i = layer // 2
        mlm, mcm = _modtile(mod[layer, 0, 0:3]), _modtile(mod[layer, 1, 0:3])
        mlf, mcf = _modtile(mod[layer, 0, 3:6]), _modtile(mod[layer, 1, 3:6])
        gm = fm(f32(g_mix)[layer])
        if layer % 2 == 0:
            w_in, w_out = f32(fa_w_in)[i], f32(fa_w_out)[i]
            common1 = dict(modl=mlm, modc=mcm, g=gm, win=tile_w(w_in), Rm=rope_rm(), CS=chan_dft_table())
            in1 = []
            for k in range(NCORES):
                c_, s_ = rope_tables(k)
                in1.append(dict(common1, xT=fa1_xin(x_lat, x_ctx, k), cosT=c_, sinT=s_))
            r1 = run(_prog("fa1", build_fa1), in1).results
            Zl = np.concatenate([q["Z"][:1024] for q in r1], 0)
            Zc = np.concatenate([q["Z"][1024:] for q in r1], 0)
            kT_all = np.concatenate([q["kT"][:, :1024] for q in r1], 1)
            kcT = np.concatenate([q["kT"][:, 1024:] for q in r1], 1)
            vT_all = np.concatenate([q["vT"][:, :1024] for q in r1], 1)
            vcT = np.concatenate([q["vT"][:, 1024:] for q in r1], 1)
            wo_t = tile_w(w_out)
            sk = np.ascontiguousarray(np.tile(f32(attn_sink)[i][None, :], (128, 1)))
            in2 = []
            for k in range(NCORES):
                if k not in seqtab:
                    seqtab[k] = (seq_tables(k), attn_mask(k))
                (Tl, Tc), msk = seqtab[k]
                Kd, Vw = fa2_kv(kT_all, kcT, vT_all, vcT, k)
                in2.append(dict(Zl=Zl, Zc=Zc, Tl=Tl, Tc=Tc, qT=r1[k]["qT"], Kd=Kd, Vw=Vw, mask=msk, sink=sk,
                                wout=wo_t, xT=fa1_xin(x_lat, x_ctx, k), modl=mlm, modc=mcm))
            r2 = run(_prog("fa2", build_fa2), in2).results
            ys = [from_fm(q["yT"]) for q in r2]
            x_lat = np.concatenate([y[:1024] for y in ys], 0)
            x_ctx = np.concatenate([y[1024:] for y in ys], 0)
        else:
            Wt = rg_weights(f32(rg_w_in)[i], f32(rg_conv_w)[i], f32(rg_conv_b)[i], f32(rg_w_a)[i], f32(rg_b_a)[i],
                            f32(rg_w_i)[i], f32(rg_b_i)[i], f32(rg_lambda)[i], f32(rg_w_out)[i])
            common = dict(modl=mlm, modc=mcm, g=gm, **Wt)
            xin = [rg_xin(x_lat, x_ctx, k) for k in range(NCORES)]
            wA = np.ascontiguousarray(Wt["wall"][:32])
            wB = np.ascontiguousarray(Wt["wall"][32:48])
            in1 = [dict(common, wall=wA, xT=xin[k][0], edge=xin[k][1]) for k in range(NCORES)]
            r1 = run(_prog("rgA", lambda: build_rg(1)), in1).results
            car = rg_carries([q["S"] for q in r1])
            in2 = [dict(Y0=r1[k]["Y0"], PF=r1[k]["PF"], PR=r1[k]["PR"], CA=car[k][0], CB=car[k][1], wout=wB,
                        xT=fa1_xin(x_lat, x_ctx, k), modl=mlm, modc=mcm) for k in range(NCORES)]
            r2 = run(_prog("rgB", build_rgb), in2).results
            ys = [from_fm(q["yT"]) for q in r2]
            x_lat = np.concatenate([y[:1024] for y in ys], 0)
            x_ctx = np.concatenate([y[1024:] for y in ys], 0)
        last = (layer == 3)
        nctx = 0 if last else 32
        Wf = ffn_weights(f32(ffn_w_up)[layer], f32(ffn_conv_w)[layer], f32(ffn_conv_b)[layer], f32(ffn_w_down)[layer])
        per = ffn_inputs(x_lat, x_ctx, nctx)
        commonf = dict(modl=mlf, modc=mcf, gf=fm(f32(g_ffn)[layer]), **Wf)
        if last:
            commonf["gfin"] = fm(f32(g_final))
        inf = [dict(commonf, **per[k]) for k in range(NCORES)]
        rf = run(_prog(("ffn", nctx, last), lambda: build_ffn(nctx, final=last)), inf).results
        x_lat, xc_new = ffn_outputs(rf, nctx)
        if not last:
            x_ctx = xc_new
    return np.ascontiguousarray(x_lat.reshape(1, 8192, 2048).astype(np.float32))
```

```python
import contextlib
import numpy as np
import ml_dtypes
import concourse.bass as bass
import concourse.mybir as mybir
from concourse.bass_utils import run_bass_kernel_spmd

F32 = mybir.dt.float32
BF16 = mybir.dt.bfloat16
AF = mybir.ActivationFunctionType
ALU = mybir.AluOpType
EPS = 1e-6
NCORES = 8


class Buf:
    __slots__ = ("w", "r", "pr", "name")

    def __init__(self, name=""):
        self.w = {}
        self.r = {}
        self.pr = {}
        self.name = name


def _merge(dst, src):
    for k, (s, v) in src.items():
        if k not in dst or dst[k][1] < v:
            dst[k] = (s, v)


class Tk:
    NDS = 12

    def __init__(self, nc, es):
        self.nc = nc
        self.es = es
        self.eng = {"pe": nc.tensor, "act": nc.scalar, "dve": nc.vector, "pool": nc.gpsimd, "sp": nc.sync}
        self.sem = {}
        self.cnt = {}
        for e in ("pe", "act", "dve", "pool"):
            self.sem[e] = es.enter_context(nc.semaphore("s_" + e))
            self.cnt[e] = 0
        self.seen = {e: {} for e in self.eng}
        self.dsem = {}
        self.dk = {}
        for q in ("sp", "pool", "act"):
            self.dsem[q] = [es.enter_context(nc.semaphore(f"d_{q}{i}")) for i in range(self.NDS)]
            self.dk[q] = 0
        self.nbuf = 0

    def buf(self, name=""):
        self.nbuf += 1
        return Buf(name or f"b{self.nbuf}")

    def sbuf(self, name, shape, dt):
        return self.es.enter_context(self.nc.sbuf_tensor(name, shape, dt))

    def psum(self, name, shape, dt=F32):
        return self.es.enter_context(self.nc.psum_tensor(name, shape, dt))

    def _wait(self, e, need):
        seen = self.seen[e]
        for k, (s, v) in need.items():
            if seen.get(k, 0) >= v:
                continue
            self.eng[e].wait_ge(s, v)
            seen[k] = v

    def _needs(self, e, reads, writes, nowaw):
        need = {}
        for b in reads:
            _merge(need, b.w)
        for b in writes:
            _merge(need, b.r)
            _merge(need, b.pr)
            if not nowaw:
                _merge(need, b.w)
        if e == "pe":
            need.pop("pe", None)
        return need

    def _mark(self, key, ev, reads, writes):
        for b in reads:
            if key not in b.r or b.r[key][1] < ev[1]:
                b.r[key] = ev
        for b in writes:
            if b.r:
                b.pr = dict(b.r)
                b.r = {}
                b.w = {}
            b.w[key] = ev

    def op(self, e, fn, reads=(), writes=(), nowaw=False):
        self._wait(e, self._needs(e, reads, writes, nowaw))
        ins = fn()
        self.cnt[e] += 1
        ins.then_inc(self.sem[e], 1)
        ev = (self.sem[e], self.cnt[e])
        self._mark(e, ev, reads, writes)
        return ins

    def dma(self, q, out, in_, reads=(), writes=(), nowaw=False):
        need = self._needs(q, reads, writes, nowaw)
        k = self.dk[q]
        slot = k % self.NDS
        gen = k // self.NDS
        s = self.dsem[q][slot]
        key = f"d_{q}{slot}"
        if gen > 0:
            _merge(need, {key: (s, 16 * gen)})
        self._wait(q, need)
        ins = self.eng[q].dma_start(out=out, in_=in_)
        ins.then_inc(s, 16)
        self.dk[q] = k + 1
        ev = (s, 16 * (gen + 1))
        self._mark(key, ev, reads, writes)
        return ins

    def finish(self, q, bufs):
        need = {}
        for b in bufs:
            _merge(need, b.w)
        self._wait(q, need)


def new_nc():
    return bass.Bass("TRN2", target_bir_lowering=False)


def dram_in(nc, name, shape, dt=F32):
    return nc.dram_tensor(name, list(shape), dt, kind="ExternalInput").ap()


def dram_out(nc, name, shape, dt=F32):
    return nc.dram_tensor(name, list(shape), dt, kind="ExternalOutput").ap()


def run(nc, in_maps, trace=False):
    res = run_bass_kernel_spmd(nc, in_maps, core_ids=list(range(len(in_maps))), trace=trace)
    return res


D = 2048
KC = 16


def col_banks(W):
    return [(s, min(s + 512, W)) for s in range(0, W, 512)]


def fm(v):
    return np.ascontiguousarray(np.asarray(v).reshape(-1, 128).T)


def to_fm(x):
    T, F = x.shape
    return np.ascontiguousarray(x.T.reshape(F // 128, 128, T).transpose(1, 0, 2))


def from_fm(y):
    return np.ascontiguousarray(y.transpose(2, 1, 0).reshape(y.shape[2], -1))


def tile_w(w, nsplit=None):
    K, N = w.shape
    t = w.reshape(K // 128, 128, N // 128, 128)
    return np.ascontiguousarray(t.transpose(2, 1, 0, 3)).reshape(N // 128, 128, (K // 128) * 128)


class Common:
    pass


def emit_consts(tk, nc):
    ones = tk.sbuf("ones", [128, 128], F32)
    b = tk.buf("ones")
    tk.op("dve", lambda: nc.vector.memset(ones[:], 1.0), writes=[b])
    return ones, b


def emit_norm_mod(tk, nc, X, bX, H, bH, segs, W, mods, gvec, bC, ones, bOnes, PT, bPT, tmp, btmp, RS, bRS,
                  AB, bAB, hmask=None, sqb=None):
    banks = col_banks(W)
    names = [s[0] for s in segs]
    for i, nm in enumerate(names):
        m = mods[nm]
        tk.op("dve", lambda i=i, m=m: nc.vector.tensor_scalar(
            out=AB[:, i, :], in0=m[:, 1, :], scalar1=1.0, scalar2=None, op0=ALU.add),
            reads=[bC], writes=[bAB], nowaw=True)
        tk.op("dve", lambda i=i: nc.vector.tensor_tensor(
            out=AB[:, i, :], in0=AB[:, i, :], in1=gvec[:], op=ALU.mult),
            reads=[bC, bAB], writes=[bAB])
    sq_t, sq_b, sq_ones = (tmp, btmp, ones) if sqb is None else sqb
    for c in range(KC):
        s = c % 2
        tk.op("act", lambda c=c, s=s: nc.scalar.activation(out=sq_t[s][:, 0:W], in_=X[:, c, :], func=AF.Square),
              reads=[bX[c]], writes=[sq_b[s]])
        for (b0, b1) in banks:
            tk.op("pe", lambda c=c, s=s, b0=b0, b1=b1: nc.tensor.matmul(
                PT[:, b0:b1], lhsT=sq_ones[:], rhs=sq_t[s][:, b0:b1], start=(c == 0), stop=(c == KC - 1)),
                reads=[sq_b[s], bOnes], writes=[bPT], nowaw=(c > 0))
    tk.op("dve", lambda: nc.vector.tensor_scalar(out=RS[:, 0:W], in0=PT[:, 0:W], scalar1=1.0 / D, scalar2=EPS,
                                                  op0=ALU.mult, op1=ALU.add), reads=[bPT], writes=[bRS])
    tk.op("act", lambda: nc.scalar.activation(out=RS[:, 0:W], in_=RS[:, 0:W], func=AF.Sqrt), reads=[bRS], writes=[bRS])
    tk.op("dve", lambda: nc.vector.reciprocal(out=RS[:, 0:W], in_=RS[:, 0:W]), reads=[bRS], writes=[bRS])
    for c in range(KC):
        s = c % 2
        tk.op("dve", lambda c=c, s=s: nc.vector.tensor_tensor(out=tmp[s][:, 0:W], in0=X[:, c, :], in1=RS[:, 0:W],
                                                              op=ALU.mult),
              reads=[bX[c], bRS], writes=[btmp[s]])
        for i, (nm, s0, s1) in enumerate(segs):
            m = mods[nm]
            tk.op("act", lambda c=c, s=s, s0=s0, s1=s1, i=i, m=m: nc.scalar.activation(
                out=H[:, c, s0:s1], in_=tmp[s][:, s0:s1], func=AF.Identity,
                bias=m[:, 0, c:c + 1], scale=AB[:, i, c:c + 1]),
                reads=[btmp[s], bAB, bC], writes=[bH], nowaw=True)
    if hmask:
        edg, cols = hmask
        for k, col in enumerate(cols):
            tk.op("dve", lambda k=k, col=col: nc.vector.tensor_scalar(
                out=H[:, :, col:col + 1], in0=H[:, :, col:col + 1], scalar1=edg[:, k:k + 1], scalar2=None,
                op0=ALU.mult), reads=[bH, bC], writes=[bH])


class Linear:
    def __init__(self, tk, nc, name, nslots=3, kc=KC):
        self.tk, self.nc = tk, nc
        self.kc = kc
        self.WT = [tk.sbuf(f"{name}_w{i}", [128, kc * 128], BF16) for i in range(nslots)]
        self.bW = [tk.buf() for _ in range(nslots)]
        self.n = 0
        self.loaded = 0
        self.queue = []

    def plan(self, tiles):
        self.queue = list(tiles)
        for _ in range(len(self.WT) - 1):
            self._load()

    def _load(self):
        if self.loaded < len(self.queue):
            i = self.loaded
            s = i % len(self.WT)
            self.tk.dma("pool", self.WT[s][:], self.queue[i], writes=[self.bW[s]])
            self.loaded += 1

    def run(self, H, bH, W, PT, bPT, hreads=()):
        tk, nc = self.tk, self.nc
        bPTs = list(bPT) if isinstance(bPT, (list, tuple)) else [bPT]
        self._load()
        i = self.n
        s = i % len(self.WT)
        self.n += 1
        wt = self.WT[s]
        for c in range(self.kc):
            for (b0, b1) in col_banks(W):
                tk.op("pe", lambda c=c, b0=b0, b1=b1: nc.tensor.matmul(
                    PT[:, b0:b1], lhsT=wt[:, c * 128:(c + 1) * 128], rhs=H[:, c, b0:b1],
                    start=(c == 0), stop=(c == self.kc - 1)),
                    reads=[bH, self.bW[s]] + list(hreads), writes=bPTs, nowaw=(c > 0))


D = 2048
KC = 16
DFF = 5632
NP = 44
GP = 4
NG = NP // GP


def ffn_layout(NCTX):
    segs = [("lat", 0, 1026)]
    W = 1026
    if NCTX:
        segs.append(("ctx", 1026, 1026 + NCTX + 2))
        W += NCTX + 2
    return segs, W


def build_ffn(NCTX, final=False):
    segs, W = ffn_layout(NCTX)
    banks = [(s, min(s + 512, W)) for s in range(0, W, 512)]
    NOUT = 1024 + NCTX
    nc = new_nc()
    xT = dram_in(nc, "xT", [128, KC, W])
    edge = dram_in(nc, "edge", [128, 4])
    modl = dram_in(nc, "modl", [128, 3, KC])
    modc = dram_in(nc, "modc", [128, 3, KC])
    gfd = dram_in(nc, "gf", [128, KC])
    wup = dram_in(nc, "wup", [NP, 128, KC * 256])
    wdn = dram_in(nc, "wdn", [NG, 128, GP * D])
    cwd = dram_in(nc, "cw", [128, 2 * NP, 3])
    cbd = dram_in(nc, "cb", [128, 2 * NP])
    yT = dram_out(nc, "yT", [128, KC, NOUT])
    if final:
        gfind = dram_in(nc, "gfin", [128, KC])

    es = contextlib.ExitStack()
    with es:
        tk = Tk(nc, es)
        X = tk.sbuf("X", [128, KC, W], F32)
        H = tk.sbuf("H", [128, KC, W], BF16)
        Abuf = [tk.sbuf(f"A{i}", [128, GP, W], BF16) for i in range(2)]
        WU = [tk.sbuf(f"WU{i}", [128, KC * 256], BF16) for i in range(3)]
        WD = [tk.sbuf(f"WD{i}", [128, GP * D], BF16) for i in range(2)]
        UG = [tk.sbuf(f"UG{i}", [128, W], F32) for i in range(2)]
        UV = [tk.sbuf(f"UV{i}", [128, W], F32) for i in range(2)]
        TG = tk.sbuf("TG", [128, W], F32)
        TV = tk.sbuf("TV", [128, W], F32)
        SQ = [TG, TV]
        RS = UG[0]
        ones = tk.sbuf("ones", [128, 128], F32)
        onesb = tk.sbuf("onesb", [128, 128], BF16)
        SQB = [tk.sbuf(f"SQB{i}", [128, W], BF16) for i in range(2)]
        edg = tk.sbuf("edg", [128, 4], F32)
        ml = tk.sbuf("ml", [128, 3, KC], F32)
        mc = tk.sbuf("mc", [128, 3, KC], F32)
        gf = tk.sbuf("gfs", [128, KC], F32)
        AB = tk.sbuf("AB", [128, 2, KC], F32)
        cw = tk.sbuf("cws", [128, 2 * NP, 3], F32)
        cb = tk.sbuf("cbs", [128, 2 * NP], F32)
        PG = tk.psum("PG", [128, 1536])
        PV = tk.psum("PV", [128, 1536])
        PD = [tk.psum(f"PD{i}", [128, 512]) for i in range(2)]

        bX = [tk.buf(f"X{c}") for c in range(KC)]
        bH = tk.buf("H")
        bA = [tk.buf() for _ in range(2)]
        bWU = [tk.buf() for _ in range(3)]
        bWD = [tk.buf() for _ in range(2)]
        bUG = [tk.buf() for _ in range(2)]
        bUV = [tk.buf() for _ in range(2)]
        bTG, bTV = tk.buf(), tk.buf()
        bRS = bUG[0]
        bSQ = [bTG, bTV]
        bPG, bPV = tk.buf(), tk.buf()
        bPD = [tk.buf() for _ in range(2)]
        bC = tk.buf("consts")

        for c in range(KC):
            tk.dma("sp", X[:, c, :], xT[:, c, :], writes=[bX[c]])
        tk.dma("sp", edg[:], edge[:, :], writes=[bC], nowaw=True)
        tk.dma("sp", ml[:], modl[:, :, :], writes=[bC], nowaw=True)
        tk.dma("sp", mc[:], modc[:, :, :], writes=[bC], nowaw=True)
        tk.dma("sp", gf[:], gfd[:, :], writes=[bC], nowaw=True)
        tk.dma("sp", cw[:], cwd[:, :, :], writes=[bC], nowaw=True)
        tk.dma("sp", cb[:], cbd[:, :], writes=[bC], nowaw=True)

        def load_wu(p):
            tk.dma("pool", WU[p % 3][:], wup[p, :, :], reads=(bX if p < 2 else []), writes=[bWU[p % 3]])

        def load_wd(g):
            tk.dma("pool", WD[g % 2][:], wdn[g, :, :], reads=(bX if g == 0 else []), writes=[bWD[g % 2]])

        load_wu(0)
        load_wu(1)
        load_wd(0)

        tk.op("dve", lambda: nc.vector.memset(ones[:], 1.0), writes=[bC], nowaw=True)
        tk.op("dve", lambda: nc.vector.memset(onesb[:], 1.0), writes=[bC], nowaw=True)
        bSQB = [tk.buf(), tk.buf()]
        for i in range(2):
            tk.op("dve", lambda i=i: nc.vector.memset(Abuf[i][:], 0.0), writes=[bA[i]])
        bAB = tk.buf()
        for i, m in enumerate((ml, mc)):
            tk.op("dve", lambda i=i, m=m: nc.vector.tensor_scalar(
                out=AB[:, i, :], in0=m[:, 1, :], scalar1=1.0, scalar2=None, op0=ALU.add),
                reads=[bC], writes=[bAB], nowaw=True)
            tk.op("dve", lambda i=i: nc.vector.tensor_tensor(
                out=AB[:, i, :], in0=AB[:, i, :], in1=gf[:], op=ALU.mult),
                reads=[bC, bAB], writes=[bAB])

        for c in range(KC):
            s = c % 2
            tk.op("act", lambda c=c, s=s: nc.scalar.activation(out=SQB[s][:], in_=X[:, c, :], func=AF.Square),
                  reads=[bX[c]], writes=[bSQB[s]])
            for (b0, b1) in banks:
                tk.op("pe", lambda c=c, s=s, b0=b0, b1=b1: nc.tensor.matmul(
                    PG[:, b0:b1], lhsT=onesb[:], rhs=SQB[s][:, b0:b1], start=(c == 0), stop=(c == KC - 1)),
                    reads=[bSQB[s], bC], writes=[bPG], nowaw=(c > 0))
        tk.op("dve", lambda: nc.vector.tensor_scalar(out=RS[:], in0=PG[:, 0:W], scalar1=1.0 / D, scalar2=EPS,
                                                      op0=ALU.mult, op1=ALU.add), reads=[bPG], writes=[bRS])
        tk.op("act", lambda: nc.scalar.activation(out=RS[:], in_=RS[:], func=AF.Sqrt), reads=[bRS], writes=[bRS])
        tk.op("dve", lambda: nc.vector.reciprocal(out=RS[:], in_=RS[:]), reads=[bRS], writes=[bRS])
        for c in range(KC):
            s = c % 2
            tk.op("dve", lambda c=c, s=s: nc.vector.tensor_tensor(out=SQ[s][:], in0=X[:, c, :], in1=RS[:], op=ALU.mult),
                  reads=[bX[c], bRS], writes=[bSQ[s]])
            for (nm, s0, s1) in segs:
                i = 0 if nm == "lat" else 1
                m = ml if nm == "lat" else mc
                tk.op("act", lambda c=c, s=s, s0=s0, s1=s1, i=i, m=m: nc.scalar.activation(
                    out=H[:, c, s0:s1], in_=SQ[s][:, s0:s1], func=AF.Identity,
                    bias=m[:, 0, c:c + 1], scale=AB[:, i, c:c + 1]),
                    reads=[bSQ[s], bAB, bC], writes=[bH], nowaw=True)
        hcols = [0, 1025] + ([1026, W - 1] if NCTX else [])
        for k, col in enumerate(hcols):
            tk.op("dve", lambda k=k, col=col: nc.vector.tensor_scalar(
                out=H[:, :, col:col + 1], in0=H[:, :, col:col + 1], scalar1=edg[:, k:k + 1], scalar2=None,
                op0=ALU.mult), reads=[bH, bC], writes=[bH])

        def up_pair(p):
            g, j = divmod(p, GP)
            wu = WU[p % 3]
            s = p % 2
            for half, (PT, bPT, U, bU) in enumerate(((PG, bPG, UG[s], bUG[s]), (PV, bPV, UV[s], bUV[s]))):
                for c in range(KC):
                    for (b0, b1) in banks:
                        tk.op("pe", lambda c=c, b0=b0, b1=b1, PT=PT, half=half: nc.tensor.matmul(
                            PT[:, b0:b1], lhsT=wu[:, c * 256 + half * 128: c * 256 + half * 128 + 128],
                            rhs=H[:, c, b0:b1], start=(c == 0), stop=(c == KC - 1)),
                            reads=[bH, bWU[p % 3]], writes=[bPT], nowaw=(c > 0))
                tk.op("act", lambda PT=PT, U=U: nc.scalar.copy(out=U[:], in_=PT[:, 0:W]), reads=[bPT], writes=[bU])
            deferred = []
            for half, (U, bU, T, bT) in enumerate(((UG[s], bUG[s], TG, bTG), (UV[s], bUV[s], TV, bTV))):
                ch = p if half == 0 else NP + p
                deferred.append(lambda U=U, T=T, ch=ch, bU=bU, bT=bT: tk.op("dve", lambda: nc.vector.tensor_scalar(
                    out=T[:, 1:W - 1], in0=U[:, 1:W - 1], scalar1=cw[:, ch, 1:2], scalar2=cb[:, ch:ch + 1],
                    op0=ALU.mult, op1=ALU.add), reads=[bU, bC], writes=[bT]))
                deferred.append(lambda U=U, T=T, ch=ch, bU=bU, bT=bT: tk.op("dve", lambda: nc.vector.scalar_tensor_tensor(
                    out=T[:, 1:W - 1], in0=U[:, 0:W - 2], scalar=cw[:, ch, 0:1], in1=T[:, 1:W - 1],
                    op0=ALU.mult, op1=ALU.add), reads=[bU, bT, bC], writes=[bT]))
                deferred.append(lambda U=U, T=T, ch=ch, bU=bU, bT=bT: tk.op("dve", lambda: nc.vector.scalar_tensor_tensor(
                    out=T[:, 1:W - 1], in0=U[:, 2:W], scalar=cw[:, ch, 2:3], in1=T[:, 1:W - 1],
                    op0=ALU.mult, op1=ALU.add), reads=[bU, bT, bC], writes=[bT]))
            deferred.append(lambda: tk.op("act", lambda: nc.scalar.activation(
                out=TG[:, 1:W - 1], in_=TG[:, 1:W - 1], func=AF.Silu), reads=[bTG], writes=[bTG]))
            deferred.append(lambda g=g, j=j: tk.op("dve", lambda: nc.vector.tensor_tensor(
                out=Abuf[g % 2][:, j, 1:W - 1], in0=TG[:, 1:W - 1], in1=TV[:, 1:W - 1], op=ALU.mult),
                reads=[bTG, bTV], writes=[bA[g % 2]], nowaw=(j > 0)))
            return deferred

        dstate = {"n": 0}

        dbanks = [(1, 513), (513, 1025)]

        def down_part(g, part, deferred):
            items = [(bi, d) for bi in range(len(dbanks)) for d in range(KC)]
            per = (len(items) + GP - 1) // GP
            for (bi, d) in items[part * per:(part + 1) * per]:
                b0, b1 = dbanks[bi]
                k = dstate["n"] % 2
                dstate["n"] += 1
                for f in range(GP):
                    tk.op("pe", lambda f=f, d=d, b0=b0, b1=b1, k=k: nc.tensor.matmul(
                        PD[k][:, 0:b1 - b0], lhsT=WD[g % 2][:, f * D + d * 128: f * D + d * 128 + 128],
                        rhs=Abuf[g % 2][:, f, b0:b1], start=(f == 0), stop=(f == GP - 1)),
                        reads=[bA[g % 2], bWD[g % 2]], writes=[bPD[k]], nowaw=(f > 0))
                tk.op("dve", lambda d=d, b0=b0, b1=b1, k=k: nc.vector.scalar_tensor_tensor(
                    out=X[:, d, b0:b1], in0=PD[k][:, 0:b1 - b0], scalar=ml[:, 2, d:d + 1],
                    in1=X[:, d, b0:b1], op0=ALU.mult, op1=ALU.add),
                    reads=[bPD[k], bC, bX[d]], writes=[bX[d]])
                if deferred:
                    deferred.pop(0)()
            if NCTX and part == GP - 1:
                c0, c1 = 1027, 1027 + NCTX
                k = dstate["n"] % 2
                dstate["n"] += 1
                for d in range(KC):
                    for f in range(GP):
                        tk.op("pe", lambda f=f, d=d, k=k: nc.tensor.matmul(
                            PD[k][:, d * NCTX:(d + 1) * NCTX], lhsT=WD[g % 2][:, f * D + d * 128: f * D + d * 128 + 128],
                            rhs=Abuf[g % 2][:, f, c0:c1], start=(f == 0), stop=(f == GP - 1)),
                            reads=[bA[g % 2], bWD[g % 2]], writes=[bPD[k]], nowaw=not (d == 0 and f == 0))
                for d in range(KC):
                    tk.op("dve", lambda d=d, k=k: nc.vector.scalar_tensor_tensor(
                        out=X[:, d, c0:c1], in0=PD[k][:, d * NCTX:(d + 1) * NCTX], scalar=mc[:, 2, d:d + 1],
                        in1=X[:, d, c0:c1], op0=ALU.mult, op1=ALU.add),
                        reads=[bPD[k], bC, bX[d]], writes=[bX[d]])
                    if deferred:
                        deferred.pop(0)()

        prev_def = []
        for g in range(NG + 1):
            if 1 <= g < NG:
                load_wd(g)
            for j in range(GP):
                p = g * GP + j
                while prev_def:
                    prev_def.pop(0)()
                if g < NG:
                    if p + 2 < NP:
                        load_wu(p + 2)
                    prev_def = up_pair(p)
                if g >= 1:
                    down_part(g - 1, j, [])

        bOut = tk.buf()
        if final:
            gfin = tk.sbuf("gfins", [128, KC], F32)
            bGF = tk.buf()
            tk.dma("sp", gfin[:], gfind[:, :], writes=[bGF])
            for c in range(KC):
                s = c % 2
                tk.op("act", lambda c=c, s=s: nc.scalar.activation(out=SQB[s][:], in_=X[:, c, :], func=AF.Square),
                      reads=[bX[c]], writes=[bSQB[s]])
                for (b0, b1) in banks:
                    tk.op("pe", lambda c=c, s=s, b0=b0, b1=b1: nc.tensor.matmul(
                        PG[:, b0:b1], lhsT=onesb[:], rhs=SQB[s][:, b0:b1], start=(c == 0), stop=(c == KC - 1)),
                        reads=[bSQB[s], bC], writes=[bPG], nowaw=(c > 0))
            tk.op("dve", lambda: nc.vector.tensor_scalar(out=RS[:], in0=PG[:, 0:W], scalar1=1.0 / D, scalar2=EPS,
                                                          op0=ALU.mult, op1=ALU.add), reads=[bPG], writes=[bRS])
            tk.op("act", lambda: nc.scalar.activation(out=RS[:], in_=RS[:], func=AF.Sqrt), reads=[bRS], writes=[bRS])
            tk.op("dve", lambda: nc.vector.reciprocal(out=RS[:], in_=RS[:]), reads=[bRS], writes=[bRS])
            for c in range(KC):
                s = c % 2
                tk.op("dve", lambda c=c, s=s: nc.vector.tensor_tensor(out=SQ[s][:], in0=X[:, c, :], in1=RS[:], op=ALU.mult),
                      reads=[bX[c], bRS], writes=[bSQ[s]])
                tk.op("act", lambda c=c, s=s: nc.scalar.activation(out=SQ[s][:], in_=SQ[s][:], func=AF.Identity,
                                                                   scale=gfin[:, c:c + 1]),
                      reads=[bSQ[s], bGF], writes=[bSQ[s]])
                tk.dma("sp", yT[:, c, 0:1024], SQ[s][:, 1:1025], reads=[bSQ[s]], writes=[bOut], nowaw=True)
        for c in range(KC if not final else 0):
            tk.dma("sp", yT[:, c, 0:1024], X[:, c, 1:1025], reads=[bX[c]], writes=[bOut], nowaw=True)
            if NCTX:
                tk.dma("sp", yT[:, c, 1024:1024 + NCTX], X[:, c, 1027:1027 + NCTX], reads=[bX[c]], writes=[bOut],
                       nowaw=True)
        tk.finish("sp", [bOut])
    return nc


def fm(v):
    return np.ascontiguousarray(v.reshape(KC, 128).T)


def ffn_weights(w_up, conv_w, conv_b, w_down):
    wu = w_up.reshape(KC, 128, 2, NP, 128)
    wu = np.ascontiguousarray(wu.transpose(3, 1, 0, 2, 4)).reshape(NP, 128, KC * 256)
    wd = w_down.reshape(NG, GP, 128, D)
    wd = np.ascontiguousarray(wd.transpose(0, 2, 1, 3)).reshape(NG, 128, GP * D)
    cw = np.ascontiguousarray(conv_w.reshape(3, 2 * NP, 128).transpose(2, 1, 0))
    cb = np.ascontiguousarray(conv_b.reshape(2 * NP, 128).T)
    return dict(wup=wu, wdn=wd, cw=cw, cb=cb)


def ffn_inputs(x_lat, x_ctx, NCTX):
    segs, W = ffn_layout(NCTX)
    outs = []
    for i in range(NCORES):
        cols = np.zeros((W, D), np.float32)
        lo, hi = i * 1024, (i + 1) * 1024
        cols[1:1025] = x_lat[lo:hi]
        e = np.zeros((128, 4), np.float32)
        if i > 0:
            cols[0] = x_lat[lo - 1]; e[:, 0] = 1
        if i < NCORES - 1:
            cols[1025] = x_lat[hi]; e[:, 1] = 1
        if NCTX:
            clo, chi = i * NCTX, (i + 1) * NCTX
            cols[1027:1027 + NCTX] = x_ctx[clo:chi]
            if i > 0:
                cols[1026] = x_ctx[clo - 1]; e[:, 2] = 1
            if i < NCORES - 1:
                cols[W - 1] = x_ctx[chi]; e[:, 3] = 1
        xT = np.ascontiguousarray(cols.T.reshape(KC, 128, W).transpose(1, 0, 2))
        outs.append(dict(xT=xT, edge=e))
    return outs


def ffn_outputs(results, NCTX):
    lat, ctx = [], []
    for r in results:
        y = r["yT"]
        t = y.transpose(2, 1, 0).reshape(y.shape[2], D)
        lat.append(t[:1024])
        if NCTX:
            ctx.append(t[1024:])
    return np.concatenate(lat, 0), (np.concatenate(ctx, 0) if NCTX else None)


W1 = 1056
NLAT = 1024
NCTX_FA = 32
SEGS1 = [("lat", 0, 1024), ("ctx", 1024, 1056)]
NOC = 18


def build_fa1(stage=9):
    W = W1
    nc = new_nc()
    xT = dram_in(nc, "xT", [128, KC, W])
    modl = dram_in(nc, "modl", [128, 3, KC])
    modc = dram_in(nc, "modc", [128, 3, KC])
    gd = dram_in(nc, "g", [128, KC])
    win = dram_in(nc, "win", [NOC, 128, KC * 128])
    cosd = dram_in(nc, "cosT", [128, NLAT])
    sind = dram_in(nc, "sinT", [128, NLAT])
    rmd = dram_in(nc, "Rm", [128, 128])
    csd = dram_in(nc, "CS", [128, 2 * 512])
    Zo = dram_out(nc, "Z", [W, 2048], BF16)
    qo = dram_out(nc, "qT", [128, 8, W], BF16)
    ko = dram_out(nc, "kT", [128, W], BF16)
    vo = dram_out(nc, "vT", [128, W], BF16)

    es = contextlib.ExitStack()
    with es:
        tk = Tk(nc, es)
        X = tk.sbuf("X", [128, KC, W], F32)
        H = tk.sbuf("H", [128, KC, W], BF16)
        FT = tk.sbuf("FT", [128, 8, W], BF16)
        tmp = [tk.sbuf(f"tmp{i}", [128, W], F32) for i in range(2)]
        RS = tk.sbuf("RS", [128, W], F32)
        ml = tk.sbuf("ml", [128, 3, KC], F32)
        mc = tk.sbuf("mc", [128, 3, KC], F32)
        g = tk.sbuf("gs", [128, KC], F32)
        AB = tk.sbuf("AB", [128, 2, KC], F32)
        COS = tk.sbuf("COS", [128, NLAT], F32)
        SIN = tk.sbuf("SIN", [128, NLAT], F32)
        RM = tk.sbuf("RM", [128, 128], BF16)
        CS = tk.sbuf("CSs", [128, 2 * 512], BF16)
        QR = [tk.sbuf(f"QR{i}", [128, NLAT], BF16) for i in range(2)]
        QO = [tk.sbuf(f"QO{i}", [128, W], BF16) for i in range(2)]
        ZT = [tk.sbuf(f"ZT{i}", [128, 2048], BF16) for i in range(2)]
        PA = tk.psum("PA", [128, 1536])
        PB = tk.psum("PB", [128, 1536])
        PR = tk.psum("PR", [128, 1024])

        bX = [tk.buf() for _ in range(KC)]
        bH, bFT, bRS, bAB, bC = tk.buf(), tk.buf(), tk.buf(), tk.buf(), tk.buf()
        btmp = [tk.buf() for _ in range(2)]
        bQR = [tk.buf() for _ in range(2)]
        bQO = [tk.buf() for _ in range(2)]
        bZT = [tk.buf() for _ in range(2)]
        bPA, bPB = tk.buf(), tk.buf()
        bPR = [tk.buf(), tk.buf()]

        for c in range(KC):
            tk.dma("sp", X[:, c, :], xT[:, c, :], writes=[bX[c]])
        for (dst, src) in ((ml[:], modl[:, :, :]), (mc[:], modc[:, :, :]), (g[:], gd[:, :]),
                           (COS[:], cosd[:, :]), (SIN[:], sind[:, :])):
            tk.dma("sp", dst, src, writes=[bC], nowaw=True)
        tk.dma("pool", RM[:], rmd[:, :], writes=[bC], nowaw=True)
        tk.dma("pool", CS[:], csd[:, :], writes=[bC], nowaw=True)
        lin = Linear(tk, nc, "win")
        lin.plan([win[i, :, :] for i in range(NOC)])
        ones, bOnes = emit_consts(tk, nc)
        onesb = tk.sbuf("onesb", [128, 128], BF16)
        SQB = [tk.sbuf(f"SQB{i}", [128, W], BF16) for i in range(2)]
        bSQB = [tk.buf(), tk.buf()]
        tk.op("dve", lambda: nc.vector.memset(onesb[:], 1.0), writes=[bOnes], nowaw=True)

        emit_norm_mod(tk, nc, X, bX, H, bH, SEGS1, W, {"lat": ml, "ctx": mc}, g, bC, ones, bOnes, PA, bPA,
                      tmp, btmp, RS, bRS, AB, bAB, sqb=(SQB, bSQB, onesb))

        outbuf = tk.buf()
        PTs = [(PA, bPA), (PB, bPB)]
        for oc in range(NOC):
            if stage <= 2 and oc >= 8:
                break
            if stage in (3, 31, 32, 33) and oc >= 9:
                break
            if stage == 4 and oc >= 17:
                break
            PT, bPT = PTs[oc % 2]
            lin.run(H, bH, W, PT, bPT)
            if oc < 8:
                tk.op("act", lambda oc=oc, PT=PT: nc.scalar.copy(out=FT[:, oc, :], in_=PT[:, 0:W]),
                      reads=[bPT], writes=[bFT], nowaw=True)
                if oc == 7 and stage >= 2:
                    tiles = [(t * 128, 128, 0) for t in range(8)] + [(W - 128, 128, 128 - NCTX_FA)]
                    for ti, (t0, nt, r0) in enumerate(tiles):
                        zt, bzt = ZT[ti % 2], bZT[ti % 2]
                        for gi in range(4):
                            hb = gi % 2
                            for cc in range(2):
                                tk.op("pe", lambda gi=gi, cc=cc, t0=t0, nt=nt, hb=hb: nc.tensor.matmul(
                                    PR[0:nt, hb * 512:(hb + 1) * 512], lhsT=FT[:, 2 * gi + cc, t0:t0 + nt],
                                    rhs=CS[:, cc * 512:(cc + 1) * 512], start=(cc == 0), stop=(cc == 1)),
                                    reads=[bFT, bC], writes=[bPR[hb]], nowaw=(cc > 0))
                            tk.op("act", lambda gi=gi, nt=nt, hb=hb, zt=zt: nc.scalar.copy(
                                out=zt[0:nt, gi * 512:(gi + 1) * 512], in_=PR[0:nt, hb * 512:(hb + 1) * 512]),
                                reads=[bPR[hb]], writes=[bzt], nowaw=(gi > 0))
                        tk.dma("sp", Zo[t0 + r0:t0 + nt, :], zt[r0:nt, :], reads=[bzt], writes=[outbuf], nowaw=True)
            elif oc < 17:
                s = oc % 2
                qr, bqr, qo_, bqo = QR[s], bQR[s], QO[s], bQO[s]
                tk.op("act", lambda qr=qr, PT=PT: nc.scalar.copy(out=qr[:], in_=PT[:, 0:NLAT]),
                      reads=[bPT], writes=[bqr])
                for hb in range(2):
                    tk.op("pe", lambda hb=hb, qr=qr: nc.tensor.matmul(
                        PR[:, hb * 512:(hb + 1) * 512], lhsT=RM[:], rhs=qr[:, hb * 512:(hb + 1) * 512],
                        start=True, stop=True), reads=[bqr, bC], writes=[bPR[hb]])
                if stage == 31:
                    continue
                tk.op("act", lambda PT=PT: nc.scalar.copy(out=tmp[0][:, 0:NLAT], in_=PT[:, 0:NLAT]),
                      reads=[bPT], writes=[btmp[0]])
                tk.op("act", lambda: nc.scalar.copy(out=tmp[1][:, 0:NLAT], in_=PR[:, 0:NLAT]),
                      reads=[bPR[0], bPR[1]], writes=[btmp[1]])
                tk.op("dve", lambda: nc.vector.tensor_tensor(
                    out=tmp[0][:, 0:NLAT], in0=tmp[0][:, 0:NLAT], in1=COS[:], op=ALU.mult),
                    reads=[btmp[0], bC], writes=[btmp[0]])
                tk.op("dve", lambda: nc.vector.tensor_tensor(
                    out=tmp[1][:, 0:NLAT], in0=tmp[1][:, 0:NLAT], in1=SIN[:], op=ALU.mult),
                    reads=[btmp[1], bC], writes=[btmp[1]])
                tk.op("dve", lambda qo_=qo_: nc.vector.tensor_tensor(
                    out=qo_[:, 0:NLAT], in0=tmp[0][:, 0:NLAT], in1=tmp[1][:, 0:NLAT], op=ALU.add),
                    reads=[btmp[0], btmp[1]], writes=[bqo])
                if stage == 32:
                    continue
                tk.op("act", lambda qo_=qo_, PT=PT: nc.scalar.copy(out=qo_[:, NLAT:W], in_=PT[:, NLAT:W]),
                      reads=[bPT], writes=[bqo], nowaw=True)
                if stage == 33:
                    continue
                dst = qo[:, oc - 8, :] if oc < 16 else ko[:, :]
                tk.dma("sp", dst, qo_[:], reads=[bqo], writes=[outbuf], nowaw=True)
            else:
                s = oc % 2
                qo_, bqo = QO[s], bQO[s]
                tk.op("act", lambda qo_=qo_, PT=PT: nc.scalar.copy(out=qo_[:], in_=PT[:, 0:W]),
                      reads=[bPT], writes=[bqo])
                tk.dma("sp", vo[:, :], qo_[:], reads=[bqo], writes=[outbuf], nowaw=True)
        tk.finish("sp", [outbuf])
    return nc


def rope_tables(core):
    t = np.arange(core * 1024, (core + 1) * 1024)
    row, col = t // 64, t % 64
    p = np.arange(128)
    dd = p % 64
    half = dd // 32
    within = dd % 32
    j = within % 16
    part = within // 16
    inv = (10000.0 ** (-np.arange(16, dtype=np.float32) / 16)).astype(np.float32)
    pos = np.where(half[:, None] == 0, row[None, :], col[None, :]).astype(np.float32)
    ang = pos * inv[j][:, None]
    return np.cos(ang).astype(np.float32), np.sin(ang).astype(np.float32)


def rope_rm():
    Rm = np.zeros((128, 128), np.float32)
    for pp in range(128):
        within = (pp % 64) % 32
        if within < 16:
            Rm[pp + 16, pp] = -1.0
        else:
            Rm[pp - 16, pp] = 1.0
    return Rm


def chan_dft_table():
    c = np.arange(256)
    ang = 2 * np.pi * np.outer(c, c) / 256.0
    C = (np.cos(ang) / 16.0).astype(np.float32)
    S = (np.sin(ang) / 16.0).astype(np.float32)
    CS = np.zeros((128, 2, 512), np.float32)
    for cc in range(2):
        CS[:, cc, 0:256] = C[cc * 128:(cc + 1) * 128]
        CS[:, cc, 256:512] = S[cc * 128:(cc + 1) * 128]
    return CS.reshape(128, 1024)


def fa1_xin(x_lat, x_ctx, core):
    cols = np.concatenate([x_lat[core * 1024:(core + 1) * 1024], x_ctx[core * NCTX_FA:(core + 1) * NCTX_FA]], 0)
    return to_fm(cols)


W_FA2 = 1056
NLAT = 1024
NCTX_FA = 32
NSEQ = 8192
NKB = 10
NVT = 12
KW = NKB * 128
SEGS_FA2 = [("lat", 0, 1024), ("ctx", 1024, 1056)]


def build_fa2():
    nc = new_nc()
    Zl = dram_in(nc, "Zl", [NSEQ, 2048], BF16)
    Zc = dram_in(nc, "Zc", [256, 2048], BF16)
    Tl = dram_in(nc, "Tl", [64, 128, 2, 1024], BF16)
    Tc = dram_in(nc, "Tc", [2, 128, 2, NCTX_FA], BF16)
    qd = dram_in(nc, "qT", [128, 8, W_FA2], BF16)
    kd = dram_in(nc, "Kd", [2, 128, KW + 256], BF16)
    vd = dram_in(nc, "Vw", [128, NVT, 128], BF16)
    md = dram_in(nc, "mask", [128, NKB, 384], BF16)
    sd = dram_in(nc, "sink", [128, 16])
    wout = dram_in(nc, "wout", [16, 128, KC * 128])
    xT = dram_in(nc, "xT", [128, KC, W_FA2])
    modl = dram_in(nc, "modl", [128, 3, KC])
    modc = dram_in(nc, "modc", [128, 3, KC])
    yT = dram_out(nc, "yT", [128, KC, W_FA2])

    es = contextlib.ExitStack()
    with es:
        tk = Tk(nc, es)
        CAT = tk.sbuf("CAT", [128, KC, W_FA2], BF16)
        Q = tk.sbuf("Q", [128, 8, W_FA2], BF16)
        ZB = [tk.sbuf(f"ZB{i}", [128, 2048], BF16) for i in range(4)]
        TB = [tk.sbuf(f"TB{i}", [128, 2, 512], BF16) for i in range(4)]
        KD = tk.sbuf("KD", [128, 2, KW + 256], BF16)
        VW = tk.sbuf("VW", [128, NVT, 128], BF16)
        VA = tk.sbuf("VA", [128, 2, 2, NVT, 128], BF16)
        MK = tk.sbuf("MK", [128, NKB, 384], BF16)
        SK = tk.sbuf("SK", [128, 16], F32)
        EB = [tk.sbuf(f"EB{i}", [128, 512], BF16) for i in range(4)]
        RD = tk.sbuf("RD", [128, W_FA2], F32)
        NUM = tk.sbuf("NUM", [128, W_FA2], F32)
        XB = [tk.sbuf(f"XB{i}", [128, W_FA2], F32) for i in range(3)]
        ml = tk.sbuf("ml", [128, 3, KC], F32)
        mc = tk.sbuf("mc", [128, 3, KC], F32)
        ALL = tk.psum("ALL", [128, 4096])

        def bank(b, n=1):
            return ALL[:, b * 512:(b + n) * 512]

        bBank = [tk.buf(f"bank{i}") for i in range(8)]
        bCAT, bQ, bKD, bVW, bVA, bMK, bSK, bC = (tk.buf() for _ in range(8))
        bZB = [tk.buf() for _ in range(4)]
        bTB = [tk.buf() for _ in range(4)]
        bEB = [tk.buf() for _ in range(4)]
        bRD, bNUM = tk.buf(), tk.buf()
        bXB = [tk.buf() for _ in range(3)]

        tk.dma("sp", Q[:], qd[:, :, :], writes=[bQ])
        for kv in range(2):
            tk.dma("sp", KD[:, kv, :], kd[kv, :, :], writes=[bKD], nowaw=True)
        tk.dma("sp", VW[:], vd[:, :, :], writes=[bVW])
        tk.dma("sp", MK[:], md[:, :, :], writes=[bMK])
        tk.dma("sp", SK[:], sd[:, :], writes=[bSK])
        tk.dma("sp", ml[:], modl[:, :, :], writes=[bC], nowaw=True)
        tk.dma("sp", mc[:], modc[:, :, :], writes=[bC], nowaw=True)
        lin = Linear(tk, nc, "wout")
        lin.plan([wout[i, :, :] for i in range(16)])

        for hb in range(2):
            for n in range(64):
                s = n % 4
                tk.dma("sp", ZB[s][:], Zl[n * 128:(n + 1) * 128, :], writes=[bZB[s]])
                tk.dma("sp", TB[s][:], Tl[n, :, :, hb * 512:(hb + 1) * 512], writes=[bTB[s]])
                for fc in range(8):
                    g, ch = divmod(fc, 2)
                    for pq in range(2):
                        tk.op("pe", lambda fc=fc, g=g, ch=ch, pq=pq, s=s, n=n: nc.tensor.matmul(
                            bank(fc), lhsT=ZB[s][:, g * 512 + pq * 256 + ch * 128: g * 512 + pq * 256 + ch * 128 + 128],
                            rhs=TB[s][:, pq, :], start=(n == 0 and pq == 0), stop=(n == 63 and pq == 1)),
                            reads=[bZB[s], bTB[s]], writes=[bBank[fc]], nowaw=not (n == 0 and pq == 0))
            for fc in range(8):
                tk.op("act", lambda fc=fc, hb=hb: nc.scalar.copy(out=CAT[:, fc, hb * 512:(hb + 1) * 512], in_=bank(fc)),
                      reads=[bBank[fc]], writes=[bCAT], nowaw=True)
        for n in range(2):
            s = n % 4
            tk.dma("sp", ZB[s][:], Zc[n * 128:(n + 1) * 128, :], writes=[bZB[s]])
            tk.dma("sp", TB[s][:, :, 0:NCTX_FA], Tc[n, :, :, :], writes=[bTB[s]])
            for fc in range(8):
                g, ch = divmod(fc, 2)
                for pq in range(2):
                    tk.op("pe", lambda fc=fc, g=g, ch=ch, pq=pq, s=s, n=n: nc.tensor.matmul(
                        ALL[:, fc * 512: fc * 512 + NCTX_FA],
                        lhsT=ZB[s][:, g * 512 + pq * 256 + ch * 128: g * 512 + pq * 256 + ch * 128 + 128],
                        rhs=TB[s][:, pq, 0:NCTX_FA], start=(n == 0 and pq == 0), stop=(n == 1 and pq == 1)),
                        reads=[bZB[s], bTB[s]], writes=[bBank[fc]], nowaw=not (n == 0 and pq == 0))
        for fc in range(8):
            tk.op("act", lambda fc=fc: nc.scalar.copy(out=CAT[:, fc, NLAT:W_FA2], in_=ALL[:, fc * 512: fc * 512 + NCTX_FA]),
                  reads=[bBank[fc]], writes=[bCAT], nowaw=True)

        tk.op("act", lambda: nc.scalar.activation(out=SK[:], in_=SK[:], func=AF.Exp), reads=[bSK], writes=[bSK])
        tk.op("dve", lambda: nc.vector.memset(VA[:], 1.0), writes=[bVA])
        for kv in range(2):
            tk.op("dve", lambda kv=kv: nc.vector.tensor_copy(out=VA[:, 0, kv, :, 0:64], in_=VW[:, :, kv * 64:(kv + 1) * 64]),
                  reads=[bVW], writes=[bVA])
            tk.op("dve", lambda kv=kv: nc.vector.tensor_copy(out=VA[:, 1, kv, :, 64:128], in_=VW[:, :, kv * 64:(kv + 1) * 64]),
                  reads=[bVW], writes=[bVA])

        ectr = {"n": 0}
        allwork = []
        cb_banks = col_banks(W_FA2)
        for h in range(16):
            ch, var = divmod(h, 2)
            base = var * 64
            dbase = 64 - base
            kv = h // 8
            pob = 2 + 3 * (h % 2)
            bPO = bBank[pob:pob + 3]
            work = []
            for cbk in range(2):
                for (c0, c1) in cb_banks:
                    work.append((NKB + cbk, KW + cbk * 128, c0, c1, None))
            for j in range(NKB):
                q0, q1 = max(0, j - 2) * 128, min(8, j + 1) * 128
                pieces = [(q0, q1)] if (q0 // 512 == (q1 - 1) // 512) else [(q0, 512), (512, q1)]
                for (c0, c1) in pieces:
                    work.append((j, j * 128, c0, c1, (j, c0 - (j - 2) * 128, c1 - (j - 2) * 128)))
            last = {}
            first = {}
            for wi, wk in enumerate(work):
                b = wk[2] // 512
                last[b] = wi
                first.setdefault(b, wi)
            hctx = dict(h=h, ch=ch, var=var, base=base, dbase=dbase, kv=kv, pob=pob, bPO=bPO, first=first, last=last,
                        nwork=len(work))
            for wi, wk in enumerate(work):
                allwork.append((hctx, wi, wk))

        def emit_front(hc, wi, wk):
            (vt, kc0, c0, c1, msk) = wk
            base, kv, ch = hc["base"], hc["kv"], hc["ch"]
            sb = ectr["n"] % 2
            e = ectr["n"] % 4
            ectr["n"] += 1
            n = c1 - c0
            tk.op("pe", lambda: nc.tensor.matmul(
                ALL[:, sb * 512: sb * 512 + n], lhsT=KD[base:base + 64, kv, kc0:kc0 + 128],
                rhs=Q[base:base + 64, ch, c0:c1], start=True, stop=True),
                reads=[bKD, bQ], writes=[bBank[sb]])
            tk.op("act", lambda: nc.scalar.activation(
                out=EB[e][:, 0:n], in_=ALL[:, sb * 512: sb * 512 + n], func=AF.Exp, scale=0.125),
                reads=[bBank[sb]], writes=[bEB[e]])
            if msk is not None:
                mj, m0, m1 = msk
                tk.op("dve", lambda: nc.vector.tensor_tensor(
                    out=EB[e][:, 0:n], in0=EB[e][:, 0:n], in1=MK[:, mj, m0:m1], op=ALU.mult),
                    reads=[bEB[e], bMK], writes=[bEB[e]])
            return e

        def emit_back(hc, wi, wk, e):
            (vt, kc0, c0, c1, msk) = wk
            base, dbase, kv, ch, var, pob, bPO, h = (hc[k] for k in ("base", "dbase", "kv", "ch", "var", "pob", "bPO", "h"))
            first, last = hc["first"], hc["last"]
            n = c1 - c0
            b = c0 // 512
            tk.op("pe", lambda: nc.tensor.matmul(
                ALL[:, pob * 512 + c0: pob * 512 + c1], lhsT=VA[:, var, kv, vt, :], rhs=EB[e][:, 0:n],
                start=(first[b] == wi), stop=(last[b] == wi)),
                reads=[bVA, bEB[e]], writes=[bPO[b]], nowaw=(first[b] != wi))
            if wi != hc["nwork"] - 1:
                return
            PO = ALL[:, pob * 512: pob * 512 + W_FA2]
            tk.op("dve", lambda: nc.vector.tensor_scalar(
                out=RD[base:base + 64, :], in0=PO[dbase:dbase + 64, :], scalar1=SK[dbase:dbase + 64, h:h + 1],
                scalar2=None, op0=ALU.add), reads=bPO + [bSK], writes=[bRD])
            tk.op("dve", lambda: nc.vector.reciprocal(out=RD[base:base + 64, :], in_=RD[base:base + 64, :]),
                  reads=[bRD], writes=[bRD])
            tk.op("act", lambda: nc.scalar.copy(out=NUM[base:base + 64, :], in_=PO[base:base + 64, :]),
                  reads=bPO, writes=[bNUM])
            tk.op("dve", lambda: nc.vector.tensor_tensor(
                out=CAT[base:base + 64, 8 + ch, :], in0=NUM[base:base + 64, :], in1=RD[base:base + 64, :],
                op=ALU.mult), reads=[bNUM, bRD], writes=[bCAT], nowaw=True)

        DEPTH = 2
        pend = []
        for (hc, wi, wk) in allwork:
            e = emit_front(hc, wi, wk)
            pend.append((hc, wi, wk, e))
            if len(pend) > DEPTH:
                emit_back(*pend.pop(0))
        while pend:
            emit_back(*pend.pop(0))

        bOut = tk.buf()
        for oc in range(16):
            pb = 3 * (oc % 2)
            PT = ALL[:, pb * 512: pb * 512 + 1536]
            xs = oc % 3
            tk.dma("sp", XB[xs][:], xT[:, oc, :], writes=[bXB[xs]])
            lin.run(CAT, bCAT, W_FA2, PT, bBank[pb:pb + 3])
            for (nm, s0, s1) in SEGS_FA2:
                m = ml if nm == "lat" else mc
                tk.op("dve", lambda oc=oc, s0=s0, s1=s1, m=m, PT=PT, xs=xs: nc.vector.scalar_tensor_tensor(
                    out=XB[xs][:, s0:s1], in0=PT[:, s0:s1], scalar=m[:, 2, oc:oc + 1], in1=XB[xs][:, s0:s1],
                    op0=ALU.mult, op1=ALU.add), reads=bBank[pb:pb + 3] + [bXB[xs], bC], writes=[bXB[xs]])
            tk.dma("sp", yT[:, oc, :], XB[xs][:], reads=[bXB[xs]], writes=[bOut], nowaw=True)
        tk.finish("sp", [bOut])
    return nc


def seq_tables(core):
    n = np.arange(NSEQ, dtype=np.int64)
    k = np.arange(core * 1024, (core + 1) * 1024, dtype=np.int64)
    ph = (np.outer(n, k) % NSEQ).astype(np.float64) * (2 * np.pi / NSEQ)
    sc = 1.0 / np.sqrt(NSEQ)
    T = np.stack([np.cos(ph) * sc, -np.sin(ph) * sc], 1)
    Tl = T.reshape(64, 128, 2, 1024).astype(ml_dtypes.bfloat16)
    n = np.arange(256, dtype=np.int64)
    k = np.arange(core * NCTX_FA, (core + 1) * NCTX_FA, dtype=np.int64)
    ph = (np.outer(n, k) % 256).astype(np.float64) * (2 * np.pi / 256)
    T = np.stack([np.cos(ph) / 16.0, -np.sin(ph) / 16.0], 1)
    Tc = T.reshape(2, 128, 2, NCTX_FA).astype(ml_dtypes.bfloat16)
    return Tl, Tc


def attn_mask(core):
    lo = core * 1024
    kk = np.arange(128)
    m = np.zeros((128, NKB, 384), np.float32)
    for j in range(NKB):
        kpos = lo - 128 + 128 * j + kk
        qpos = lo + (j - 2) * 128 + np.arange(384)
        ok = (np.abs(kpos[:, None] - qpos[None, :]) <= 128) & (kpos[:, None] >= 0) & (kpos[:, None] < NSEQ)
        m[:, j, :] = ok
    return m.astype(ml_dtypes.bfloat16)


def fa2_kv(kT_all, kcT, vT_all, vcT, core):
    lo, hi = core * 1024, (core + 1) * 1024
    kw = np.zeros((128, KW), kT_all.dtype)
    vw = np.zeros((128, KW), vT_all.dtype)
    a, b = max(lo - 128, 0), min(hi + 128, NSEQ)
    kw[:, a - (lo - 128): b - (lo - 128)] = kT_all[:, a:b]
    vw[:, a - (lo - 128): b - (lo - 128)] = vT_all[:, a:b]
    kfull = np.concatenate([kw, kcT], 1)
    Kd = np.stack([np.concatenate([kfull[kv * 64:(kv + 1) * 64]] * 2, 0) for kv in range(2)], 0)
    vfull = np.concatenate([vw, vcT], 1).T
    Vw = np.ascontiguousarray(vfull.reshape(NVT, 128, 128).transpose(1, 0, 2))
    return np.ascontiguousarray(Kd), Vw


W_RG = 1062
NCTX_RG = 32
SEGS_RG = [("lat", 0, 1027), ("ctx", 1027, 1062)]
VALID = [(2, 1026), (1029, 1061)]
HALO = [0, 1, 1026, 1027, 1028, 1061]
NOUT_RG = 1024 + NCTX_RG
LRU_C = 8.0
NT = 11


def rev(t2d):
    return bass.AP(tensor=t2d.tensor, offset=t2d.offset + (t2d.ap[-1][1] - 1) * t2d.ap[-1][0],
                   ap=[list(t2d.ap[0]), [-t2d.ap[-1][0], t2d.ap[-1][1]]])


def build_rg(phase):
    nc = new_nc()
    xT = dram_in(nc, "xT", [128, KC, W_RG])
    edge = dram_in(nc, "edge", [128, 6])
    modl = dram_in(nc, "modl", [128, 3, KC])
    modc = dram_in(nc, "modc", [128, 3, KC])
    gd = dram_in(nc, "g", [128, KC])
    wall = dram_in(nc, "wall", [32 if phase == 1 else 48, 128, KC * 128])
    cwd = dram_in(nc, "cw", [128, KC, 4])
    cbd = dram_in(nc, "cb", [128, KC])
    wgd = dram_in(nc, "wg", [128, 64 * 256])
    gbd = dram_in(nc, "gb", [128, 64])
    lamd = dram_in(nc, "lam", [128, 32])
    if phase == 2:
        cad = dram_in(nc, "CA", [128, 64, 16])
        cbd2 = dram_in(nc, "CB", [128, 64, 16])
        yT = dram_out(nc, "yT", [128, KC, NOUT_RG])
    else:
        So = dram_out(nc, "S", [128, 2, 2, 2, KC])
        Y0o = dram_out(nc, "Y0", [128, KC, NOUT_RG], BF16)
        PFo = dram_out(nc, "PF", [128, KC, NOUT_RG], BF16)
        PRo = dram_out(nc, "PR", [128, KC, NOUT_RG], BF16)

    es = contextlib.ExitStack()
    with es:
        tk = Tk(nc, es)
        H = tk.sbuf("H", [128, KC, W_RG], BF16)
        XCB = tk.sbuf("XCB", [128, KC, W_RG], BF16)
        if phase == 2:
            Y = tk.sbuf("Y", [128, KC, W_RG], BF16)
        else:
            OB = [tk.sbuf(f"OB{i}", [128, 3, W_RG], BF16) for i in range(2)]
            bOB = [tk.buf() for _ in range(2)]
            PC = [tk.sbuf(f"PC{i}", [128, W_RG], F32) for i in range(2)]
            bPC = [tk.buf() for _ in range(2)]
        WG = tk.sbuf("WG", [128, 64 * 256], BF16)
        T = [tk.sbuf(f"T{i}", [128, W_RG], F32) for i in range(NT)]
        XT = [tk.sbuf(f"XT{i}", [128, W_RG], F32) for i in range(4)]
        XB = [tk.sbuf(f"XB{i}", [128, W_RG], F32) for i in range(2)]
        edg = tk.sbuf("edg", [128, 6], F32)
        ml = tk.sbuf("ml", [128, 3, KC], F32)
        mc = tk.sbuf("mc", [128, 3, KC], F32)
        g = tk.sbuf("gs", [128, KC], F32)
        AB = tk.sbuf("AB", [128, 2, KC], F32)
        cw = tk.sbuf("cws", [128, KC, 4], F32)
        cb = tk.sbuf("cbs", [128, KC], F32)
        nb = tk.sbuf("nb", [128, 64], F32)
        c8 = tk.sbuf("c8", [128, 32], F32)
        c16 = tk.sbuf("c16", [128, 32], F32)
        S = tk.sbuf("Ssum", [128, 2, 2, 2, KC], F32)
        H0 = tk.sbuf("H0", [128, 64], F32)
        if phase == 2:
            CA = tk.sbuf("CAs", [128, 64, 16], F32)
            CB = tk.sbuf("CBs", [128, 64, 16], F32)
            CT = tk.sbuf("CT", [128, 16], F32)
        PA = tk.psum("PA", [128, 1536])
        PB = tk.psum("PB", [128, 1536])

        bH, bXCB, bY, bWG, bC, bAB, bS, bH0 = (tk.buf() for _ in range(8))
        bT = [tk.buf() for _ in range(NT)]
        bXT = [tk.buf() for _ in range(4)]
        bXB = [tk.buf() for _ in range(2)]
        bPA, bPB = tk.buf(), tk.buf()
        PS = [(PA, bPA), (PB, bPB)]
        pctr = {"n": 0}

        def next_ps():
            p = PS[pctr["n"] % 2]
            pctr["n"] += 1
            return p

        for (dst, src) in ((edg[:], edge[:, :]), (ml[:], modl[:, :, :]), (mc[:], modc[:, :, :]), (g[:], gd[:, :]),
                           (cw[:], cwd[:, :, :]), (cb[:], cbd[:, :]), (nb[:], gbd[:, :]), (c8[:], lamd[:, :])):
            tk.dma("sp", dst, src, writes=[bC], nowaw=True)
        if phase == 2:
            tk.dma("sp", CA[:], cad[:, :, :], writes=[bC], nowaw=True)
            tk.dma("sp", CB[:], cbd2[:, :, :], writes=[bC], nowaw=True)
        tk.dma("pool", WG[:], wgd[:, :], writes=[bWG])
        lin = Linear(tk, nc, "wall", nslots=2)
        order = [0, 1]
        for c in range(KC):
            if c + 2 < KC:
                order.append(c + 2)
            order.append(16 + c)
        if phase == 2:
            order += list(range(32, 48))
        lin.plan([wall[i, :, :] for i in order])
        ones, bOnes = emit_consts(tk, nc)
        onesb = tk.sbuf("onesb", [128, 128], BF16)
        SQB = [tk.sbuf(f"SQB{i}", [128, W_RG], BF16) for i in range(2)]
        bSQB = [tk.buf(), tk.buf()]
        tk.op("dve", lambda: nc.vector.memset(onesb[:], 1.0), writes=[bOnes], nowaw=True)
        bXCBc_init = []
        tk.op("dve", lambda: nc.vector.memset(XCB[:], 0.0), writes=[bXCB])
        tk.op("dve", lambda: nc.vector.memset(T[9][:], 0.0), writes=[bT[9]])
        ZERO = T[9]
        tk.op("act", lambda: nc.scalar.activation(out=c8[:], in_=c8[:], func=AF.Exp, scale=-1.0), reads=[bC], writes=[bC])
        tk.op("act", lambda: nc.scalar.activation(out=c8[:], in_=c8[:], func=AF.Ln, bias=1.0), reads=[bC], writes=[bC])
        tk.op("dve", lambda: nc.vector.tensor_scalar(out=c16[:], in0=c8[:], scalar1=-2.0 * LRU_C, scalar2=None, op0=ALU.mult),
              reads=[bC], writes=[bC])
        tk.op("dve", lambda: nc.vector.tensor_scalar(out=c8[:], in0=c8[:], scalar1=-LRU_C, scalar2=None, op0=ALU.mult),
              reads=[bC], writes=[bC])

        if phase == 2:
            for idx in range(64):
                tk.op("dve", lambda idx=idx: nc.vector.tensor_tensor_scan(
                    out=CT[:], data0=CA[:, idx, :], data1=CB[:, idx, :], initial=0.0, op0=ALU.mult, op1=ALU.add),
                    reads=[bC], writes=[bH0])
                tk.op("dve", lambda idx=idx: nc.vector.tensor_copy(out=H0[:, idx:idx + 1], in_=CT[:, 15:16]),
                      reads=[bH0], writes=[bH0])

        class XS_:
            n = 0

        def xload(c):
            s = XS_.n % 2
            XS_.n += 1
            tk.dma("sp", XB[s][:], xT[:, c, :], writes=[bXB[s]])
            return XB[s], bXB[s]

        class XView:
            cur = {}

        banks = col_banks(W_RG)
        mods = {"lat": ml, "ctx": mc}
        for i, (nm, _, _) in enumerate(SEGS_RG):
            m = mods[nm]
            tk.op("dve", lambda i=i, m=m: nc.vector.tensor_scalar(
                out=AB[:, i, :], in0=m[:, 1, :], scalar1=1.0, scalar2=None, op0=ALU.add),
                reads=[bC], writes=[bAB], nowaw=True)
            tk.op("dve", lambda i=i: nc.vector.tensor_tensor(
                out=AB[:, i, :], in0=AB[:, i, :], in1=g[:], op=ALU.mult), reads=[bC, bAB], writes=[bAB])
        for c in range(KC):
            xb, bxb = xload(c)
            s = c % 2
            tk.op("act", lambda xb=xb, s=s: nc.scalar.activation(out=SQB[s][:], in_=xb[:], func=AF.Square),
                  reads=[bxb], writes=[bSQB[s]])
            for (b0, b1) in banks:
                tk.op("pe", lambda c=c, s=s, b0=b0, b1=b1: nc.tensor.matmul(
                    PA[:, b0:b1], lhsT=onesb[:], rhs=SQB[s][:, b0:b1], start=(c == 0), stop=(c == KC - 1)),
                    reads=[bSQB[s], bOnes], writes=[bPA], nowaw=(c > 0))
        RS, bRS = T[2], bT[2]
        tk.op("dve", lambda: nc.vector.tensor_scalar(out=RS[:], in0=PA[:, 0:W_RG], scalar1=1.0 / D, scalar2=EPS,
                                                      op0=ALU.mult, op1=ALU.add), reads=[bPA], writes=[bRS])
        tk.op("act", lambda: nc.scalar.activation(out=RS[:], in_=RS[:], func=AF.Sqrt), reads=[bRS], writes=[bRS])
        tk.op("dve", lambda: nc.vector.reciprocal(out=RS[:], in_=RS[:]), reads=[bRS], writes=[bRS])
        for c in range(KC):
            xb, bxb = xload(c)
            s = c % 2
            tk.op("dve", lambda xb=xb, s=s: nc.vector.tensor_tensor(out=T[s][:], in0=xb[:], in1=RS[:], op=ALU.mult),
                  reads=[bxb, bRS], writes=[bT[s]])
            for i, (nm, s0, s1) in enumerate(SEGS_RG):
                m = mods[nm]
                tk.op("act", lambda c=c, s=s, s0=s0, s1=s1, i=i, m=m: nc.scalar.activation(
                    out=H[:, c, s0:s1], in_=T[s][:, s0:s1], func=AF.Identity,
                    bias=m[:, 0, c:c + 1], scale=AB[:, i, c:c + 1]),
                    reads=[bT[s], bAB, bC], writes=[bH], nowaw=True)
        for k, col in enumerate(HALO):
            tk.op("dve", lambda k=k, col=col: nc.vector.tensor_scalar(
                out=H[:, :, col:col + 1], in0=H[:, :, col:col + 1], scalar1=edg[:, k:k + 1], scalar2=None,
                op0=ALU.mult), reads=[bH, bC], writes=[bH])

        def xs_chunk(c):
            PT, bPT = next_ps()
            lin.run(H, bH, W_RG, PT, bPT)
            s = c % 2
            xs, bxs = XT[s], bXT[s]
            xc, bxc = XT[2 + s], bXT[2 + s]
            tk.op("act", lambda PT=PT, xs=xs: nc.scalar.copy(out=xs[:], in_=PT[:, 0:W_RG]), reads=[bPT], writes=[bxs])
            tk.op("dve", lambda c=c, xs=xs, xc=xc: nc.vector.tensor_scalar(
                out=xc[:, 2:W_RG - 1], in0=xs[:, 0:W_RG - 3], scalar1=cw[:, c, 0:1], scalar2=cb[:, c:c + 1],
                op0=ALU.mult, op1=ALU.add), reads=[bxs, bC], writes=[bxc])
            for k in range(1, 4):
                last = (k == 3)
                dst = XCB[:, c, 2:W_RG - 1] if last else xc[:, 2:W_RG - 1]
                tk.op("dve", lambda c=c, xs=xs, xc=xc, k=k, dst=dst: nc.vector.scalar_tensor_tensor(
                    out=dst, in0=xs[:, k:W_RG - 3 + k], scalar=cw[:, c, k:k + 1], in1=xc[:, 2:W_RG - 1],
                    op0=ALU.mult, op1=ALU.add), reads=[bxs, bxc, bC],
                    writes=([bXCBc[c]] if last else [bxc]), nowaw=last)

        bXCBc = [tk.buf() for _ in range(KC)]
        xs_chunk(0)
        xs_chunk(1)

        bOutA = tk.buf()
        if phase == 1:
            for d in range(2):
                tk.op("dve", lambda d=d: nc.vector.memset(PC[d][:], 0.0), writes=[bPC[d]])
        def sigmoid_from_psum(PT, bPT, dst, bdst, bias_ap):
            tk.op("act", lambda: nc.scalar.activation(out=dst[:], in_=PT[:, 0:W_RG], func=AF.Sigmoid, bias=bias_ap),
                  reads=[bPT, bC], writes=[bdst])

        for c in range(KC):
            hb = c // 2
            if c + 2 < KC:
                xs_chunk(c + 2)
            HS = []
            for d in range(2):
                tr, btr = T[3 * d + 0], bT[3 * d + 0]
                ti, bti = T[3 * d + 1], bT[3 * d + 1]
                ta, bta = T[3 * d + 2], bT[3 * d + 2]
                hs, bhs = T[6 + d], bT[6 + d]
                for typ, (dst, bdst) in enumerate(((tr, btr), (ti, bti))):
                    PT, bPT = next_ps()
                    widx = (d * 2 + typ) * 16 + c
                    for ic in range(2):
                        for (b0, b1) in banks:
                            tk.op("pe", lambda ic=ic, b0=b0, b1=b1, PT=PT, widx=widx: nc.tensor.matmul(
                                PT[:, b0:b1], lhsT=WG[:, widx * 256 + ic * 128: widx * 256 + ic * 128 + 128],
                                rhs=XCB[:, hb * 2 + ic, b0:b1], start=(ic == 0), stop=(ic == 1)),
                                reads=[bXCB, bXCBc[hb * 2 + ic], bWG], writes=[bPT], nowaw=(ic > 0))
                    sigmoid_from_psum(PT, bPT, dst, bdst, nb[:, widx:widx + 1])
                tk.op("act", lambda d=d, tr=tr, ta=ta: nc.scalar.activation(
                    out=ta[:], in_=tr[:], func=AF.Exp, scale=c8[:, d * 16 + c: d * 16 + c + 1]),
                    reads=[btr, bC], writes=[bta])
                tk.op("act", lambda d=d, tr=tr: nc.scalar.activation(
                    out=tr[:], in_=tr[:], func=AF.Exp, scale=c16[:, d * 16 + c: d * 16 + c + 1]),
                    reads=[btr, bC], writes=[btr])
                tk.op("act", lambda tr=tr: nc.scalar.activation(out=tr[:], in_=tr[:], func=AF.Ln, scale=-1.0, bias=1.0),
                      reads=[btr], writes=[btr])
                tk.op("act", lambda tr=tr: nc.scalar.activation(out=tr[:], in_=tr[:], func=AF.Exp, scale=0.5),
                      reads=[btr], writes=[btr])
                tk.op("dve", lambda ti=ti: nc.vector.tensor_tensor(out=ti[:], in0=ti[:], in1=XCB[:, c, :], op=ALU.mult),
                      reads=[bti, bXCB, bXCBc[c]], writes=[bti])
                tk.op("dve", lambda ti=ti, tr=tr: nc.vector.tensor_tensor(out=ti[:], in0=ti[:], in1=tr[:], op=ALU.mult),
                      reads=[bti, btr], writes=[bti])
                for si, (v0, v1) in enumerate(VALID):
                    idx = (d * 2 + si) * 16 + c
                    init = 0.0 if phase == 1 else H0[:, idx:idx + 1]
                    o_, a_, g_ = hs[:, v0:v1], ta[:, v0:v1], ti[:, v0:v1]
                    if d == 1:
                        o_, a_, g_ = rev(o_), rev(a_), rev(g_)
                    tk.op("dve", lambda o_=o_, a_=a_, g_=g_, init=init: nc.vector.tensor_tensor_scan(
                        out=o_, data0=a_, data1=g_, initial=init, op0=ALU.mult, op1=ALU.add),
                        reads=[bta, bti, bH0], writes=[bhs], nowaw=(si > 0))
                    if phase == 1:
                        endc = (v1 - 1) if d == 0 else v0
                        tk.op("dve", lambda hs=hs, d=d, si=si, endc=endc: nc.vector.tensor_copy(
                            out=S[:, d, si, 1, c:c + 1], in_=hs[:, endc:endc + 1]), reads=[bhs], writes=[bS], nowaw=True)
                        z_ = ZERO[:, v0:v1]
                        p_ = PC[d][:, v0:v1]
                        if d == 1:
                            z_, p_ = rev(z_), rev(p_)
                        tk.op("dve", lambda p_=p_, a_=a_, z_=z_: nc.vector.tensor_tensor_scan(
                            out=p_, data0=a_, data1=z_, initial=1.0, op0=ALU.mult, op1=ALU.add),
                            reads=[bta, bT[9]], writes=[bPC[d]], nowaw=(si > 0))
                        tk.op("dve", lambda d=d, si=si, endc=endc: nc.vector.tensor_copy(
                            out=S[:, d, si, 0, c:c + 1], in_=PC[d][:, endc:endc + 1]), reads=[bPC[d]], writes=[bS], nowaw=True)
                HS.append((hs, bhs))
            if True:
                PT, bPT = next_ps()
                lin.run(H, bH, W_RG, PT, bPT)
                gt_, bgt = T[8], bT[8]
                t1, bt1 = T[10], bT[10]
                tk.op("act", lambda PT=PT: nc.scalar.copy(out=gt_[:], in_=PT[:, 0:W_RG]), reads=[bPT], writes=[bgt])
                tk.op("dve", lambda: nc.vector.tensor_tensor(out=t1[:], in0=gt_[:], in1=gt_[:], op=ALU.mult),
                      reads=[bgt], writes=[bt1])
                tk.op("dve", lambda: nc.vector.tensor_scalar(out=t1[:], in0=t1[:], scalar1=0.044715, scalar2=1.0,
                                                              op0=ALU.mult, op1=ALU.add), reads=[bt1], writes=[bt1])
                tk.op("dve", lambda: nc.vector.tensor_tensor(out=t1[:], in0=t1[:], in1=gt_[:], op=ALU.mult),
                      reads=[bt1, bgt], writes=[bt1])
                tk.op("act", lambda: nc.scalar.activation(out=t1[:], in_=t1[:], func=AF.Sigmoid, scale=1.5957691216),
                      reads=[bt1], writes=[bt1])
                tk.op("dve", lambda: nc.vector.tensor_tensor(out=gt_[:], in0=gt_[:], in1=t1[:], op=ALU.mult),
                      reads=[bt1, bgt], writes=[bgt])
                (h0_, bh0_), (h1_, bh1_) = HS
                tk.op("dve", lambda: nc.vector.tensor_tensor(out=h0_[:], in0=h0_[:], in1=h1_[:], op=ALU.add),
                      reads=[bh0_, bh1_], writes=[bh0_])
                if phase == 2:
                    tk.op("dve", lambda c=c: nc.vector.tensor_tensor(out=Y[:, c, :], in0=h0_[:], in1=gt_[:], op=ALU.mult),
                          reads=[bh0_, bgt], writes=[bY], nowaw=True)
                else:
                    ob, bob = OB[c % 2], bOB[c % 2]
                    tk.op("pool", lambda ob=ob: nc.gpsimd.tensor_tensor(out=ob[:, 0, :], in0=h0_[:], in1=gt_[:], op=ALU.mult),
                          reads=[bh0_, bgt], writes=[bob])
                    for d in range(2):
                        tk.op("pool", lambda ob=ob, d=d: nc.gpsimd.tensor_tensor(
                            out=ob[:, 1 + d, :], in0=PC[d][:], in1=gt_[:], op=ALU.mult),
                            reads=[bPC[d], bgt], writes=[bob], nowaw=True)
                    for k, dst in enumerate((Y0o, PFo, PRo)):
                        tk.dma("sp", dst[:, c, 0:1024], ob[:, k, 2:1026], reads=[bob], writes=[bOutA], nowaw=True)
                        tk.dma("sp", dst[:, c, 1024:NOUT_RG], ob[:, k, 1029:1061], reads=[bob], writes=[bOutA], nowaw=True)

        bOut = tk.buf()
        if phase == 1:
            tk.dma("sp", So[:, :, :, :, :], S[:], reads=[bS], writes=[bOut])
            tk.finish("sp", [bOutA])
        else:
            for oc in range(KC):
                PT, bPT = next_ps()
                xb, bxb = xload(oc)
                lin.run(Y, bY, W_RG, PT, bPT)
                for (nm, s0, s1) in SEGS_RG:
                    m = ml if nm == "lat" else mc
                    tk.op("dve", lambda oc=oc, s0=s0, s1=s1, m=m, PT=PT, xb=xb: nc.vector.scalar_tensor_tensor(
                        out=xb[:, s0:s1], in0=PT[:, s0:s1], scalar=m[:, 2, oc:oc + 1], in1=xb[:, s0:s1],
                        op0=ALU.mult, op1=ALU.add), reads=[bPT, bxb, bC], writes=[bxb])
                tk.dma("sp", yT[:, oc, 0:1024], xb[:, 2:1026], reads=[bxb], writes=[bOut], nowaw=True)
                tk.dma("sp", yT[:, oc, 1024:NOUT_RG], xb[:, 1029:1061], reads=[bxb], writes=[bOut], nowaw=True)
        tk.finish("sp", [bOut])
    return nc


def rg_xin(x_lat, x_ctx, core):
    cols = np.zeros((W_RG, D), np.float32)
    e = np.zeros((128, 6), np.float32)
    lo, hi = core * 1024, (core + 1) * 1024
    cols[2:1026] = x_lat[lo:hi]
    if core > 0:
        cols[0:2] = x_lat[lo - 2:lo]; e[:, 0:2] = 1
    if core < NCORES - 1:
        cols[1026] = x_lat[hi]; e[:, 2] = 1
    clo, chi = core * NCTX_RG, (core + 1) * NCTX_RG
    cols[1029:1061] = x_ctx[clo:chi]
    if core > 0:
        cols[1027:1029] = x_ctx[clo - 2:clo]; e[:, 3:5] = 1
    if core < NCORES - 1:
        cols[1061] = x_ctx[chi]; e[:, 5] = 1
    return to_fm(cols), e


def rg_weights(w_in, conv_w, conv_b, w_a, b_a, w_i, b_i, lam, w_out):
    wall = np.concatenate([tile_w(w_in[:, D:]), tile_w(w_in[:, :D]), tile_w(w_out)], 0)
    cw = np.ascontiguousarray(conv_w.reshape(4, KC, 128).transpose(2, 1, 0))
    cb = fm(conv_b)
    wg = np.zeros((128, 64, 256), np.float32)
    gb = np.zeros((128, 64), np.float32)
    for d in range(2):
        for typ, (wm, bm) in enumerate(((w_a, b_a), (w_i, b_i))):
            for c in range(KC):
                hb, jc = divmod(c, 2)
                idx = (d * 2 + typ) * 16 + c
                blk = wm[d, hb][:, jc * 128:(jc + 1) * 128]
                wg[:, idx, :] = blk.reshape(2, 128, 128).transpose(1, 0, 2).reshape(128, 256)
                gb[:, idx] = bm[d, c * 128:(c + 1) * 128]
    lamf = np.concatenate([fm(lam[0]), fm(lam[1])], 1)
    return dict(wall=wall, cw=cw, cb=cb, wg=wg.reshape(128, 64 * 256), gb=gb, lam=lamf)


def rg_carries(S_all):
    out = []
    for i in range(NCORES):
        CA = np.ones((128, 2, 2, KC, 16), np.float32)
        CB = np.zeros((128, 2, 2, KC, 16), np.float32)
        chains = {
            (0, 1): [(1, j) for j in range(i)],
            (0, 0): [(1, j) for j in range(NCORES)] + [(0, j) for j in range(i)],
            (1, 1): [(1, j) for j in range(NCORES - 1, i, -1)],
            (1, 0): [(1, j) for j in range(NCORES - 1, -1, -1)] + [(0, j) for j in range(NCORES - 1, i, -1)],
        }
        for (d, seg), ch in chains.items():
            pad = 16 - len(ch)
            for n, (sseg, j) in enumerate(ch):
                CA[:, d, seg, :, pad + n] = S_all[j][:, d, sseg, 0, :]
                CB[:, d, seg, :, pad + n] = S_all[j][:, d, sseg, 1, :]
        CA[..., 0] = 0.0
        out.append((CA.reshape(128, 64, 16), CB.reshape(128, 64, 16)))
    return out


W_B = 1056
SEGS_B = [("lat", 0, 1024), ("ctx", 1024, 1056)]


def build_rgb():
    nc = new_nc()
    Wb = W_B
    y0d = dram_in(nc, "Y0", [128, KC, Wb], BF16)
    pfd = dram_in(nc, "PF", [128, KC, Wb], BF16)
    prd = dram_in(nc, "PR", [128, KC, Wb], BF16)
    cad = dram_in(nc, "CA", [128, 64, 16])
    cbd2 = dram_in(nc, "CB", [128, 64, 16])
    wout = dram_in(nc, "wout", [16, 128, KC * 128])
    xT = dram_in(nc, "xT", [128, KC, Wb])
    modl = dram_in(nc, "modl", [128, 3, KC])
    modc = dram_in(nc, "modc", [128, 3, KC])
    yT = dram_out(nc, "yT", [128, KC, Wb])
    es = contextlib.ExitStack()
    with es:
        tk = Tk(nc, es)
        Y0 = tk.sbuf("Y0s", [128, KC, Wb], BF16)
        PF = tk.sbuf("PFs", [128, KC, Wb], BF16)
        PR = tk.sbuf("PRs", [128, KC, Wb], BF16)
        Y = tk.sbuf("Y", [128, KC, Wb], BF16)
        TT = [tk.sbuf(f"TT{i}", [128, Wb], F32) for i in range(2)]
        XB = [tk.sbuf(f"XB{i}", [128, Wb], F32) for i in range(3)]
        CA = tk.sbuf("CAs", [128, 1024], F32)
        CB = tk.sbuf("CBs", [128, 1024], F32)
        CT = tk.sbuf("CT", [128, 1024], F32)
        H0 = tk.sbuf("H0", [128, 64], F32)
        ml = tk.sbuf("ml", [128, 3, KC], F32)
        mc = tk.sbuf("mc", [128, 3, KC], F32)
        PA = tk.psum("PA", [128, 1536])
        PB = tk.psum("PB", [128, 1536])
        bIn = [tk.buf() for _ in range(KC)]
        bC, bH0, bY = tk.buf(), tk.buf(), tk.buf()
        bTT = [tk.buf() for _ in range(2)]
        bXB = [tk.buf() for _ in range(3)]
        bPA, bPB = tk.buf(), tk.buf()
        tk.dma("sp", CA[:], cad.rearrange("p a b -> p (a b)"), writes=[bC], nowaw=True)
        tk.dma("sp", CB[:], cbd2.rearrange("p a b -> p (a b)"), writes=[bC], nowaw=True)
        tk.dma("sp", ml[:], modl[:, :, :], writes=[bC], nowaw=True)
        tk.dma("sp", mc[:], modc[:, :, :], writes=[bC], nowaw=True)
        for c in range(KC):
            tk.dma("sp", Y0[:, c, :], y0d[:, c, :], writes=[bIn[c]], nowaw=True)
            tk.dma("sp", PF[:, c, :], pfd[:, c, :], writes=[bIn[c]], nowaw=True)
            tk.dma("sp", PR[:, c, :], prd[:, c, :], writes=[bIn[c]], nowaw=True)
        lin = Linear(tk, nc, "wout")
        lin.plan([wout[i, :, :] for i in range(16)])
        tk.op("dve", lambda: nc.vector.tensor_tensor_scan(
            out=CT[:], data0=CA[:], data1=CB[:], initial=0.0, op0=ALU.mult, op1=ALU.add),
            reads=[bC], writes=[bH0])
        ct2 = CT[:, 0:1024]
        last_of_chain = bass.AP(tensor=ct2.tensor, offset=ct2.offset + 15, ap=[list(ct2.ap[0]), [16, 64]])
        tk.op("dve", lambda: nc.vector.tensor_copy(out=H0[:], in_=last_of_chain), reads=[bH0], writes=[bH0])
        for c in range(KC):
            t, bt = TT[c % 2], bTT[c % 2]
            for si, (nm, s0, s1) in enumerate(SEGS_B):
                i_f = (0 * 2 + si) * 16 + c
                i_r = (1 * 2 + si) * 16 + c
                tk.op("dve", lambda c=c, s0=s0, s1=s1, i_f=i_f, t=t: nc.vector.scalar_tensor_tensor(
                    out=t[:, s0:s1], in0=PF[:, c, s0:s1], scalar=H0[:, i_f:i_f + 1], in1=Y0[:, c, s0:s1],
                    op0=ALU.mult, op1=ALU.add), reads=[bIn[c], bH0], writes=[bt], nowaw=(si > 0))
                tk.op("dve", lambda c=c, s0=s0, s1=s1, i_r=i_r, t=t: nc.vector.scalar_tensor_tensor(
                    out=Y[:, c, s0:s1], in0=PR[:, c, s0:s1], scalar=H0[:, i_r:i_r + 1], in1=t[:, s0:s1],
                    op0=ALU.mult, op1=ALU.add), reads=[bIn[c], bH0, bt], writes=[bY], nowaw=True)
        bOut = tk.buf()
        PS = [(PA, bPA), (PB, bPB)]
        for oc in range(KC):
            PT, bPT = PS[oc % 2]
            xs = oc % 3
            tk.dma("sp", XB[xs][:], xT[:, oc, :], writes=[bXB[xs]])
            lin.run(Y, bY, Wb, PT, bPT)
            for (nm, s0, s1) in SEGS_B:
                m = ml if nm == "lat" else mc
                tk.op("dve", lambda oc=oc, s0=s0, s1=s1, m=m, PT=PT, xs=xs: nc.vector.scalar_tensor_tensor(
                    out=XB[xs][:, s0:s1], in0=PT[:, s0:s1], scalar=m[:, 2, oc:oc + 1], in1=XB[xs][:, s0:s1],
                    op0=ALU.mult, op1=ALU.add), reads=[bPT, bXB[xs], bC], writes=[bXB[xs]])
            tk.dma("sp", yT[:, oc, :], XB[xs][:], reads=[bXB[xs]], writes=[bOut], nowaw=True)
        tk.finish("sp", [bOut])
    return nc


NMT = 48


def build_mod():
    nc = new_nc()
    sT = dram_in(nc, "sT", [128, KC, 2])
    wm = dram_in(nc, "wm", [NMT, 128, KC * 128])
    bmd = dram_in(nc, "bm", [128, NMT])
    mo = dram_out(nc, "mo", [128, NMT, 2])
    es = contextlib.ExitStack()
    with es:
        tk = Tk(nc, es)
        S = tk.sbuf("Ssilu", [128, KC, 2], F32)
        BM = tk.sbuf("BM", [128, NMT], F32)
        OUT = tk.sbuf("OUT", [128, NMT, 2], F32)
        WT = [tk.sbuf(f"WT{i}", [128, KC * 128], F32) for i in range(4)]
        ALL = tk.psum("ALL", [128, 4096])
        bS, bBM, bOUT = tk.buf(), tk.buf(), tk.buf()
        bWT = [tk.buf() for _ in range(4)]
        bB = [tk.buf() for _ in range(8)]
        tk.dma("sp", S[:], sT[:, :, :], writes=[bS])
        tk.dma("sp", BM[:], bmd[:, :], writes=[bBM])
        tk.op("act", lambda: nc.scalar.activation(out=S[:], in_=S[:], func=AF.Silu), reads=[bS], writes=[bS])
        for t in range(NMT):
            s = t % 4
            tk.dma("sp", WT[s][:], wm[t, :, :], writes=[bWT[s]])
            b = t % 8
            for c in range(KC):
                tk.op("pe", lambda c=c, s=s, b=b: nc.tensor.matmul(
                    ALL[:, b * 512: b * 512 + 2], lhsT=WT[s][:, c * 128:(c + 1) * 128], rhs=S[:, c, :],
                    start=(c == 0), stop=(c == KC - 1)), reads=[bWT[s], bS], writes=[bB[b]], nowaw=(c > 0))
            tk.op("act", lambda t=t, b=b: nc.scalar.activation(
                out=OUT[:, t, :], in_=ALL[:, b * 512: b * 512 + 2], func=AF.Identity, bias=BM[:, t:t + 1]),
                reads=[bB[b], bBM], writes=[bOUT], nowaw=True)
        bo = tk.buf()
        tk.dma("sp", mo[:, :, :], OUT[:], reads=[bOUT], writes=[bo])
        tk.finish("sp", [bo])
    return nc


def mod_inputs(c, c_ctx, w_mod, b_mod):
    sT = np.ascontiguousarray(np.stack([fm(c.reshape(-1)), fm(c_ctx.reshape(-1))], 2))
    ins = []
    for core in range(NCORES):
        tiles, bias = [], []
        for l in range(4):
            cols = slice(core * 1536, (core + 1) * 1536)
            tiles.append(tile_w(w_mod[l][:, cols]))
            bias.append(b_mod[l][cols].reshape(12, 128).T)
        ins.append(dict(sT=sT, wm=np.concatenate(tiles, 0), bm=np.ascontiguousarray(np.concatenate(bias, 1))))
    return ins


def mod_outputs(results):
    full = np.zeros((4, 2, 12288), np.float32)
    for core, r in enumerate(results):
        mo = r["mo"]
        for l in range(4):
            blk = mo[:, l * 12:(l + 1) * 12, :]
            for s in range(2):
                full[l, s, core * 1536:(core + 1) * 1536] = blk[:, :, s].T.reshape(-1)
    return full.reshape(4, 2, 6, 2048)


def _modtile(m3):
    return np.ascontiguousarray(np.stack([fm(m3[0]), fm(m3[1]), fm(m3[2])], 1))


_PROGS = {}


def _prog(key, fn):
    if key not in _PROGS:
        _PROGS[key] = fn()
    return _PROGS[key]


def kernel(x, c, ctx, c_ctx, w_mod, b_mod, g_mix, g_ffn, fa_w_in, fa_w_out, attn_sink,
           rg_w_in, rg_conv_w, rg_conv_b, rg_w_a, rg_b_a, rg_w_i, rg_b_i, rg_lambda, rg_w_out,
           ffn_w_up, ffn_conv_w, ffn_conv_b, ffn_w_down, g_final):
    f32 = lambda a: np.asarray(a, dtype=np.float32)
    x_lat = f32(x)[0]
    x_ctx = f32(ctx)[0]
    r = run(_prog("mod", build_mod), mod_inputs(f32(c), f32(c_ctx), f32(w_mod), f32(b_mod))).results
    mod = mod_outputs(r)
    seqtab = {}
    out = None
    for layer in range(4):
        i = layer // 2
        mlm, mcm = _modtile(mod[layer, 0, 0:3]), _modtile(mod[layer, 1, 0:3])
        mlf, mcf = _modtile(mod[layer, 0, 3:6]), _modtile(mod[layer, 1, 3:6])
        gm = fm(f32(g_mix)[layer])
        if layer % 2 == 0:
            w_in, w_out = f32(fa_w_in)[i], f32(fa_w_out)[i]
            common1 = dict(modl=mlm, modc=mcm, g=gm, win=tile_w(w_in), Rm=rope_rm(), CS=chan_dft_table())
            in1 = []
            for k in range(NCORES):
                c_, s_ = rope_tables(k)
                in1.append(dict(common1, xT=fa1_xin(x_lat, x_ctx, k), cosT=c_, sinT=s_))
            r1 = run(_prog("fa1", build_fa1), in1).results
            Zl = np.concatenate([q["Z"][:1024] for q in r1], 0)
            Zc = np.concatenate([q["Z"][1024:] for q in r1], 0)
            kT_all = np.concatenate([q["kT"][:, :1024] for q in r1], 1)
            kcT = np.concatenate([q["kT"][:, 1024:] for q in r1], 1)
            vT_all = np.concatenate([q["vT"][:, :1024] for q in r1], 1)
            vcT = np.concatenate([q["vT"][:, 1024:] for q in r1], 1)
            wo_t = tile_w(w_out)
            sk = np.ascontiguousarray(np.tile(f32(attn_sink)[i][None, :], (128, 1)))
            in2 = []
            for k in range(NCORES):
                if k not in seqtab:
                    seqtab[k] = (seq_tables(k), attn_mask(k))
                (Tl, Tc), msk = seqtab[k]
                Kd, Vw = fa2_kv(kT_all, kcT, vT_all, vcT, k)
                in2.append(dict(Zl=Zl, Zc=Zc, Tl=Tl, Tc=Tc, qT=r1[k]["qT"], Kd=Kd, Vw=Vw, mask=msk, sink=sk,
                                wout=wo_t, xT=fa1_xin(x_lat, x_ctx, k), modl=mlm, modc=mcm))
            r2 = run(_prog("fa2", build_fa2), in2).results
            ys = [from_fm(q["yT"]) for q in r2]
            x_lat = np.concatenate([y[:1024] for y in ys], 0)
            x_ctx = np.concatenate([y[1024:] for y in ys], 0)
        else:
            Wt = rg_weights(f32(rg_w_in)[i], f32(rg_conv_w)[i], f32(rg_conv_b)[i], f32(rg_w_a)[i], f32(rg_b_a)[i],
                            f32(rg_w_i)[i], f32(rg_b_i)[i], f32(rg_lambda)[i], f32(rg_w_out)[i])
            common = dict(modl=mlm, modc=mcm, g=gm, **Wt)
            xin = [rg_xin(x_lat, x_ctx, k) for k in range(NCORES)]
            wA = np.ascontiguousarray(Wt["wall"][:32])
            wB = np.ascontiguousarray(Wt["wall"][32:48])
            in1 = [dict(common, wall=wA, xT=xin[k][0], edge=xin[k][1]) for k in range(NCORES)]
            r1 = run(_prog("rgA", lambda: build_rg(1)), in1).results
            car = rg_carries([q["S"] for q in r1])
            in2 = [dict(Y0=r1[k]["Y0"], PF=r1[k]["PF"], PR=r1[k]["PR"], CA=car[k][0], CB=car[k][1], wout=wB,
                        xT=fa1_xin(x_lat, x_ctx, k), modl=mlm, modc=mcm) for k in range(NCORES)]
            r2 = run(_prog("rgB", build_rgb), in2).results
            ys = [from_fm(q["yT"]) for q in r2]
            x_lat = np.concatenate([y[:1024] for y in ys], 0)
            x_ctx = np.concatenate([y[1024:] for y in ys], 0)
        last = (layer == 3)
        nctx = 0 if last else 32
        Wf = ffn_weights(f32(ffn_w_up)[layer], f32(ffn_conv_w)[layer], f32(ffn_conv_b)[layer], f32(ffn_w_down)[layer])
        per = ffn_inputs(x_lat, x_ctx, nctx)
        commonf = dict(modl=mlf, modc=mcf, gf=fm(f32(g_ffn)[layer]), **Wf)
        if last:
            commonf["gfin"] = fm(f32(g_final))
        inf = [dict(commonf, **per[k]) for k in range(NCORES)]
        rf = run(_prog(("ffn", nctx, last), lambda: build_ffn(nctx, final=last)), inf).results
        x_lat, xc_new = ffn_outputs(rf, nctx)
        if not last:
            x_ctx = xc_new
    return np.ascontiguousarray(x_lat.reshape(1, 8192, 2048).astype(np.float32))
```
